# Optimizing a Trainium2 kernel written in Bass

```python
import math
import jax, jax.numpy as jnp
from jax import lax
import numpy as np

D_MODEL = 2048
BATCH = 8
SEQ = 2048
DEPTH = 1
DEC_BATCH = 32
DEC_SEQ = 4
PAST_LEN = 8192
PAGE_SIZE = 128

SSM_EXPAND = 2
D_INNER = SSM_EXPAND * D_MODEL
SSM_HEAD_DIM = 64
SSM_HEADS = D_INNER // SSM_HEAD_DIM
SSM_GROUPS = 8
SSM_HPG = SSM_HEADS // SSM_GROUPS
D_STATE = 128
CONV_W = 4
CONV_DIM = D_INNER + 2 * SSM_GROUPS * D_STATE
SSD_CHUNK = 64
DT_MIN = 1e-3
DT_MAX = 1e-1
ATT_HEAD_DIM = 128
DIL_GROUPS = ((128, 1), (512, 4), (2048, 16))
N_DIL = len(DIL_GROUPS)
ATT_HPG = 4
ATT_HEADS = N_DIL * ATT_HPG
ATT_QKV_WIDTH = ATT_HEADS * ATT_HEAD_DIM
ATT_OUT_WIDTH = ATT_HPG * ATT_HEAD_DIM
ATT_BLOCK = 128
ALIBI_MAX_BIAS = 8.0
D_FF = ((8 * D_MODEL // 3 + 255) // 256) * 256
N_SUB = 3
IN_SIZES = (D_INNER, CONV_DIM, SSM_HEADS, ATT_QKV_WIDTH, ATT_QKV_WIDTH, ATT_QKV_WIDTH, 2 * D_MODEL)
N_IN = sum(IN_SIZES)
EPS = 1e-6
NEG_INF = -1e30

kernel_name = 'hybrid_ssd_dilated_swa_macaron_step'


def rmsnorm(x, g):
    xf = x.astype(jnp.float32)
    y = xf * lax.rsqrt(jnp.mean(xf * xf, axis=-1, keepdims=True) + EPS)
    return (y * g.astype(jnp.float32)).astype(x.dtype)


def split_at(x, sizes):
    return jnp.split(x, np.cumsum(sizes)[:-1].tolist(), axis=-1)


def swiglu(h, w_gu, w_down):
    g, u = jnp.split(h @ w_gu, 2, axis=-1)
    return (jax.nn.silu(g) * u) @ w_down


def alibi_slopes():
    i = np.arange(1, ATT_HEADS + 1, dtype=np.float32)
    s = np.exp2(-ALIBI_MAX_BIAS * i / ATT_HEADS).astype(np.float32)
    return jnp.asarray(s).reshape(N_DIL, ATT_HPG)


def causal_dwconv(xbc, prev, w, b):
    xp = jnp.concatenate([prev.astype(xbc.dtype), xbc], axis=1)
    y = lax.conv_general_dilated(xp, w.astype(xp.dtype)[:, None, :], (1,), 'VALID',
                                 dimension_numbers=('NWC', 'WIO', 'NWC'),
                                 feature_group_count=CONV_DIM)
    return y + b.astype(y.dtype), xp[:, -(CONV_W - 1):]


def ssd(x, dt, a, bm, cm, h0):
    f32 = jnp.float32
    Bsz, T = x.shape[:2]
    q = min(SSD_CHUNK, T)
    nc = -(-T // q)
    pad = nc * q - T

    def to_chunks(t):
        t = jnp.pad(t.astype(f32), [(0, 0), (0, pad)] + [(0, 0)] * (t.ndim - 2))
        return t.reshape((Bsz, nc, q) + t.shape[2:])

    xc = to_chunks(x).reshape(Bsz, nc, q, SSM_GROUPS, SSM_HPG, SSM_HEAD_DIM)
    dtc = to_chunks(dt).reshape(Bsz, nc, q, SSM_GROUPS, SSM_HPG)
    bc = to_chunks(bm)
    cc = to_chunks(cm)
    la = jnp.moveaxis(dtc * a.reshape(SSM_GROUPS, SSM_HPG), 2, -1)
    cum = jnp.cumsum(la, axis=-1)
    seg = cum[..., :, None] - cum[..., None, :]
    causal = jnp.tril(jnp.ones((q, q), dtype=bool))
    Lmat = jnp.exp(jnp.where(causal, seg, -jnp.inf))
    xdt = xc * dtc[..., None]
    cb = jnp.einsum('bcign,bcjgn->bcgij', cc, bc)
    y_diag = jnp.einsum('bcgij,bcgrij,bcjgrp->bcigrp', cb, Lmat, xdt)
    decay_end = jnp.exp(cum[..., -1:] - cum)
    states = jnp.einsum('bcjgn,bcgrj,bcjgrp->bcgrpn', bc, decay_end, xdt)
    chunk_decay = jnp.exp(cum[..., -1])

    def step(h, inp):
        s, dec = inp
        return h * dec[..., None, None] + s, h

    h_init = h0.astype(f32).reshape(Bsz, SSM_GROUPS, SSM_HPG, SSM_HEAD_DIM, D_STATE)
    h_last, h_prev = lax.scan(step, h_init, (jnp.moveaxis(states, 1, 0), jnp.moveaxis(chunk_decay, 1, 0)))
    h_prev = jnp.moveaxis(h_prev, 0, 1)
    y_off = jnp.einsum('bcign,bcgri,bcgrpn->bcigrp', cc, jnp.exp(cum), h_prev)
    y = (y_diag + y_off).reshape(Bsz, nc * q, SSM_HEADS, SSM_HEAD_DIM)[:, :T]
    return y, h_last.reshape(Bsz, SSM_HEADS, SSM_HEAD_DIM, D_STATE)


def dilated_attn_prompt(q, k, v, window, dil, slopes):
    f32 = jnp.float32
    Bsz, S, H, Dh = q.shape
    M = S // dil
    nb = -(-M // ATT_BLOCK)
    Mp = nb * ATT_BLOCK
    kmax = window // dil

    def residues(t):
        t = t.reshape(Bsz, M, dil, H, Dh).transpose(0, 2, 1, 3, 4)
        return jnp.pad(t, ((0, 0), (0, 0), (0, Mp - M), (0, 0), (0, 0)))

    def band(t):
        t = jnp.pad(residues(t), ((0, 0), (0, 0), (ATT_BLOCK, 0), (0, 0), (0, 0)))
        t = t.reshape(Bsz, dil, nb + 1, ATT_BLOCK, H, Dh)
        return jnp.concatenate([t[:, :, :-1], t[:, :, 1:]], axis=3)

    qb = residues(q).reshape(Bsz, dil, nb, ATT_BLOCK, H, Dh)
    kb, vb = band(k), band(v)
    s = jnp.einsum('brnqhd,brnkhd->brnhqk', qb, kb, preferred_element_type=f32) * (Dh ** -0.5)
    qi = jnp.arange(ATT_BLOCK)[:, None]
    ki = jnp.arange(2 * ATT_BLOCK)[None, :]
    dist = qi + ATT_BLOCK - ki
    key_idx = (jnp.arange(nb)[:, None, None] - 1) * ATT_BLOCK + ki[None]
    valid = (dist >= 0) & (dist <= kmax) & (key_idx >= 0)
    s = s - slopes[:, None, None] * (dil * dist).astype(f32)
    s = jnp.where(valid[:, None], s, NEG_INF)
    m = jnp.max(s, axis=-1, keepdims=True)
    pr = jnp.exp(s - m)
    l = jnp.sum(pr, axis=-1, keepdims=True)
    o = jnp.einsum('brnhqk,brnkhd->brnqhd', pr, vb.astype(f32)) / jnp.moveaxis(l, 3, 4)
    lse = (m + jnp.log(l))[..., 0]
    o = o.reshape(Bsz, dil, Mp, H, Dh)[:, :, :M].transpose(0, 2, 1, 3, 4).reshape(Bsz, S, H, Dh)
    lse = jnp.swapaxes(lse, 3, 4).reshape(Bsz, dil, Mp, H)[:, :, :M].transpose(0, 2, 1, 3).reshape(Bsz, S, H)
    return o, lse


def dilated_attn_sample(q, k, v, kv_buf, window, dil, slopes):
    f32 = jnp.float32
    T, Dh = q.shape[1], q.shape[3]
    lb = kv_buf.shape[1]
    kmax = window // dil
    kc = jnp.concatenate([kv_buf[:, :, 0].astype(k.dtype), k], axis=1)
    vc = jnp.concatenate([kv_buf[:, :, 1].astype(v.dtype), v], axis=1)
    steps = jnp.arange(kmax + 1)
    idx = lb + jnp.arange(T)[:, None] - dil * steps[None, :]
    valid = idx >= 0
    idx = jnp.maximum(idx, 0)
    kg = kc[:, idx].astype(f32)
    vg = vc[:, idx].astype(f32)
    s = jnp.einsum('bthd,btkhd->bthk', q.astype(f32), kg) * (Dh ** -0.5)
    s = s - slopes[:, None] * (dil * steps).astype(f32)[None, :]
    s = jnp.where(valid[:, None, :], s, NEG_INF)
    m = jnp.max(s, axis=-1, keepdims=True)
    pr = jnp.exp(s - m)
    l = jnp.sum(pr, axis=-1, keepdims=True)
    o = jnp.einsum('bthk,btkhd->bthd', pr, vg) / l
    lse = (m + jnp.log(l))[..., 0]
    new_buf = jnp.stack([kc, vc], axis=2)[:, -lb:]
    return o, lse, new_buf


def mixer(h, conv_prev, ssm_prev, kv_bufs, p):
    f32 = jnp.float32
    Bsz, T, _ = h.shape
    prompt = kv_bufs is None
    if prompt:
        conv_prev = jnp.zeros((Bsz, CONV_W - 1, CONV_DIM), h.dtype)
        ssm_prev = jnp.zeros((Bsz, SSM_HEADS, SSM_HEAD_DIM, D_STATE), f32)
    z, xbc, dt_raw, q, k, v, gates = split_at(h @ p['w_in'], IN_SIZES)

    xbc, conv_new = causal_dwconv(xbc, conv_prev, p['conv_w'], p['conv_b'])
    xs, bm, cm = split_at(jax.nn.silu(xbc), (D_INNER, SSM_GROUPS * D_STATE, SSM_GROUPS * D_STATE))
    xs = xs.reshape(Bsz, T, SSM_HEADS, SSM_HEAD_DIM)
    bm = bm.reshape(Bsz, T, SSM_GROUPS, D_STATE)
    cm = cm.reshape(Bsz, T, SSM_GROUPS, D_STATE)
    dt = jax.nn.softplus(dt_raw.astype(f32) + p['dt_bias'].astype(f32))
    a = -jnp.exp(p['a_log'].astype(f32))
    y, ssm_new = ssd(xs, dt, a, bm, cm, ssm_prev)
    y = y + xs.astype(f32) * p['d_skip'].astype(f32)[:, None]
    y = y.reshape(Bsz, T, D_INNER).astype(h.dtype)
    y = rmsnorm(y * jax.nn.silu(z), p['g_ssm_norm'])
    branch_ssm = y @ p['w_ssm_proj']

    shp = (Bsz, T, N_DIL, ATT_HPG, ATT_HEAD_DIM)
    q, k, v = q.reshape(shp), k.reshape(shp), v.reshape(shp)
    slopes = alibi_slopes()
    outs, lses, kv_new = [], [], []
    for gi, (window, dil) in enumerate(DIL_GROUPS):
        qg, kg, vg = q[:, :, gi], k[:, :, gi], v[:, :, gi]
        if prompt:
            o, lse = dilated_attn_prompt(qg, kg, vg, window, dil, slopes[gi])
            buf = jnp.stack([kg, vg], axis=2)[:, -min(window, T):]
        else:
            o, lse, buf = dilated_attn_sample(qg, kg, vg, kv_bufs[gi], window, dil, slopes[gi])
        outs.append(o)
        lses.append(lse)
        kv_new.append(buf)
    wts = jax.nn.softmax(jnp.stack(lses, axis=0), axis=0)
    att = jnp.sum(wts[..., None] * jnp.stack(outs, axis=0), axis=0)
    att = att.reshape(Bsz, T, ATT_OUT_WIDTH).astype(h.dtype)
    branch_att = att @ p['w_att_proj']

    g_ssm, g_att = jnp.split(jax.nn.sigmoid(gates), 2, axis=-1)
    out = (g_ssm * branch_ssm + g_att * branch_att) @ p['w_out']
    return out, conv_new, ssm_new.astype(h.dtype), tuple(kv_new)


def layer(x, c, conv_prev, ssm_prev, kv_bufs, p):
    Bsz = x.shape[0]
    mod = (jax.nn.silu(c) @ p['w_ada'] + p['b_ada']).reshape(Bsz, N_SUB, 3, 1, D_MODEL)

    def pre(x, i, g):
        return rmsnorm(x, g) * (1 + mod[:, i, 1]) + mod[:, i, 0]

    def post(x, i, out, g, w):
        return x + w * mod[:, i, 2] * rmsnorm(out, g)

    x = post(x, 0, swiglu(pre(x, 0, p['g_pre_ffn1']), p['w_gu_ffn1'], p['w_down_ffn1']), p['g_post_ffn1'], 0.5)
    mix, conv_new, ssm_new, kv_new = mixer(pre(x, 1, p['g_pre_mix']), conv_prev, ssm_prev, kv_bufs, p)
    x = post(x, 1, mix, p['g_post_mix'], 1.0)
    x = post(x, 2, swiglu(pre(x, 2, p['g_pre_ffn2']), p['w_gu_ffn2'], p['w_down_ffn2']), p['g_post_ffn2'], 0.5)
    return x, conv_new, ssm_new, kv_new


def setup_inputs(seed: int = 0) -> dict:
    key = jax.random.key(seed)
    ks = iter(jax.random.split(key, 40))
    f32 = jnp.float32
    L = DEPTH

    def nrm(shape, scale):
        return jax.random.normal(next(ks), shape, f32) * scale

    def gain(shape):
        return 1.0 + nrm(shape, 0.1)

    dt0 = jnp.exp(jax.random.uniform(next(ks), (L, SSM_HEADS), f32, math.log(DT_MIN), math.log(DT_MAX)))
    dt_bias = dt0 + jnp.log(-jnp.expm1(-dt0))
    a_log = jnp.log(jax.random.uniform(next(ks), (L, SSM_HEADS), f32, 1.0, 16.0))
    kvshape = lambda w: (L, DEC_BATCH, min(w, PAST_LEN), 2, ATT_HPG, ATT_HEAD_DIM)
    return {
        'x_prompt': nrm((BATCH, SEQ, D_MODEL), 1.0),
        'x_sample': nrm((DEC_BATCH, DEC_SEQ, D_MODEL), 1.0),
        'c_prompt': nrm((BATCH, D_MODEL), 1.0),
        'c_sample': nrm((DEC_BATCH, D_MODEL), 1.0),
        'state_ssm': nrm((L, DEC_BATCH, SSM_HEADS, SSM_HEAD_DIM, D_STATE), 0.1),
        'state_conv': nrm((L, DEC_BATCH, CONV_W - 1, CONV_DIM), 1.0),
        'cache_kv_w128': nrm(kvshape(128), 1.0),
        'cache_kv_w512': nrm(kvshape(512), 1.0),
        'cache_kv_w2048': nrm(kvshape(2048), 1.0),
        'w_ada': nrm((L, D_MODEL, N_SUB * 3 * D_MODEL), D_MODEL ** -0.5),
        'b_ada': nrm((L, N_SUB * 3 * D_MODEL), 0.02),
        'g_pre_ffn1': gain((L, D_MODEL)),
        'g_post_ffn1': gain((L, D_MODEL)),
        'w_gu_ffn1': nrm((L, D_MODEL, 2 * D_FF), D_MODEL ** -0.5),
        'w_down_ffn1': nrm((L, D_FF, D_MODEL), D_FF ** -0.5),
        'g_pre_mix': gain((L, D_MODEL)),
        'g_post_mix': gain((L, D_MODEL)),
        'w_in': nrm((L, D_MODEL, N_IN), D_MODEL ** -0.5),
        'conv_w': nrm((L, CONV_W, CONV_DIM), CONV_W ** -0.5),
        'conv_b': nrm((L, CONV_DIM), 0.02),
        'dt_bias': dt_bias,
        'a_log': a_log,
        'd_skip': gain((L, SSM_HEADS)),
        'g_ssm_norm': gain((L, D_INNER)),
        'w_ssm_proj': nrm((L, D_INNER, D_MODEL), D_INNER ** -0.5),
        'w_att_proj': nrm((L, ATT_OUT_WIDTH, D_MODEL), ATT_OUT_WIDTH ** -0.5),
        'w_out': nrm((L, D_MODEL, D_MODEL), D_MODEL ** -0.5),
        'g_pre_ffn2': gain((L, D_MODEL)),
        'g_post_ffn2': gain((L, D_MODEL)),
        'w_gu_ffn2': nrm((L, D_MODEL, 2 * D_FF), D_MODEL ** -0.5),
        'w_down_ffn2': nrm((L, D_FF, D_MODEL), D_FF ** -0.5),
    }


def reference(x_prompt, x_sample, c_prompt, c_sample, state_ssm, state_conv, cache_kv_w128, cache_kv_w512,
              cache_kv_w2048, w_ada, b_ada, g_pre_ffn1, g_post_ffn1, w_gu_ffn1, w_down_ffn1, g_pre_mix,
              g_post_mix, w_in, conv_w, conv_b, dt_bias, a_log, d_skip, g_ssm_norm, w_ssm_proj, w_att_proj,
              w_out, g_pre_ffn2, g_post_ffn2, w_gu_ffn2, w_down_ffn2):
    kv_in = (cache_kv_w128, cache_kv_w512, cache_kv_w2048)
    yp, ys = x_prompt, x_sample
    ssm_p, ssm_s, conv_p, conv_s = [], [], [], []
    kv_p = [[] for _ in DIL_GROUPS]
    kv_s = [[] for _ in DIL_GROUPS]
    for l in range(DEPTH):
        p = {'w_ada': w_ada[l], 'b_ada': b_ada[l],
             'g_pre_ffn1': g_pre_ffn1[l], 'g_post_ffn1': g_post_ffn1[l],
             'w_gu_ffn1': w_gu_ffn1[l], 'w_down_ffn1': w_down_ffn1[l],
             'g_pre_mix': g_pre_mix[l], 'g_post_mix': g_post_mix[l], 'w_in': w_in[l],
             'conv_w': conv_w[l], 'conv_b': conv_b[l], 'dt_bias': dt_bias[l], 'a_log': a_log[l],
             'd_skip': d_skip[l], 'g_ssm_norm': g_ssm_norm[l], 'w_ssm_proj': w_ssm_proj[l],
             'w_att_proj': w_att_proj[l], 'w_out': w_out[l],
             'g_pre_ffn2': g_pre_ffn2[l], 'g_post_ffn2': g_post_ffn2[l],
             'w_gu_ffn2': w_gu_ffn2[l], 'w_down_ffn2': w_down_ffn2[l]}
        yp, cp, sp, kp = layer(yp, c_prompt, None, None, None, p)
        ys, cs, ss, ksm = layer(ys, c_sample, state_conv[l], state_ssm[l], tuple(kv[l] for kv in kv_in), p)
        ssm_p.append(sp)
        ssm_s.append(ss)
        conv_p.append(cp)
        conv_s.append(cs)
        for gi in range(N_DIL):
            kv_p[gi].append(kp[gi])
            kv_s[gi].append(ksm[gi])
    return (yp, ys, jnp.stack(ssm_p), jnp.stack(ssm_s), jnp.stack(conv_p), jnp.stack(conv_s),
            jnp.stack(kv_p[0]), jnp.stack(kv_s[0]), jnp.stack(kv_p[1]), jnp.stack(kv_s[1]),
            jnp.stack(kv_p[2]), jnp.stack(kv_s[2]))
```

```python
import numpy as np
from contextlib import ExitStack
import concourse.bass as bass
import concourse.mybir as mybir
from concourse.bass_utils import run_bass_kernel_spmd

F32 = mybir.dt.float32
BF16 = mybir.dt.bfloat16
AF = mybir.ActivationFunctionType
ALU = mybir.AluOpType
AX = mybir.AxisListType

D = 2048
S = 2048
NS = 16
NT = S + NS
DFF = 5632
NIN = 19008
DIN = 4096
EPS = 1e-6
OFF_Z, OFF_XBC, OFF_DT, OFF_Q, OFF_K, OFF_V, OFF_G = 0, 4096, 10240, 10304, 11840, 13376, 14912
DILS = ((128, 1), (512, 4), (2048, 16))
SLOPES = np.exp2(-8.0 * np.arange(1, 13, dtype=np.float32) / 12).astype(np.float32).reshape(3, 4)
NRING = 6
SAME_ENG_SYNC = True
WBUF_ELEMS = 4096
NWBUF = 5
CFG = {}


class KB:
    def __init__(s, nc, es):
        s.nc, s.es = nc, es
        s.eng = dict(pe=nc.tensor, act=nc.scalar, dve=nc.vector, pool=nc.gpsimd, sp=nc.sync)
        s.csem = {e: es.enter_context(nc.semaphore('c_' + e)) for e in ('pe', 'act', 'dve', 'pool')}
        s.cnt = {e: 0 for e in s.csem}
        s.dsem = {q: [es.enter_context(nc.semaphore('d_%s%d' % (q, i))) for i in range(NRING)] for q in ('sp', 'pool')}
        s.dcnt = {q: 0 for q in s.dsem}
        s.waited = {e: {} for e in s.eng}
        s.res = {}
        s.pend = []
        s.last_dma = {}
        s.multi = {}
        s.log = {e: [] for e in ('pe', 'act', 'dve', 'pool', 'sp')}

    def _wait(s, e, tok):
        name, sem, val, peng = tok
        if peng == e and not SAME_ENG_SYNC:
            return
        if s.waited[e].get(name, 0) >= val:
            return
        s.eng[e].wait_ge(sem, val)
        s.waited[e][name] = val
        s.log[e].append(('w', name, val))

    def _deps(s, e, r, w):
        for k in r:
            st = s.res.get(k)
            if st and st[0]:
                s._wait(e, st[0])
            for t in s.multi.get(k, ()):
                s._wait(e, t)
        for k in w:
            st = s.res.get(k)
            if st:
                if st[0]:
                    s._wait(e, st[0])
                for t in st[1].values():
                    s._wait(e, t)

    def _reg(s, tok, r, w):
        for k in r:
            st = s.res.setdefault(k, [None, {}])
            st[1][tok[0]] = tok
        for k in w:
            s.res[k] = [tok, {}]
            s.multi.pop(k, None)

    def op(s, e, fn, r=(), w=(), inc=True):
        if e != 'pe':
            assert not s.pend, "non-PE op emitted inside an open PE accumulation group"
        s._deps(e, r, w)
        inst = fn()
        if inc:
            s.cnt[e] += 1
            inst.then_inc(s.csem[e], 1)
            s.log[e].append(('i', 'c_' + e, 1))
            tok = ('c_' + e, s.csem[e], s.cnt[e], e)
            for (pr, pw) in s.pend:
                s._reg(tok, pr, pw)
            s.pend = []
            s._reg(tok, r, w)
        else:
            s.pend.append((tuple(r), tuple(w)))
        return inst

    def dma(s, q, out, in_, r=(), w=(), after=()):
        assert not s.pend
        s._deps(q, r, tuple(w) + tuple(after))
        i = s.dcnt[q]
        slot, uses = i % NRING, i // NRING
        sem = s.dsem[q][slot]
        name = 'd_%s%d' % (q, slot)
        if uses > 0:
            s._wait(q, (name, sem, 16 * uses, 'dma'))
        s.eng[q].dma_start(out=out, in_=in_).then_inc(sem, 16)
        s.log[q].append(('i', name, 16))
        s.dcnt[q] += 1
        tok = (name, sem, 16 * (uses + 1), 'dma')
        s._reg(tok, r, w)
        s.last_dma[name] = tok

    def drain(s):
        for tok in s.last_dma.values():
            s._wait('sp', tok)

    def barrier(s):
        toks = [('c_' + e, s.csem[e], s.cnt[e], e) for e in s.csem if s.cnt[e] > 0] + list(s.last_dma.values())
        for e in s.eng:
            for t in toks:
                name, sem, val, peng = t
                if s.waited[e].get(name, 0) >= val:
                    continue
                s.eng[e].wait_ge(sem, val)
                s.waited[e][name] = val
                s.log[e].append(('w', name, val))

    def fence(s, queues=('sp', 'pool')):
        for q in queues:
            for tok in list(s.last_dma.values()):
                s._wait(q, tok)

    def V(s, fn, r=(), w=()):
        return s.op('dve', fn, r, w)

    def A(s, fn, r=(), w=()):
        return s.op('act', fn, r, w)

    def G(s, fn, r=(), w=()):
        return s.op('pool', fn, r, w)

    def mm(s, out, lhsT, rhs, start, stop, r=(), w=(), inc=None):
        nc = s.nc
        return s.op('pe', lambda: nc.tensor.matmul(out, lhsT=lhsT, rhs=rhs, start=start, stop=stop), r, w,
                    inc=(stop if inc is None else inc))

    def tr(s, out, in_, ident, r=(), w=(), inc=True):
        nc = s.nc
        return s.op('pe', lambda: nc.tensor.transpose(out, in_, ident), r, w, inc=inc)


def fold_keys(kb, dst, srcs):
    st = kb.res.setdefault(dst, [None, {}])
    for k in srcs:
        s_ = kb.res.get(k)
        if s_ and s_[0]:
            st[1][s_[0][0]] = s_[0]
    kb.multi[dst] = [kb.res[k][0] for k in srcs if kb.res.get(k) and kb.res[k][0]]

def build(debug=False):
    nc = bass.Bass("TRN2", target_bir_lowering=False)
    es = ExitStack()
    with es:
        _build(nc, es, debug)
    return nc


def _build(nc, es, debug):
    kb = KB(nc, es)
    CFG['kb'] = kb

    def din(name, shape, dt=F32):
        return nc.dram_tensor(name, list(shape), dt, kind="ExternalInput").ap()

    def dout(name, shape, dt=F32):
        return nc.dram_tensor(name, list(shape), dt, kind="ExternalOutput").ap()

    def dscr(name, shape, dt=F32):
        return nc.dram_tensor(name, list(shape), dt, kind=("ExternalOutput" if debug else "Internal")).ap()

    def dbg(name, ap, keys):
        if not debug:
            return
        shp = list(ap.shape)
        t = nc.dram_tensor("dbg_" + name, shp, ap.dtype, kind="ExternalOutput").ap()
        kb.dma('sp', t, ap, r=keys)

    cdist = din("cdist", [128, 256])
    x_p = din("x_p", [S, D]); x_s = din("x_s", [NS, D]); c5 = din("c5", [5, D])
    st_ssm = din("st_ssm", [4, 32, 128, 128])
    st_conv = din("st_conv", [4, 3, 6144])
    cache = [din("cache%d" % g, [4, DILS[g][0], 2, 512]) for g in range(3)]
    w_ada = din("w_ada", [D, 18432]); b_ada = din("b_ada", [144, 128])
    gvec = {n: din(n, [16, 128]) for n in ("g_pre_ffn1", "g_post_ffn1", "g_pre_mix", "g_post_mix", "g_pre_ffn2", "g_post_ffn2")}
    w_gu = [din("w_gu_ffn1", [D, 2 * DFF]), din("w_gu_ffn2", [D, 2 * DFF])]
    w_dn = [din("w_down_ffn1", [DFF, D]), din("w_down_ffn2", [DFF, D])]
    w_in = din("w_in", [D, NIN])
    conv_w = din("conv_w", [4, 48, 128]); conv_b = din("conv_b", [48, 128])
    dt_bias = din("dt_bias", [64, 1]); a_log = din("a_log", [64, 1]); d_skip = din("d_skip", [64, 1])
    g_ssm = din("g_ssm_norm", [32, 128])
    w_ssm = din("w_ssm_proj", [DIN, D]); w_att = din("w_att_proj", [512, D]); w_out = din("w_out", [D, D])

    y_p = dout("y_p", [S, D]); y_s = dout("y_s", [NS, D])
    ssm_p = dout("ssm_p", [32, 128, 128]); ssm_s = dout("ssm_s", [4, 32, 128, 128])
    conv_p = dout("conv_p", [3, 6144]); conv_s = dout("conv_s", [4, 3, 6144])
    kv_p = [dout("kv_p%d" % g, [min(DILS[g][0], S), 2, 512]) for g in range(3)]
    kv_s = [dout("kv_s%d" % g, [4, DILS[g][0], 2, 512]) for g in range(3)]

    x1T = dscr("x1T", [16, 128, NT]); zsT = dscr("zsT", [32, 128, NT], BF16); xbcT = dscr("xbcT", [48, 128, NT], BF16)
    dtT = dscr("dtT", [64, NT]); laT = dscr("laT", [64, NT])
    qT = dscr("qT", [12, 128, NT], BF16); kT = dscr("kT", [12, 128, NT], BF16)
    vtok = dscr("vtok", [NT, 1536], BF16); kvnew = dscr("kvnew", [NS, 2, 1536])
    gT = dscr("gT", [32, 128, NT], BF16); ynT = dscr("ynT", [32, 128, NT], BF16); attT = dscr("attT", [4, 128, NT], BF16)
    oscr = dscr("oscr", [3, S, 516])

    used_names = {}

    def sb(name, shape, dt=F32, st=es):
        k = used_names.get(name, 0)
        used_names[name] = k + 1
        if k:
            name = "%s_r%d" % (name, k)
        return st.enter_context(nc.sbuf_tensor(name, list(shape), dt))

    def ps(name, shape, dt=F32, st=es):
        return st.enter_context(nc.psum_tensor(name, list(shape), dt))

    identF = sb("identF", [128, 128]); identB = sb("identB", [128, 128], BF16)
    onesB = sb("onesB", [128, 128], BF16); onesF = sb("onesF", [128, 128])
    mle = sb("mle", [128, 128]); mgt = sb("mgt", [128, 128])
    modT = sb("modT", [128, 144, 5])
    gcol = {n: sb("c_" + n, [128, 16]) for n in gvec}
    gssm_c = sb("gssm_c", [128, 32]); convw_c = sb("convw_c", [128, 4, 48]); convb_c = sb("convb_c", [128, 48])
    dtb_c = sb("dtb_c", [64, 1]); a_c = sb("a_c", [64, 1]); dsk_c = sb("dsk_c", [64, 1])
    dskP = sb("dskP", [128, 32])
    Aaf = [sb("Aaf%d" % i, [128, 16, 5]) for i in range(3)]
    Gaf = [sb("Gaf%d" % i, [128, 16, 5]) for i in range(3)]

    def cst(t, val):
        kb.G(lambda: nc.gpsimd.memset(t[:], val), w=[t.name])

    def asel(t, pattern, cmp, base=0, cm=1, fill=0.0):
        kb.G(lambda: nc.gpsimd.affine_select(out=t[:], in_=t[:], pattern=pattern, compare_op=cmp, fill=fill,
                                             base=base, channel_multiplier=cm), r=[t.name], w=[t.name])

    cst(identF, 1.0); asel(identF, [[-1, 128]], ALU.is_equal)
    cst(identB, 1.0); asel(identB, [[-1, 128]], ALU.is_equal)
    cst(onesB, 1.0); cst(onesF, 1.0)
    cst(mle, 1.0); asel(mle, [[1, 128]], ALU.is_ge, cm=-1)
    cst(mgt, 1.0); asel(mgt, [[-1, 128]], ALU.is_gt, cm=1)

    PS = [ps("psb%d" % i, [128, 512]) for i in range(8)]
    PSK = ["psb%d" % i for i in range(8)]

    def psbf(i):
        return PS[i][:].bitcast(BF16)

    with ExitStack() as p0:
        vtmp = sb("vtmp", [128, 128], st=p0)
        expd = sb("expd", [64, 32, 128], st=p0)
        kb.V(lambda: nc.vector.tensor_copy(out=expd[:].rearrange("h c (two p) -> h (c two) p", two=2),
                                           in_=identF[0:64, 0:64].unsqueeze(2).to_broadcast([64, 64, 64])), r=["identF"], w=["expd"])

        def vec_cols(src_ap, n, dst_ap, dst_key):
            kb.dma('sp', vtmp[0:n, :], src_ap, w=["vtmp"])
            kb.tr(PS[0][:, 0:n], vtmp[0:n, :], identF[0:n, 0:n], r=["vtmp", "identF"], w=[PSK[0]])
            kb.V(lambda: nc.vector.tensor_copy(out=dst_ap, in_=PS[0][:, 0:n]), r=[PSK[0]], w=[dst_key])

        for n_, t in gcol.items():
            vec_cols(gvec[n_], 16, t[:], t.name)
        vec_cols(g_ssm, 32, gssm_c[:], "gssm_c")
        for j in range(4):
            vec_cols(conv_w[j], 48, convw_c[:, j, :], "convw_c")
        vec_cols(conv_b, 48, convb_c[:], "convb_c")
        bcol = sb("bcol", [128, 144], st=p0)
        vec_cols(b_ada[0:72], 72, bcol[:, 0:72], "bcol")
        vec_cols(b_ada[72:144], 72, bcol[:, 72:144], "bcol")
        kb.dma('sp', dtb_c[:], dt_bias, w=["dtb_c"])
        kb.dma('sp', a_c[:], a_log, w=["a_c"])
        kb.dma('sp', dsk_c[:], d_skip, w=["dsk_c"])
        kb.A(lambda: nc.scalar.activation(out=a_c[:], in_=a_c[:], func=AF.Exp), r=["a_c"], w=["a_c"])
        kb.V(lambda: nc.vector.tensor_scalar(out=a_c[:], in0=a_c[:], scalar1=-1.0, scalar2=None, op0=ALU.mult), r=["a_c"], w=["a_c"])
        for hp in range(32):
            kb.mm(PS[0][:, hp:hp + 1], expd[:, hp, :], dsk_c[:, 0:1], True, True, r=["expd", "dsk_c"], w=[PSK[0]])
        kb.V(lambda: nc.vector.tensor_copy(out=dskP[:], in_=PS[0][:, 0:32]), r=[PSK[0]], w=["dskP"])

        c5t = sb("c5t", [5, D], st=p0); scT = sb("scT", [128, 16, 5], BF16, st=p0)
        kb.dma('sp', c5t[:], c5, w=["c5t"])
        for c in range(16):
            kb.tr(PS[0][:, c * 5:(c + 1) * 5], c5t[0:5, c * 128:(c + 1) * 128], identF[0:5, 0:5], r=["c5t", "identF"], w=[PSK[0]])
        kb.A(lambda: nc.scalar.activation(out=scT[:].rearrange("p c r -> p (c r)"), in_=PS[0][:, 0:80], func=AF.Silu), r=[PSK[0]], w=["scT"])
        wa = [sb("wa%d" % i, [128, 16, 512], BF16, st=p0) for i in range(3)]
        nblk = 36
        wa_src = w_ada.rearrange("(kc p) f -> p kc f", p=128)

        def wa_load(i):
            kb.dma('pool', wa[i % 3][:], wa_src[:, :, i * 512:(i + 1) * 512], w=["wa%d" % (i % 3)])

        wa_load(0); wa_load(1); wa_load(2)
        for i in range(nblk):
            pb = 1 + (i % 2)
            for j in range(4):
                for kc in range(16):
                    kb.mm(PS[pb][:, j * 5:(j + 1) * 5], wa[i % 3][:, kc, j * 128:(j + 1) * 128], scT[:, kc, :], kc == 0, kc == 15,
                          r=["wa%d" % (i % 3), "scT"], w=[PSK[pb]])
            f0 = i * 4
            kb.V(lambda: nc.vector.tensor_tensor(out=modT[:, f0:f0 + 4, :], in0=PS[pb][:, 0:20].rearrange("p (j r) -> p j r", r=5),
                                                 in1=bcol[:, f0:f0 + 4].unsqueeze(2).to_broadcast([128, 4, 5]), op=ALU.add),
                 r=[PSK[pb], "bcol"], w=["modT"])
            if i + 3 < nblk:
                wa_load(i + 3)
        pre_names = ("g_pre_ffn1", "g_pre_mix", "g_pre_ffn2"); post_names = ("g_post_ffn1", "g_post_mix", "g_post_ffn2")
        wts = (0.5, 1.0, 0.5)
        for i in range(3):
            sc = modT[:, (i * 3 + 1) * 16:(i * 3 + 2) * 16, :]
            gt = modT[:, (i * 3 + 2) * 16:(i * 3 + 3) * 16, :]
            gp = gcol[pre_names[i]]; gq = gcol[post_names[i]]
            kb.V(lambda: nc.vector.scalar_tensor_tensor(out=Aaf[i][:], in0=sc, scalar=1.0, in1=gp[:].unsqueeze(2).to_broadcast([128, 16, 5]),
                                                        op0=ALU.add, op1=ALU.mult), r=["modT", gp.name], w=[Aaf[i].name])
            kb.V(lambda: nc.vector.scalar_tensor_tensor(out=Gaf[i][:], in0=gt, scalar=wts[i], in1=gq[:].unsqueeze(2).to_broadcast([128, 16, 5]),
                                                        op0=ALU.mult, op1=ALU.mult), r=["modT", gq.name], w=[Gaf[i].name])

    kb.barrier()

    def Bsh(i, c, row):
        return modT[:, i * 48 + c, row:row + 1]

    TP = 512
    NCOL = TP + NS
    TILES = [(t * TP, TP, t == 0) for t in range(S // TP)]

    def token_phase(st, which):
        xT = sb("xT", [128, 16, NCOL], st=st)
        hT = sb("hT", [128, 16, NCOL], BF16, st=st)
        aT = sb("aT", [128, 44, NCOL], BF16, st=st)
        oT = sb("oT", [128, 16, NCOL], st=st)
        tmpA = [sb("tmpA%d" % i, [128, NCOL + 3], st=st) for i in range(2)]
        tmpB = [sb("tmpB%d" % i, [128, NCOL], st=st) for i in range(2)]
        stg = [sb("stg%d" % i, [128, NCOL], BF16, st=st) for i in range(2)]
        rstd = sb("rstd", [128, NCOL], st=st)
        wb = [sb("wb%d" % i, [128, WBUF_ELEMS], BF16, st=st) for i in range(NWBUF)]
        xin = [sb("xin%d" % i, [128, D], st=st) for i in range(1)]
        halo = sb("halo", [128, 48, 3], st=st)
        cstl = [sb("cstl%d" % i, [128, 128], st=st) for i in range(3)]
        xps = sb("xps", [128, 28], st=st); accs = sb("accs", [128, 16], st=st)
        epsc = sb("epsc", [128, 1], st=st)
        kb.G(lambda: nc.gpsimd.memset(epsc[:], EPS), w=["epsc"])
        kb.G(lambda: nc.gpsimd.memset(halo[:], 0.0), w=["halo"])
        cnt = {'w': 0, 'a': 0, 'b': 0, 's': 0, 'x': 0, 'r': 0, 'd': 0}

        def rr(lst, k):
            i = cnt[k] % len(lst); cnt[k] += 1
            return lst[i]

        def sreg():
            i = cnt['r'] % 32; cnt['r'] += 1
            return (lambda m=128: PS[6][0:m, i * 16:(i + 1) * 16]), PSK[6]

        def run_jobs(jobs):
            n = len(jobs)
            for i in range(min(NWBUF, n)):
                jobs[i][0](wb[i % NWBUF])
            for i in range(n):
                jobs[i][1](wb[i % NWBUF])
                if i + NWBUF < n:
                    jobs[i + NWBUF][0](wb[(i + NWBUF) % NWBUF])

        def wload(buf, view_shape, src_ap):
            nel = int(np.prod(view_shape[1:]))
            dst = buf[:, 0:nel]
            if len(view_shape) == 3:
                dst = dst.rearrange("p (a b) -> p a b", b=view_shape[2])
            kb.dma('pool', dst, src_ap, w=[buf.name])

        for (col0, np_, has_s) in TILES:
            n = np_ + (NS if has_s else 0)
            segs = [(0, np_, 0)] + ([(np_ + 4 * b, np_ + 4 * b + 4, 1 + b) for b in range(4)] if has_s else [])
            parts = [(0, np_, col0)] + ([(np_, n, S)] if has_s else [])
            groups = [(0, np_, False)] + ([(np_, n, True)] if has_s else [])

            def gout(gi, bank):
                c0, c1, iss = groups[gi]
                if iss:
                    f, key = sreg()
                    return f, key
                return (lambda m=128: PS[bank][0:m, 0:c1 - c0]), PSK[bank]

            def norm_stats(src, key_r):
                outs = [gout(gi, 7) for gi in range(len(groups))]
                for c in range(16):
                    sq = rr(stg, 's')
                    kb.A(lambda: nc.scalar.activation(out=sq[:, 0:n], in_=src[:, c, 0:n], func=AF.Square), r=[key_r], w=[sq.name])
                    for gi, (c0, c1, iss) in enumerate(groups):
                        kb.mm(outs[gi][0](), onesB[:], sq[:, c0:c1], c == 0, c == 15, r=[sq.name, "onesB"], w=[outs[gi][1]], inc=True)
                for gi, (c0, c1, iss) in enumerate(groups):
                    kb.A(lambda: nc.scalar.activation(out=rstd[:, c0:c1], in_=outs[gi][0](), func=AF.Sqrt, bias=epsc[:, 0:1], scale=1.0 / D),
                         r=[outs[gi][1], "epsc"], w=["rstd"])
                kb.V(lambda: nc.vector.reciprocal(out=rstd[:, 0:n], in_=rstd[:, 0:n]), r=["rstd"], w=["rstd"])

            def pre_norm(i):
                norm_stats(xT, "xT")
                for c in range(16):
                    t = rr(tmpB, 'b')
                    for (c0, c1, row) in segs:
                        kb.V(lambda: nc.vector.scalar_tensor_tensor(out=t[:, c0:c1], in0=xT[:, c, c0:c1], scalar=Aaf[i][:, c, row:row + 1],
                                                                    in1=rstd[:, c0:c1], op0=ALU.mult, op1=ALU.mult),
                             r=["xT", "rstd", Aaf[i].name], w=[t.name])
                        kb.A(lambda: nc.scalar.activation(out=hT[:, c, c0:c1], in_=t[:, c0:c1], func=AF.Identity, bias=Bsh(i, c, row), scale=1.0),
                             r=[t.name, "modT"], w=["hT"])

            def post_res(i):
                norm_stats(oT, "oT")
                for c in range(16):
                    t = rr(tmpB, 'b')
                    for (c0, c1, row) in segs:
                        kb.V(lambda: nc.vector.scalar_tensor_tensor(out=t[:, c0:c1], in0=oT[:, c, c0:c1], scalar=Gaf[i][:, c, row:row + 1],
                                                                    in1=rstd[:, c0:c1], op0=ALU.mult, op1=ALU.mult),
                             r=["oT", "rstd", Gaf[i].name], w=[t.name])
                    kb.V(lambda: nc.vector.tensor_tensor(out=xT[:, c, 0:n], in0=xT[:, c, 0:n], in1=t[:, 0:n], op=ALU.add), r=["xT", t.name], w=["xT"])

            def ffn(l):
                jobs = []
                gu = w_gu[l].rearrange("(kc p) f -> p kc f", p=128)
                dn = w_dn[l].rearrange("(fc p) d -> p fc d", p=128)
                for f in range(44):
                    def ld(buf, f=f):
                        v = buf[:, 0:4096].rearrange("p (k t m) -> p k t m", t=2, m=128)
                        kb.dma('pool', v[:, :, 0, :], gu[:, :, f * 128:(f + 1) * 128], w=[buf.name])
                        kb.dma('pool', v[:, :, 1, :], gu[:, :, DFF + f * 128:DFF + (f + 1) * 128], w=[buf.name])

                    def use(buf, f=f):
                        v = buf[:, 0:4096].rearrange("p (k t m) -> p k t m", t=2, m=128)
                        pg, pu = (1, 2) if f % 2 == 0 else (3, 4)
                        for gi, (c0, c1, iss) in enumerate(groups):
                            og, kg = gout(gi, pg); ou, ku = gout(gi, pu)
                            for kc in range(16):
                                kb.mm(og(), v[:, kc, 0, :], hT[:, kc, c0:c1], kc == 0, kc == 15, r=[buf.name, "hT"], w=[kg])
                            for kc in range(16):
                                kb.mm(ou(), v[:, kc, 1, :], hT[:, kc, c0:c1], kc == 0, kc == 15, r=[buf.name, "hT"], w=[ku])
                            t = rr(tmpB, 'b')
                            kb.A(lambda: nc.scalar.activation(out=t[:, c0:c1], in_=og(), func=AF.Silu), r=[kg], w=[t.name])
                            kb.V(lambda: nc.vector.tensor_tensor(out=aT[:, f, c0:c1], in0=t[:, c0:c1], in1=ou(), op=ALU.mult),
                                 r=[t.name, ku], w=["aT"])
                    jobs.append((ld, use))
                for d in range(16):
                    dst_ = {}
                    for half in range(2):
                        def ld(buf, d=d, half=half):
                            wload(buf, [128, 22, 128], dn[:, half * 22:(half + 1) * 22, d * 128:(d + 1) * 128])

                        def use(buf, d=d, half=half, dst_=dst_):
                            v = buf[:, 0:2816].rearrange("p (k m) -> p k m", m=128)
                            pb = (5, 7)[d % 2]
                            for gi, (c0, c1, iss) in enumerate(groups):
                                if half == 0:
                                    dst_[gi] = gout(gi, pb)
                                o_, k_ = dst_[gi]
                                for fc in range(22):
                                    last = (fc == 21)
                                    kb.mm(o_(), v[:, fc, :], aT[:, half * 22 + fc, c0:c1], half == 0 and fc == 0, half == 1 and last,
                                          r=[buf.name, "aT"], w=[k_], inc=last)
                                if half == 1:
                                    kb.A(lambda: nc.scalar.copy(out=oT[:, d, c0:c1], in_=o_()), r=[k_], w=["oT"])
                        jobs.append((ld, use))
                run_jobs(jobs)

            def load_x_tokmajor(src_rows_ap, nrows, c_at):
                xi = rr(xin, 'x')
                kb.dma('sp', xi[0:nrows, :], src_rows_ap, w=[xi.name])
                for c4 in range(4):
                    for j in range(4):
                        c = c4 * 4 + j
                        kb.tr(PS[0][:, j * 128:j * 128 + nrows], xi[0:nrows, c * 128:(c + 1) * 128], identF[0:nrows, 0:nrows],
                              r=[xi.name, "identF"], w=[PSK[0]], inc=(j == 3))
                    kb.V(lambda: nc.vector.tensor_copy(out=xT[:, c4 * 4:c4 * 4 + 4, c_at:c_at + nrows],
                                                       in_=PS[0][:].rearrange("p (j t) -> p j t", t=128)[:, :, 0:nrows]),
                         r=[PSK[0]], w=["xT"])

            def store_y_tokmajor(dst_rows_ap, nrows, c_at):
                xi = rr(xin, 'x')
                for c4 in range(4):
                    for j in range(4):
                        c = c4 * 4 + j
                        kb.tr(PS[0][0:nrows, j * 128:(j + 1) * 128], xT[:, c, c_at:c_at + nrows], identF[:, :],
                              r=["xT", "identF"], w=[PSK[0]], inc=(j == 3))
                    kb.V(lambda: nc.vector.tensor_copy(out=xi[0:nrows, c4 * 512:(c4 + 1) * 512], in_=PS[0][0:nrows, :]), r=[PSK[0]], w=[xi.name])
                kb.dma('sp', dst_rows_ap, xi[0:nrows, :], r=[xi.name])

            def out_parts(dst3, chunk, sg):
                for (c0, c1, sc0) in parts:
                    kb.dma('sp', dst3[chunk, :, sc0:sc0 + (c1 - c0)], sg[:, c0:c1], r=[sg.name])

            def in_proj():
                win = w_in.rearrange("(kc p) f -> p kc f", p=128)
                jobs = []

                def fm_job(colw, ncols2, evac, fin):
                    def ld(buf):
                        wload(buf, [128, 16, ncols2], win[:, :, colw:colw + ncols2])

                    def use(buf):
                        v = buf[:, 0:16 * ncols2].rearrange("p (k m) -> p k m", m=ncols2)
                        for j in range((ncols2 + 127) // 128):
                            m = min(128, ncols2 - j * 128)
                            pb = 1 + (cnt['a'] % 4); cnt['a'] += 1
                            for gi, (c0, c1, iss) in enumerate(groups):
                                o_, k_ = gout(gi, pb)
                                for kc in range(16):
                                    kb.mm(o_(m), v[:, kc, j * 128:j * 128 + m], hT[:, kc, c0:c1], kc == 0, kc == 15, r=[buf.name, "hT"], w=[k_])
                                evac(lambda: o_(m), k_, j, c0, c1, iss)
                            fin(j)
                    jobs.append((ld, use))

                def simple(dst3, chunk_of, actfn):
                    state = {}

                    def evac(o_, k_, j, c0, c1, iss):
                        if 'sg' not in state:
                            state['sg'] = rr(stg, 's')
                        sg = state['sg']
                        actfn(sg[:, c0:c1], o_(), k_, sg.name)

                    def fin(j):
                        out_parts(dst3, chunk_of(j), state.pop('sg'))
                    return evac, fin

                for cc in range(16):
                    ev, fin = simple(zsT, (lambda j, cc=cc: cc * 2 + j),
                                     lambda o, i, k, nm: kb.A(lambda: nc.scalar.activation(out=o, in_=i, func=AF.Silu), r=[k], w=[nm]))
                    fm_job(OFF_Z + cc * 256, 256, ev, fin)
                for cc in range(24):
                    state = {}

                    def evac(o_, k_, j, c0, c1, iss, cc=cc, state=state):
                        ch = cc * 2 + j
                        if 'sg' not in state:
                            state['sg'] = rr(stg, 's')
                        sg = state['sg']
                        if not iss:
                            nn = c1 - c0
                            xp = rr(tmpA, 'a'); acc = rr(tmpB, 'b')
                            kb.V(lambda: nc.vector.tensor_copy(out=xp[:, 0:3], in_=halo[:, ch, :]), r=["halo"], w=[xp.name])
                            kb.A(lambda: nc.scalar.copy(out=xp[:, 3:3 + nn], in_=o_()), r=[k_], w=[xp.name])
                            kb.V(lambda: nc.vector.tensor_copy(out=halo[:, ch, :], in_=xp[:, nn:nn + 3]), r=[xp.name], w=["halo"])
                            kb.V(lambda: nc.vector.tensor_scalar(out=acc[:, 0:nn], in0=xp[:, 0:nn], scalar1=convw_c[:, 0, ch:ch + 1], scalar2=None, op0=ALU.mult),
                                 r=[xp.name, "convw_c"], w=[acc.name])
                            for tap in range(1, 4):
                                kb.V(lambda: nc.vector.scalar_tensor_tensor(out=acc[:, 0:nn], in0=xp[:, tap:tap + nn], scalar=convw_c[:, tap, ch:ch + 1],
                                                                            in1=acc[:, 0:nn], op0=ALU.mult, op1=ALU.add), r=[xp.name, acc.name, "convw_c"], w=[acc.name])
                            if col0 + np_ == S:
                                kb.tr(PS[0][0:3, 0:128], xp[:, nn:nn + 3], identF[:, :], r=[xp.name, "identF"], w=[PSK[0]])
                                cs = cstl[0]
                                kb.V(lambda: nc.vector.tensor_copy(out=cs[0:3, 0:128], in_=PS[0][0:3, 0:128]), r=[PSK[0]], w=[cs.name])
                                kb.dma('sp', conv_p[:, ch * 128:(ch + 1) * 128], cs[0:3, 0:128], r=[cs.name])
                            kb.A(lambda: nc.scalar.activation(out=sg[:, c0:c1], in_=acc[:, 0:nn], func=AF.Silu, bias=convb_c[:, ch:ch + 1], scale=1.0),
                                 r=[acc.name, "convb_c"], w=[sg.name])
                        else:
                            xv = xps[:, 0:28].rearrange("p (b t) -> p b t", t=7)
                            av = accs[:, 0:16].rearrange("p (b t) -> p b t", t=4)
                            cs = cstl[0]
                            kb.dma('sp', cs[0:12, 0:128], st_conv.rearrange("b t c -> (b t) c")[:, ch * 128:(ch + 1) * 128], w=[cs.name])
                            kb.tr(PS[0][:, 0:12], cs[0:12, 0:128], identF[0:12, 0:12], r=[cs.name, "identF"], w=[PSK[0]])
                            kb.V(lambda: nc.vector.tensor_copy(out=xv[:, :, 0:3], in_=PS[0][:, 0:12].rearrange("p (b t) -> p b t", t=3)), r=[PSK[0]], w=["xps"])
                            kb.A(lambda: nc.scalar.copy(out=xv[:, :, 3:7], in_=o_().rearrange("p (b t) -> p b t", t=4)), r=[k_], w=["xps"])
                            kb.V(lambda: nc.vector.tensor_scalar(out=av, in0=xv[:, :, 0:4], scalar1=convw_c[:, 0, ch:ch + 1], scalar2=None, op0=ALU.mult),
                                 r=["xps", "convw_c"], w=["accs"])
                            for tap in range(1, 4):
                                kb.V(lambda: nc.vector.scalar_tensor_tensor(out=av, in0=xv[:, :, tap:tap + 4], scalar=convw_c[:, tap, ch:ch + 1],
                                                                            in1=av, op0=ALU.mult, op1=ALU.add), r=["xps", "accs", "convw_c"], w=["accs"])
                            cs2 = cstl[1]
                            kb.V(lambda: nc.vector.tensor_copy(out=cs2[:, 0:12].rearrange("p (b t) -> p b t", t=3), in_=xv[:, :, 4:7]), r=["xps"], w=[cs2.name])
                            kb.tr(PS[0][0:12, 0:128], cs2[:, 0:12], identF[:, :], r=[cs2.name, "identF"], w=[PSK[0]])
                            cs3 = cstl[2]
                            kb.V(lambda: nc.vector.tensor_copy(out=cs3[0:12, 0:128], in_=PS[0][0:12, 0:128]), r=[PSK[0]], w=[cs3.name])
                            kb.dma('sp', conv_s.rearrange("b t c -> (b t) c")[:, ch * 128:(ch + 1) * 128], cs3[0:12, 0:128], r=[cs3.name])
                            kb.A(lambda: nc.scalar.activation(out=sg[:, c0:c1], in_=accs[:, 0:16], func=AF.Silu, bias=convb_c[:, ch:ch + 1], scale=1.0),
                                 r=["accs", "convb_c"], w=[sg.name])

                    def fin(j, cc=cc, state=state):
                        out_parts(xbcT, cc * 2 + j, state.pop('sg'))
                    fm_job(OFF_XBC + cc * 256, 256, evac, fin)
                dstate = {}

                def evac_dt(o_, k_, j, c0, c1, iss):
                    if 't2' not in dstate:
                        dstate['t1'] = rr(tmpB, 'b'); dstate['t2'] = rr(tmpA, 'a')
                    t, t2 = dstate['t1'], dstate['t2']
                    kb.A(lambda: nc.scalar.activation(out=t[0:64, c0:c1], in_=o_(), func=AF.Exp, bias=dtb_c[:, 0:1], scale=1.0),
                         r=[k_, "dtb_c"], w=[t.name])
                    kb.A(lambda: nc.scalar.activation(out=t2[0:64, c0:c1], in_=t[0:64, c0:c1], func=AF.Ln, bias=onesF[0:64, 0:1], scale=1.0),
                         r=[t.name, "onesF"], w=[t2.name])

                def fin_dt(j):
                    t2 = dstate.pop('t2'); t = dstate.pop('t1')
                    for (c0, c1, sc0) in parts:
                        kb.dma('sp', dtT[:, sc0:sc0 + (c1 - c0)], t2[0:64, c0:c1], r=[t2.name])
                    kb.V(lambda: nc.vector.tensor_scalar(out=t[0:64, 0:n], in0=t2[0:64, 0:n], scalar1=a_c[:, 0:1], scalar2=None, op0=ALU.mult),
                         r=[t2.name, "a_c"], w=[t.name])
                    for (c0, c1, sc0) in parts:
                        kb.dma('sp', laT[:, sc0:sc0 + (c1 - c0)], t[0:64, c0:c1], r=[t.name])
                fm_job(OFF_DT, 64, evac_dt, fin_dt)
                for (off, dst3, scl) in ((OFF_Q, qT, 128.0 ** -0.5), (OFF_K, kT, 1.0)):
                    for cc in range(6):
                        ev, fin = simple(dst3, (lambda j, cc=cc: cc * 2 + j),
                                         lambda o, i, k, nm, scl=scl: kb.A(lambda: nc.scalar.mul(out=o, in_=i, mul=scl), r=[k], w=[nm]))
                        fm_job(off + cc * 256, 256, ev, fin)
                for cc in range(16):
                    ev, fin = simple(gT, (lambda j, cc=cc: cc * 2 + j),
                                     lambda o, i, k, nm: kb.A(lambda: nc.scalar.activation(out=o, in_=i, func=AF.Sigmoid), r=[k], w=[nm]))
                    fm_job(OFF_G + cc * 256, 256, ev, fin)
                blocks = [(b0, min(128, np_ - b0), False, col0 + b0) for b0 in range(0, np_, 128)] + ([(np_, NS, True, 0)] if has_s else [])
                for cg in range(12):
                    def ld(buf, cg=cg):
                        wload(buf, [128, 16, 256], win[:, :, OFF_K + cg * 256:OFF_K + (cg + 1) * 256])

                    def use(buf, cg=cg):
                        v = buf[:, 0:4096].rearrange("p (k m) -> p k m", m=256)
                        isv = cg >= 6
                        cgl = cg - 6 if isv else cg
                        g = cgl // 2
                        for (b0, m, iss, tok0) in blocks:
                            pb = 1 + (cnt['a'] % 4); cnt['a'] += 1
                            for kc in range(16):
                                kb.mm(PS[pb][0:m, 0:256], hT[:, kc, b0:b0 + m], v[:, kc, :], kc == 0, kc == 15,
                                      r=[buf.name, "hT"], w=[PSK[pb]])
                            t = rr(tmpB, 'b')
                            kb.A(lambda: nc.scalar.copy(out=t[0:m, 0:256], in_=PS[pb][0:m, 0:256]), r=[PSK[pb]], w=[t.name])
                            if iss:
                                kb.dma('sp', kvnew[:, 1 if isv else 0, cgl * 256:(cgl + 1) * 256], t[0:m, 0:256], r=[t.name])
                                vrow0 = S
                            else:
                                win_g = min(DILS[g][0], S)
                                if tok0 >= S - win_g:
                                    r0 = tok0 - (S - win_g)
                                    kb.dma('sp', kv_p[g][r0:r0 + m, 1 if isv else 0, (cgl % 2) * 256:(cgl % 2 + 1) * 256], t[0:m, 0:256], r=[t.name])
                                vrow0 = tok0
                            if isv:
                                sg = rr(stg, 's')
                                kb.V(lambda: nc.vector.tensor_copy(out=sg[0:m, 0:256], in_=t[0:m, 0:256]), r=[t.name], w=[sg.name])
                                kb.dma('sp', vtok[vrow0:vrow0 + m, cgl * 256:(cgl + 1) * 256], sg[0:m, 0:256], r=[sg.name])
                    jobs.append((ld, use))
                run_jobs(jobs)

            def mix_out():
                mrg = hT
                for (c0, c1, sc0) in parts:
                    for c in range(32):
                        kb.dma('sp', aT[:, c, c0:c1], ynT[c, :, sc0:sc0 + (c1 - c0)], w=[("aTl", c, c0)], after=["aT"])
                    for c in range(4):
                        kb.dma('sp', aT[:, 32 + c, c0:c1], attT[c, :, sc0:sc0 + (c1 - c0)], w=[("aTl", 32 + c, c0)], after=["aT"])
                aT_keys = ["aT"] + [("aTl", c, c0) for c in range(36) for (c0, c1, sc0) in parts]
                jobs = []
                ws = w_ssm.rearrange("(kc p) d -> p kc d", p=128)
                wat = w_att.rearrange("(kc p) d -> p kc d", p=128)
                wo = w_out.rearrange("(kc p) d -> p kc d", p=128)
                for d in range(16):
                    dst_ = {}

                    def ld1(buf, d=d):
                        wload(buf, [128, 32, 128], ws[:, :, d * 128:(d + 1) * 128])

                    def use1(buf, d=d, dst_=dst_):
                        v = buf[:, 0:4096].rearrange("p (k m) -> p k m", m=128)
                        for gi, (c0, c1, iss) in enumerate(groups):
                            dst_[gi] = gout(gi, 1 + 2 * (d % 2))
                            o1, k1 = dst_[gi]
                            for kc in range(32):
                                kb.mm(o1(), v[:, kc, :], aT[:, kc, c0:c1], kc == 0, kc == 31, r=[buf.name] + (aT_keys if d == 0 else ["aT"]), w=[k1])
                    jobs.append((ld1, use1))

                    def ld(buf, d=d):
                        wload(buf, [128, 4, 128], wat[:, :, d * 128:(d + 1) * 128])

                    def use(buf, d=d, dst_=dst_):
                        v = buf[:, 0:512].rearrange("p (k m) -> p k m", m=128)
                        g1 = rr(stg, 's'); g2 = rr(stg, 's')
                        for (c0, c1, sc0) in parts:
                            kb.dma('sp', g1[:, c0:c1], gT[d, :, sc0:sc0 + (c1 - c0)], w=[g1.name])
                            kb.dma('sp', g2[:, c0:c1], gT[16 + d, :, sc0:sc0 + (c1 - c0)], w=[g2.name])
                        for gi, (c0, c1, iss) in enumerate(groups):
                            o1, k1 = dst_[gi]; o2, k2 = gout(gi, 2 + 2 * (d % 2))
                            for kc in range(4):
                                kb.mm(o2(), v[:, kc, :], aT[:, 32 + kc, c0:c1], kc == 0, kc == 3, r=[buf.name, "aT"], w=[k2])
                            t1 = rr(tmpB, 'b'); t2 = rr(tmpA, 'a')
                            kb.V(lambda: nc.vector.tensor_tensor(out=t1[:, c0:c1], in0=o1(), in1=g1[:, c0:c1], op=ALU.mult), r=[k1, g1.name], w=[t1.name])
                            kb.V(lambda: nc.vector.tensor_tensor(out=t2[:, c0:c1], in0=o2(), in1=g2[:, c0:c1], op=ALU.mult), r=[k2, g2.name], w=[t2.name])
                            kb.V(lambda: nc.vector.tensor_tensor(out=mrg[:, d, c0:c1], in0=t1[:, c0:c1], in1=t2[:, c0:c1], op=ALU.add), r=[t1.name, t2.name], w=["hT"])
                    jobs.append((ld, use))
                for d2 in range(8):
                    def ld(buf, d2=d2):
                        wload(buf, [128, 16, 256], wo[:, :, d2 * 256:(d2 + 1) * 256])

                    def use(buf, d2=d2):
                        v = buf[:, 0:4096].rearrange("p (k m) -> p k m", m=256)
                        for j in range(2):
                            for gi, (c0, c1, iss) in enumerate(groups):
                                o_, k_ = gout(gi, 3 + j)
                                for kc in range(16):
                                    kb.mm(o_(), v[:, kc, j * 128:(j + 1) * 128], mrg[:, kc, c0:c1], kc == 0, kc == 15, r=[buf.name, "hT"], w=[k_])
                                kb.A(lambda: nc.scalar.copy(out=oT[:, d2 * 2 + j, c0:c1], in_=o_()), r=[k_], w=["oT"])
                    jobs.append((ld, use))
                run_jobs(jobs)

            if which == 1:
                for blk in range(np_ // 128):
                    load_x_tokmajor(x_p[col0 + blk * 128:col0 + (blk + 1) * 128, :], 128, blk * 128)
                if has_s:
                    load_x_tokmajor(x_s, NS, np_)
                pre_norm(0)
                ffn(0)
                post_res(0)
                for (c0, c1, sc0) in parts:
                    for c in range(16):
                        kb.dma('sp', x1T[c, :, sc0:sc0 + (c1 - c0)], xT[:, c, c0:c1], r=["xT"])
                pre_norm(1)
                in_proj()
            else:
                for (c0, c1, sc0) in parts:
                    for c in range(16):
                        kb.dma('sp', xT[:, c, c0:c1], x1T[c, :, sc0:sc0 + (c1 - c0)], w=[("xTl", c, c0)], after=["xT"])
                fold_keys(kb, "xT", [("xTl", c, c0) for c in range(16) for (c0, c1, sc0) in parts])
                mix_out()
                post_res(1)
                pre_norm(2)
                ffn(1)
                post_res(2)
                for blk in range(np_ // 128):
                    store_y_tokmajor(y_p[col0 + blk * 128:col0 + (blk + 1) * 128, :], 128, blk * 128)
                if has_s:
                    store_y_tokmajor(y_s, NS, np_)

    def phase2():
        EXP, LN = AF.Exp, AF.Ln

        def gate_norm_store(st_, yg, zs, n, col0, tag):
            sqall = sb("gsq" + tag, [128, 32, n], BF16, st=st_)
            rs = sb("grs" + tag, [128, n], st=st_)
            ynb = sb("gyn" + tag, [128, 32, n], BF16, st=st_)
            epsc2 = sb("geps" + tag, [128, 1], st=st_)
            kb.G(lambda: nc.gpsimd.memset(epsc2[:], EPS), w=[epsc2.name])

            def run(col0=col0):
                kb.V(lambda: nc.vector.tensor_tensor(out=yg[:], in0=yg[:], in1=zs[:], op=ALU.mult), r=[yg.name, zs.name], w=[yg.name])
                kb.A(lambda: nc.scalar.activation(out=sqall[:], in_=yg[:], func=AF.Square), r=[yg.name], w=[sqall.name])
                for c in range(32):
                    kb.mm(PS[7][:, 0:n], onesB[:], sqall[:, c, :], c == 0, c == 31, r=[sqall.name, "onesB"], w=[PSK[7]])
                kb.A(lambda: nc.scalar.activation(out=rs[:], in_=PS[7][:, 0:n], func=AF.Sqrt, bias=epsc2[:, 0:1], scale=1.0 / DIN),
                     r=[PSK[7], epsc2.name], w=[rs.name])
                kb.V(lambda: nc.vector.reciprocal(out=rs[:], in_=rs[:]), r=[rs.name], w=[rs.name])
                kb.V(lambda: nc.vector.tensor_tensor(out=yg[:], in0=yg[:], in1=gssm_c[:].unsqueeze(2).to_broadcast([128, 32, n]), op=ALU.mult),
                     r=[yg.name, "gssm_c"], w=[yg.name])
                kb.V(lambda: nc.vector.tensor_tensor(out=ynb[:], in0=yg[:], in1=rs[:].unsqueeze(1).to_broadcast([128, 32, n]), op=ALU.mult),
                     r=[yg.name, rs.name], w=[ynb.name])
                kb.dma('sp', ynT[:, :, col0:col0 + n].rearrange("c p t -> p c t"), ynb[:], r=[ynb.name])
            return run

        with ExitStack() as s2:
            xsT = sb("xsT", [128, 32, 128], BF16, st=s2); bcT = sb("bcT", [128, 16, 128], BF16, st=s2)
            zsc = sb("zsc", [128, 32, 128], BF16, st=s2)
            dtl = sb("dtl", [64, 2, 128], st=s2); dtlk = sb("dtlk", [128, 128], st=s2); decd = sb("decd", [128, 128], st=s2)
            dtde = sb("dtde", [128, 64], st=s2)
            xdtp = sb("xdtp", [128, 64, 128], BF16, st=s2); xdd = sb("xdd", [128, 4096], BF16, st=s2)
            btok = sb("btok", [128, 1024], BF16, st=s2)
            rhsL = sb("rhsL", [128, 64, 128], st=s2)
            MT = sb("MT", [128, 64, 128], BF16, st=s2); CsT = sb("CsT", [128, 64, 128], BF16, st=s2)
            CBm = sb("CBm", [128, 8, 128], st=s2)
            h32 = sb("h32", [128, 4096], st=s2); hpad = sb("hpad", [128, 64, 128], BF16, st=s2)
            yg = sb("yg", [128, 32, 128], st=s2)
            et = [sb("et%d" % i, [128, 512], st=s2) for i in range(4)]
            gns = gate_norm_store(s2, yg, zsc, 128, 0, "p")
            kb.G(lambda: nc.gpsimd.memset(xdtp[:], 0.0), w=["xdtp"])
            kb.G(lambda: nc.gpsimd.memset(hpad[:], 0.0), w=["hpad"])
            kb.G(lambda: nc.gpsimd.memset(h32[:], 0.0), w=["h32"])
            lak = dtlk[:, 64:128]; dtk = dtlk[:, 0:64]
            for c in range(S // 128):
                c0 = c * 128
                kb.dma('sp', xsT[:], xbcT[0:32, :, c0:c0 + 128].rearrange("c p t -> p c t"), w=["xsT"])
                kb.dma('sp', bcT[:], xbcT[32:48, :, c0:c0 + 128].rearrange("c p t -> p c t"), w=["bcT"])
                kb.dma('sp', zsc[:], zsT[:, :, c0:c0 + 128].rearrange("c p t -> p c t"), w=["zsc"])
                kb.dma('sp', dtl[:, 0, :], dtT[:, c0:c0 + 128], w=["dtl"])
                kb.dma('sp', dtl[:, 1, :], laT[:, c0:c0 + 128], w=["dtl"])
                kb.tr(PS[0][:, 0:64], dtl[:, 0, :], identF[0:64, 0:64], r=["dtl", "identF"], w=[PSK[0]], inc=False)
                kb.tr(PS[0][:, 64:128], dtl[:, 1, :], identF[0:64, 0:64], r=["dtl", "identF"], w=[PSK[0]])
                kb.V(lambda: nc.vector.tensor_copy(out=dtlk[:], in_=PS[0][:, 0:128]), r=[PSK[0]], w=["dtlk"])
                kb.G(lambda: nc.gpsimd.tensor_tensor(out=rhsL[:], in0=lak.unsqueeze(2).to_broadcast([128, 64, 128]),
                                                     in1=mle[:].unsqueeze(1).to_broadcast([128, 64, 128]), op=ALU.mult), r=["dtlk", "mle"], w=["rhsL"])
                kb.mm(PS[0][:, 0:64], mgt[:], lak, True, True, r=["mgt", "dtlk"], w=[PSK[0]], inc=False)
                kb.mm(PS[0][:, 64:128], onesF[:], lak, True, True, r=["onesF", "dtlk"], w=[PSK[0]])
                kb.A(lambda: nc.scalar.activation(out=decd[:], in_=PS[0][:, 0:128], func=EXP), r=[PSK[0]], w=["decd"])
                kb.V(lambda: nc.vector.tensor_tensor(out=dtde[:], in0=dtk, in1=decd[:, 0:64], op=ALU.mult), r=["dtlk", "decd"], w=["dtde"])
                for q4 in range(4):
                    pb = 5 + q4 % 2
                    for j in range(8):
                        kb.tr(psbf(pb)[:, j * 128:(j + 1) * 128], xsT[:, q4 * 8 + j, :], identB[:], r=["xsT", "identB"], w=[PSK[pb]], inc=(j == 7))
                    pv = psbf(pb).rearrange("p (hp two m) -> p hp two m", two=2, m=64)
                    for two in range(2):
                        dsel = dtk[:, q4 * 16:(q4 + 1) * 16].rearrange("p (hp two) -> p hp two", two=2)[:, :, two].unsqueeze(2).to_broadcast([128, 8, 64])
                        esel = dtde[:, q4 * 16:(q4 + 1) * 16].rearrange("p (hp two) -> p hp two", two=2)[:, :, two].unsqueeze(2).to_broadcast([128, 8, 64])
                        o1 = xdtp[:, q4 * 16:(q4 + 1) * 16, :].rearrange("p (hp two) m -> p hp two m", two=2)[:, :, two, two * 64:(two + 1) * 64]
                        o2 = xdd[:, q4 * 1024:(q4 + 1) * 1024].rearrange("p (hp two m) -> p hp two m", two=2, m=64)[:, :, two, :]
                        kb.V(lambda: nc.vector.tensor_tensor(out=o1, in0=pv[:, :, two, :], in1=dsel, op=ALU.mult), r=[PSK[pb], "dtlk"], w=["xdtp"])
                        kb.V(lambda: nc.vector.tensor_tensor(out=o2, in0=pv[:, :, two, :], in1=esel, op=ALU.mult), r=[PSK[pb], "dtde"], w=["xdd"])
                for g in range(8):
                    kb.tr(psbf(7)[:, g * 128:(g + 1) * 128], bcT[:, g, :], identB[:], r=["bcT", "identB"], w=[PSK[7]], inc=(g == 7))
                kb.V(lambda: nc.vector.tensor_copy(out=btok[:], in_=psbf(7)), r=[PSK[7]], w=["btok"])
                for g in range(8):
                    pb = 5 + g // 4
                    kb.mm(PS[pb][:, (g % 4) * 128:(g % 4 + 1) * 128], bcT[:, g, :], bcT[:, 8 + g, :], True, True, r=["bcT"], w=[PSK[pb]], inc=(g % 4 == 3))
                for gg in range(2):
                    kb.V(lambda: nc.vector.tensor_tensor(out=CBm[:, gg * 4:(gg + 1) * 4, :], in0=PS[5 + gg][:].rearrange("p (g i) -> p g i", i=128),
                                                         in1=mle[:].unsqueeze(1).to_broadcast([128, 4, 128]), op=ALU.mult), r=[PSK[5 + gg], "mle"], w=["CBm"])
                for b16 in range(16):
                    h0 = b16 * 4; g = b16 // 2
                    pa = 1 + b16 % 2; pc = 3 + b16 % 2
                    rv = rhsL[:, h0:h0 + 4, :].rearrange("p h i -> p (h i)")
                    kb.mm(PS[pa][:], mgt[:], rv, True, True, r=["mgt", "rhsL"], w=[PSK[pa]])
                    kb.mm(PS[pc][:], onesF[:], rv, True, True, r=["onesF", "rhsL"], w=[PSK[pc]])
                    ea = et[2 * (b16 % 2)]; eb = et[1 + 2 * (b16 % 2)]
                    kb.A(lambda: nc.scalar.activation(out=ea[:], in_=PS[pa][:], func=EXP), r=[PSK[pa]], w=[ea.name])
                    kb.V(lambda: nc.vector.tensor_tensor(out=MT[:, h0:h0 + 4, :], in0=ea[:].rearrange("p (h i) -> p h i", i=128),
                                                         in1=CBm[:, g:g + 1, :].to_broadcast([128, 4, 128]), op=ALU.mult), r=[ea.name, "CBm"], w=["MT"])
                    kb.A(lambda: nc.scalar.activation(out=eb[:], in_=PS[pc][:], func=EXP), r=[PSK[pc]], w=[eb.name])
                    if b16 % 2 == 1:
                        kb.G(lambda: nc.gpsimd.tensor_tensor(out=CsT[:, h0:h0 + 4, :], in0=eb[:].rearrange("p (h i) -> p h i", i=128),
                                                             in1=bcT[:, 8 + g:9 + g, :].to_broadcast([128, 4, 128]), op=ALU.mult), r=[eb.name, "bcT"], w=["CsT"])
                    else:
                        kb.V(lambda: nc.vector.tensor_tensor(out=CsT[:, h0:h0 + 4, :], in0=eb[:].rearrange("p (h i) -> p h i", i=128),
                                                             in1=bcT[:, 8 + g:9 + g, :].to_broadcast([128, 4, 128]), op=ALU.mult), r=[eb.name, "bcT"], w=["CsT"])
                kb.V(lambda: nc.vector.tensor_tensor(out=yg[:], in0=xsT[:], in1=dskP[:].unsqueeze(2).to_broadcast([128, 32, 128]), op=ALU.mult),
                     r=["xsT", "dskP"], w=["yg"])
                for q8 in range(8):
                    pb = (7, 0)[q8 % 2]
                    for j in range(4):
                        hp = q8 * 4 + j
                        o = PS[pb][:, j * 128:(j + 1) * 128]
                        kb.mm(o, xdtp[:, 2 * hp, :], MT[:, 2 * hp, :], True, False, r=["xdtp", "MT"], w=[PSK[pb]])
                        kb.mm(o, xdtp[:, 2 * hp + 1, :], MT[:, 2 * hp + 1, :], False, False, r=["xdtp", "MT"], w=[PSK[pb]])
                        kb.mm(o, hpad[:, 2 * hp, :], CsT[:, 2 * hp, :], False, False, r=["hpad", "CsT"], w=[PSK[pb]])
                        kb.mm(o, hpad[:, 2 * hp + 1, :], CsT[:, 2 * hp + 1, :], False, True, r=["hpad", "CsT"], w=[PSK[pb]])
                    kb.V(lambda: nc.vector.tensor_tensor(out=yg[:, q8 * 4:(q8 + 1) * 4, :], in0=yg[:, q8 * 4:(q8 + 1) * 4, :],
                                                         in1=PS[pb][:].rearrange("p (j t) -> p j t", t=128), op=ALU.add), r=["yg", PSK[pb]], w=["yg"])
                if c == 1:
                    dbg("MT", MT[:], ["MT"]); dbg("CsT", CsT[:], ["CsT"]); dbg("CBm", CBm[:], ["CBm"])
                    dbg("yg", yg[:], ["yg"]); dbg("decd", decd[:], ["decd"]); dbg("dtlk", dtlk[:], ["dtlk"]);
                gns(c0)
                for g in range(8):
                    pb = 5 + g % 2
                    kb.mm(PS[pb][:], btok[:, g * 128:(g + 1) * 128], xdd[:, g * 512:(g + 1) * 512], True, True, r=["btok", "xdd"], w=[PSK[pb]])
                    hv = h32[:, g * 512:(g + 1) * 512].rearrange("p (h m) -> p h m", m=64)
                    kb.V(lambda: nc.vector.tensor_tensor(out=hv, in0=hv, in1=decd[:, 64 + 8 * g:72 + 8 * g].unsqueeze(2).to_broadcast([128, 8, 64]), op=ALU.mult),
                         r=["h32", "decd"], w=["h32"])
                    kb.V(lambda: nc.vector.tensor_tensor(out=h32[:, g * 512:(g + 1) * 512], in0=h32[:, g * 512:(g + 1) * 512], in1=PS[pb][:], op=ALU.add),
                         r=["h32", PSK[pb]], w=["h32"])
                    for two in range(2):
                        o1 = hpad[:, g * 8:(g + 1) * 8, :].rearrange("p (hp two) m -> p hp two m", two=2)[:, :, two, two * 64:(two + 1) * 64]
                        i1 = h32[:, g * 512:(g + 1) * 512].rearrange("p (hp two m) -> p hp two m", two=2, m=64)[:, :, two, :]
                        kb.V(lambda: nc.vector.tensor_copy(out=o1, in_=i1), r=["h32"], w=["hpad"])
            for q8 in range(8):
                for j in range(4):
                    hp = q8 * 4 + j
                    kb.tr(PS[1][:, j * 128:(j + 1) * 128], h32[:, hp * 128:(hp + 1) * 128], identF[:], r=["h32", "identF"], w=[PSK[1]], inc=(j == 3))
                kb.V(lambda: nc.vector.tensor_copy(out=et[0][:], in_=PS[1][:]), r=[PSK[1]], w=["et0"])
                kb.dma('sp', ssm_p[q8 * 4:(q8 + 1) * 4].rearrange("c p n -> p c n"), et[0][:].rearrange("p (c n) -> p c n", n=128), r=["et0"])

        kb.barrier()
        with ExitStack() as s3:
            hS = sb("hS", [128, 32, 128], st=s3); tmpS = sb("tmpS", [128, 32, 128], st=s3)
            xsS = sb("xsS", [128, 32, 16], BF16, st=s3); bcS = sb("bcS", [128, 16, 16], BF16, st=s3); zsS = sb("zsS", [128, 32, 16], BF16, st=s3)
            dtlS = sb("dtlS", [64, 2, 16], st=s3); expd2 = sb("expd2", [64, 32, 128], st=s3)
            dtP = sb("dtP", [128, 32, 16], st=s3); dAP = sb("dAP", [128, 32, 16], st=s3); dtx = sb("dtx", [128, 32, 16], st=s3)
            bctok = sb("bctok", [16, 2048], BF16, st=s3); SEL = sb("SEL", [16, 16, 128], BF16, st=s3)
            ygS = sb("ygS", [128, 32, 16], st=s3)
            gnsS = gate_norm_store(s3, ygS, zsS, 16, S, "s")
            kb.V(lambda: nc.vector.tensor_copy(out=expd2[:].rearrange("h c (two p) -> h (c two) p", two=2),
                                               in_=identF[0:64, 0:64].unsqueeze(2).to_broadcast([64, 64, 64])), r=["identF"], w=["expd2"])
            kb.V(lambda: nc.vector.tensor_copy(out=SEL[:], in_=identB[0:16, 0:16].unsqueeze(2).to_broadcast([16, 16, 128])), r=["identB"], w=["SEL"])
            kb.dma('sp', xsS[:], xbcT[0:32, :, S:S + 16].rearrange("c p t -> p c t"), w=["xsS"])
            kb.dma('sp', bcS[:], xbcT[32:48, :, S:S + 16].rearrange("c p t -> p c t"), w=["bcS"])
            kb.dma('sp', zsS[:], zsT[:, :, S:S + 16].rearrange("c p t -> p c t"), w=["zsS"])
            kb.dma('sp', dtlS[:, 0, :], dtT[:, S:S + 16], w=["dtlS"])
            kb.dma('sp', dtlS[:, 1, :], laT[:, S:S + 16], w=["dtlS"])
            for hp in range(32):
                kb.mm(PS[0][:, hp * 16:(hp + 1) * 16], expd2[:, hp, :], dtlS[:, 0, :], True, True, r=["expd2", "dtlS"], w=[PSK[0]], inc=(hp == 31))
            for hp in range(32):
                kb.mm(PS[1][:, hp * 16:(hp + 1) * 16], expd2[:, hp, :], dtlS[:, 1, :], True, True, r=["expd2", "dtlS"], w=[PSK[1]], inc=(hp == 31))
            kb.V(lambda: nc.vector.tensor_copy(out=dtP[:].rearrange("p c t -> p (c t)"), in_=PS[0][:]), r=[PSK[0]], w=["dtP"])
            kb.A(lambda: nc.scalar.activation(out=dAP[:].rearrange("p c t -> p (c t)"), in_=PS[1][:], func=EXP), r=[PSK[1]], w=["dAP"])
            kb.V(lambda: nc.vector.tensor_tensor(out=dtx[:], in0=dtP[:], in1=xsS[:], op=ALU.mult), r=["dtP", "xsS"], w=["dtx"])
            for g in range(16):
                kb.tr(psbf(2 + g // 8)[0:16, (g % 8) * 128:(g % 8 + 1) * 128], bcS[:, g, :], identB[:], r=["bcS", "identB"], w=[PSK[2 + g // 8]], inc=(g % 8 == 7))
            kb.V(lambda: nc.vector.tensor_copy(out=bctok[:, 0:1024], in_=psbf(2)[0:16, :]), r=[PSK[2]], w=["bctok"])
            kb.V(lambda: nc.vector.tensor_copy(out=bctok[:, 1024:2048], in_=psbf(3)[0:16, :]), r=[PSK[3]], w=["bctok"])
            dbg("dtP", dtP[:], ["dtP"]); dbg("dAP", dAP[:], ["dAP"]); dbg("dtx", dtx[:], ["dtx"]); dbg("bctok", bctok[:], ["bctok"])
            for b in range(4):
                kb.dma('sp', hS[:], st_ssm[b].rearrange("c p n -> p c n"), w=["hS"])
                for t in range(4):
                    col = 4 * b + t
                    for q in range(4):
                        kb.mm(PS[2 + q][:], SEL[:, col, :], bctok[:, q * 512:(q + 1) * 512], True, True, r=["SEL", "bctok"], w=[PSK[2 + q]])
                    kb.V(lambda: nc.vector.tensor_tensor(out=hS[:], in0=hS[:], in1=dAP[:, :, col:col + 1].to_broadcast([128, 32, 128]), op=ALU.mult),
                         r=["hS", "dAP"], w=["hS"])
                    for q in range(2):
                        kb.V(lambda: nc.vector.tensor_tensor(out=tmpS[:, q * 16:(q + 1) * 16, :].rearrange("p (g j) n -> p g j n", j=4),
                                                             in0=PS[2 + q][:].rearrange("p (g n) -> p g n", n=128).unsqueeze(2).to_broadcast([128, 4, 4, 128]),
                                                             in1=dtx[:, q * 16:(q + 1) * 16, col].rearrange("p (g j) -> p g j", j=4).unsqueeze(3).to_broadcast([128, 4, 4, 128]),
                                                             op=ALU.mult), r=[PSK[2 + q], "dtx"], w=["tmpS"])
                    kb.V(lambda: nc.vector.tensor_tensor(out=hS[:], in0=hS[:], in1=tmpS[:], op=ALU.add), r=["hS", "tmpS"], w=["hS"])
                    for q in range(2):
                        kb.V(lambda: nc.vector.tensor_tensor(out=tmpS[:, q * 16:(q + 1) * 16, :].rearrange("p (g j) n -> p g j n", j=4),
                                                             in0=hS[:, q * 16:(q + 1) * 16, :].rearrange("p (g j) n -> p g j n", j=4),
                                                             in1=PS[4 + q][:].rearrange("p (g n) -> p g n", n=128).unsqueeze(2).to_broadcast([128, 4, 4, 128]),
                                                             op=ALU.mult), r=[PSK[4 + q], "hS"], w=["tmpS"])
                    kb.V(lambda: nc.vector.tensor_reduce(out=ygS[:, :, col], in_=tmpS[:], axis=AX.X, op=ALU.add), r=["tmpS"], w=["ygS"])
                kb.dma('sp', ssm_s[b].rearrange("c p n -> p c n"), hS[:], r=["hS"])
            kb.V(lambda: nc.vector.tensor_tensor(out=tmpS[:, :, 0:16], in0=xsS[:], in1=dskP[:].unsqueeze(2).to_broadcast([128, 32, 16]), op=ALU.mult),
                 r=["xsS", "dskP"], w=["tmpS"])
            kb.V(lambda: nc.vector.tensor_tensor(out=ygS[:], in0=ygS[:], in1=tmpS[:, :, 0:16], op=ALU.add), r=["ygS", "tmpS"], w=["ygS"])
            dbg("ygS", ygS[:], ["ygS"])
            gnsS()

        kb.barrier()
        with ExitStack() as s4:
            QT = [sb("QT%d" % h, [128, S], BF16, st=s4) for h in range(4)]
            KT = [sb("KT%d" % h, [128, S], BF16, st=s4) for h in range(4)]
            Vb = [sb("Vb%d" % i, [128, 512], BF16, st=s4) for i in range(3)]
            dist = sb("dist", [128, 256], st=s4)
            bias4 = sb("bias4", [128, 4, 256], st=s4)
            Sb4 = [sb("Sb4_%d" % i, [128, 4, 256], st=s4) for i in range(2)]
            Pb4 = [sb("Pb4_%d" % i, [128, 4, 256], BF16, st=s4) for i in range(2)]
            PT4 = [sb("PT4_%d" % i, [128, 8, 128], BF16, st=s4) for i in range(2)]
            st4 = [sb("st4_%d" % i, [128, 16], st=s4) for i in range(2)]
            ostg = [sb("ostg%d" % i, [128, 516], st=s4) for i in range(2)]
            kb.dma('sp', dist[:], cdist, w=["dist"])
            blk_i = 0
            for g, (win_, dil) in enumerate(DILS):
                for h in range(4):
                    kb.dma('sp', QT[h][:], qT[g * 4 + h, :, 0:S], w=[QT[h].name])
                    kb.dma('sp', KT[h][:], kT[g * 4 + h, :, 0:S], w=[KT[h].name])
                    kb.V(lambda: nc.vector.tensor_scalar(out=bias4[:, h, :], in0=dist[:], scalar1=float(-SLOPES[g, h] * dil), scalar2=None, op0=ALU.mult),
                         r=["dist"], w=["bias4"])
                for h in range(4):
                    kb.G(lambda: nc.gpsimd.affine_select(out=bias4[:, h, :], in_=bias4[:, h, :], pattern=[[-1, 256]], compare_op=ALU.is_ge, fill=-1e30,
                                                         base=128, channel_multiplier=1), r=["bias4"], w=["bias4"])
                    kb.G(lambda: nc.gpsimd.affine_select(out=bias4[:, h, :], in_=bias4[:, h, :], pattern=[[1, 256]], compare_op=ALU.is_ge, fill=-1e30,
                                                         base=0, channel_multiplier=-1), r=["bias4"], w=["bias4"])
                M = S // dil
                for r_ in range(dil):
                    vprev = None
                    for nb in range(M // 128):
                        t0 = r_ + dil * 128 * nb
                        sl = slice(t0, t0 + dil * 127 + 1, dil)
                        slp = slice(t0 - dil * 128, t0 - dil * 128 + dil * 127 + 1, dil)
                        vcur = Vb[blk_i % 3]; par = blk_i % 2; blk_i += 1
                        kb.dma('sp', vcur[:], vtok[sl, g * 512:(g + 1) * 512], w=[vcur.name])
                        og = ostg[par]; Sb = Sb4[par]; Pb = Pb4[par]; PT = PT4[par]; st_ = st4[par]
                        kw = 0 if nb > 0 else 128
                        nblk = 2 if nb > 0 else 1
                        for h in range(4):
                            pa = 1 + h // 2; c_ = (h % 2) * 256
                            kb.mm(PS[pa][:, c_ + 128:c_ + 256], QT[h][:, sl], KT[h][:, sl], True, True, r=[QT[h].name, KT[h].name], w=[PSK[pa]],
                                  inc=(nb == 0 and h % 2 == 1))
                            if nb > 0:
                                kb.mm(PS[pa][:, c_:c_ + 128], QT[h][:, sl], KT[h][:, slp], True, True, r=[QT[h].name, KT[h].name], w=[PSK[pa]], inc=(h % 2 == 1))
                        for hh in range(2):
                            kb.V(lambda: nc.vector.tensor_tensor(out=Sb[:, hh * 2:hh * 2 + 2, kw:256], in0=PS[1 + hh][:].rearrange("p (h k) -> p h k", k=256)[:, :, kw:256],
                                                                 in1=bias4[:, hh * 2:hh * 2 + 2, kw:256], op=ALU.add), r=[PSK[1 + hh], "bias4"], w=[Sb.name])
                        mx = st_[:, 0:4]; ls = st_[:, 4:8]; rl = st_[:, 8:12]; lnl = st_[:, 12:16]
                        kb.V(lambda: nc.vector.tensor_reduce(out=mx, in_=Sb[:, :, kw:256], axis=AX.X, op=ALU.max), r=[Sb.name], w=[st_.name])
                        kb.V(lambda: nc.vector.tensor_tensor(out=Sb[:, :, kw:256], in0=Sb[:, :, kw:256], in1=mx.unsqueeze(2).to_broadcast([128, 4, 256 - kw]), op=ALU.subtract),
                             r=[Sb.name, st_.name], w=[Sb.name])
                        kb.A(lambda: nc.scalar.activation(out=Pb[:, :, kw:256], in_=Sb[:, :, kw:256], func=EXP), r=[Sb.name], w=[Pb.name])
                        kb.V(lambda: nc.vector.tensor_reduce(out=ls, in_=Pb[:, :, kw:256], axis=AX.X, op=ALU.add), r=[Pb.name], w=[st_.name])
                        for h in range(4):
                            for j in range(nblk):
                                cb = (j if nb > 0 else 1) * 128
                                kb.tr(psbf(3)[:, (h * 2 + j) * 128:(h * 2 + j + 1) * 128], Pb[:, h, cb:cb + 128], identB[:], r=[Pb.name, "identB"], w=[PSK[3]],
                                      inc=(h == 3 and j == nblk - 1))
                        kb.V(lambda: nc.vector.tensor_copy(out=PT[:].rearrange("p (h j) q -> p h j q", j=2)[:, :, 0:nblk, :],
                                                           in_=psbf(3).rearrange("p (h j q) -> p h j q", j=2, q=128)[:, :, 0:nblk, :]), r=[PSK[3]], w=[PT.name])
                        for h in range(4):
                            o = PS[4][:, h * 128:(h + 1) * 128]
                            if nb > 0:
                                kb.mm(o, PT[:, h * 2, :], vprev[:, h * 128:(h + 1) * 128], True, False, r=[PT.name, vprev.name], w=[PSK[4]])
                                kb.mm(o, PT[:, h * 2 + 1, :], vcur[:, h * 128:(h + 1) * 128], False, True, r=[PT.name, vcur.name], w=[PSK[4]], inc=(h == 3))
                            else:
                                kb.mm(o, PT[:, h * 2, :], vcur[:, h * 128:(h + 1) * 128], True, True, r=[PT.name, vcur.name], w=[PSK[4]], inc=(h == 3))
                        kb.V(lambda: nc.vector.reciprocal(out=rl, in_=ls), r=[st_.name], w=[st_.name])
                        kb.V(lambda: nc.vector.tensor_tensor(out=og[:, 0:512].rearrange("p (h d) -> p h d", d=128), in0=PS[4][:].rearrange("p (h d) -> p h d", d=128),
                                                             in1=rl.unsqueeze(2).to_broadcast([128, 4, 128]), op=ALU.mult), r=[PSK[4], st_.name], w=[og.name])
                        kb.A(lambda: nc.scalar.activation(out=lnl, in_=ls, func=LN), r=[st_.name], w=[st_.name])
                        kb.V(lambda: nc.vector.tensor_tensor(out=og[:, 512:516], in0=lnl, in1=mx, op=ALU.add), r=[st_.name], w=[og.name])
                        kb.dma('sp', oscr[g, sl, :], og[:], r=[og.name])
                        vprev = vcur
        kb.barrier()
        with ExitStack() as s5:
            om = [sb("om%d" % i, [128, 3, 516], st=s5) for i in range(2)]
            mw = sb("mw", [128, 32], st=s5); att = sb("att", [128, 512], st=s5); attb = sb("attb", [128, 512], BF16, st=s5)
            mt2 = sb("mt2", [128, 512], st=s5); asg = [sb("asg%d" % i, [128, 4, 128], BF16, st=s5) for i in range(2)]
            for tb in range(S // 128):
                o_ = om[tb % 2]
                kb.dma('sp', o_[:], oscr[:, tb * 128:(tb + 1) * 128, :].rearrange("g t f -> t g f"), w=[o_.name])
                mx = mw[:, 0:4]; e3 = mw[:, 4:16].rearrange("p (g h) -> p g h", h=4); ss = mw[:, 16:20]; rs_ = mw[:, 20:24]
                kb.V(lambda: nc.vector.tensor_tensor(out=mx, in0=o_[:, 0, 512:516], in1=o_[:, 1, 512:516], op=ALU.max), r=[o_.name], w=["mw"])
                kb.V(lambda: nc.vector.tensor_tensor(out=mx, in0=mx, in1=o_[:, 2, 512:516], op=ALU.max), r=[o_.name, "mw"], w=["mw"])
                kb.V(lambda: nc.vector.tensor_tensor(out=e3, in0=o_[:, :, 512:516], in1=mx.unsqueeze(1).to_broadcast([128, 3, 4]), op=ALU.subtract), r=[o_.name, "mw"], w=["mw"])
                kb.A(lambda: nc.scalar.activation(out=e3, in_=e3, func=EXP), r=["mw"], w=["mw"])
                kb.V(lambda: nc.vector.tensor_tensor(out=ss, in0=e3[:, 0, :], in1=e3[:, 1, :], op=ALU.add), r=["mw"], w=["mw"])
                kb.V(lambda: nc.vector.tensor_tensor(out=ss, in0=ss, in1=e3[:, 2, :], op=ALU.add), r=["mw"], w=["mw"])
                kb.V(lambda: nc.vector.reciprocal(out=rs_, in_=ss), r=["mw"], w=["mw"])
                kb.V(lambda: nc.vector.tensor_tensor(out=e3, in0=e3, in1=rs_.unsqueeze(1).to_broadcast([128, 3, 4]), op=ALU.mult), r=["mw"], w=["mw"])
                for g in range(3):
                    dst = att if g == 0 else mt2
                    kb.V(lambda: nc.vector.tensor_tensor(out=dst[:].rearrange("p (h d) -> p h d", d=128), in0=o_[:, g, 0:512].rearrange("p (h d) -> p h d", d=128),
                                                         in1=e3[:, g, :].unsqueeze(2).to_broadcast([128, 4, 128]), op=ALU.mult), r=[o_.name, "mw"], w=[dst.name])
                    if g > 0:
                        kb.V(lambda: nc.vector.tensor_tensor(out=att[:], in0=att[:], in1=mt2[:], op=ALU.add), r=["att", "mt2"], w=["att"])
                kb.V(lambda: nc.vector.tensor_copy(out=attb[:], in_=att[:]), r=["att"], w=["attb"])
                for h in range(4):
                    kb.tr(psbf(1)[:, h * 128:(h + 1) * 128], attb[:, h * 128:(h + 1) * 128], identB[:], r=["attb", "identB"], w=[PSK[1]], inc=(h == 3))
                a_ = asg[tb % 2]
                kb.V(lambda: nc.vector.tensor_copy(out=a_[:].rearrange("p h t -> p (h t)"), in_=psbf(1)[:, 0:512]), r=[PSK[1]], w=[a_.name])
                kb.dma('sp', attT[:, :, tb * 128:(tb + 1) * 128].rearrange("c p t -> p c t"), a_[:], r=[a_.name])

        kb.barrier()
        with ExitStack() as s6:
            qS = sb("qS", [128, 12, 16], BF16, st=s6); kS = sb("kS", [128, 12, 16], BF16, st=s6)
            Kc = [sb("Kc%d" % i, [128, 2, 512], st=s6) for i in range(4)]
            KTs = sb("KTs", [128, 4, 129], st=s6); qSf = sb("qSf", [128, 12, 16], st=s6); PTf = sb("PTf", [128, 4], st=s6); Vcb = sb("Vcb", [128, 512], BF16, st=s6)
            vrow = sb("vrow", [1, 512], st=s6); vrowb = sb("vrowb", [1, 512], BF16, st=s6)
            sbias = sb("sbias", [1, 3, 4, 129], st=s6); sd = sb("sd", [1, 129], st=s6)
            Srow = sb("Srow", [1, 4, 129], st=s6); Prb = sb("Prb", [1, 4, 129], BF16, st=s6)
            PTs = sb("PTs", [128, 4], BF16, st=s6); oneb = sb("oneb", [1, 1], BF16, st=s6)
            oTs = sb("oTs", [128, 3, 64], st=s6); lseS = sb("lseS", [1, 3, 64], st=s6); lS = sb("lS", [1, 3, 64], st=s6)
            sm = sb("sm", [1, 16], st=s6)
            kb.G(lambda: nc.gpsimd.memset(oneb[:], 1.0), w=["oneb"])
            kb.dma('sp', sd[:], cdist[0:1, 0:129], w=["sd"])
            for g in range(3):
                for h in range(4):
                    kb.V(lambda: nc.vector.tensor_scalar(out=sbias[:, g, h, :], in0=sd[:], scalar1=float(-SLOPES[g, h] * DILS[g][1]), scalar2=None, op0=ALU.mult),
                         r=["sd"], w=["sbias"])
            kb.dma('sp', qS[:], qT[:, :, S:S + 16].rearrange("c p t -> p c t"), w=["qS"])
            kb.dma('sp', kS[:], kT[:, :, S:S + 16].rearrange("c p t -> p c t"), w=["kS"])
            kb.V(lambda: nc.vector.tensor_copy(out=qSf[:], in_=qS[:]), r=["qS"], w=["qSf"])
            ui = 0
            for b in range(4):
                for g, (lb, dil) in enumerate(DILS):
                    kb.dma('sp', kv_s[g][b, 0:lb - 4], cache[g][b, 4:lb])
                    kb.dma('sp', kv_s[g][b, lb - 4:lb], kvnew[4 * b:4 * b + 4, :, g * 512:(g + 1) * 512])
                    for t in range(4):
                        col = 4 * b + t
                        kc = Kc[ui % 4]; ui += 1
                        ncache = 128 - t if dil == 1 else 128
                        kb.dma('sp', kc[0:ncache], cache[g][b, t:t + dil * (ncache - 1) + 1:dil], w=[kc.name])
                        if ncache < 128:
                            kb.dma('sp', kc[ncache:128], kvnew[4 * b:4 * b + t, :, g * 512:(g + 1) * 512], w=[kc.name])
                        kb.dma('sp', vrow[:], kvnew[col:col + 1, 1, g * 512:(g + 1) * 512], w=["vrow"])
                        for h in range(4):
                            kb.tr(PS[1][:, h * 128:(h + 1) * 128], kc[:, 0, h * 128:(h + 1) * 128], identF[:], r=[kc.name, "identF"], w=[PSK[1]], inc=(h == 3))
                        kb.V(lambda: nc.vector.tensor_copy(out=KTs[:, :, 0:128], in_=PS[1][:].rearrange("p (h m) -> p h m", m=128)), r=[PSK[1]], w=["KTs"])
                        kb.V(lambda: nc.vector.tensor_copy(out=KTs[:, :, 128:129], in_=kS[:, g * 4:(g + 1) * 4, col:col + 1]), r=["kS"], w=["KTs"])
                        for h in range(4):
                            pbk = 2 if h < 2 else 6
                            kb.mm(PS[pbk][0:1, (h % 2) * 129:(h % 2 + 1) * 129], qSf[:, g * 4 + h, col:col + 1], KTs[:, h, :], True, True,
                                  r=["qSf", "KTs"], w=[PSK[pbk]], inc=(h % 2 == 1))
                        for hh in range(2):
                            pbk = 2 if hh == 0 else 6
                            kb.V(lambda: nc.vector.tensor_tensor(out=Srow[:, hh * 2:hh * 2 + 2, :].rearrange("p h m -> p (h m)"), in0=PS[pbk][0:1, 0:258],
                                                                 in1=sbias[:, g, hh * 2:hh * 2 + 2, :].rearrange("p h m -> p (h m)"), op=ALU.add),
                                 r=[PSK[pbk], "sbias"], w=["Srow"])
                        mx = sm[:, 0:4]; ls = sm[:, 4:8]; lnl = sm[:, 8:12]
                        kb.V(lambda: nc.vector.tensor_reduce(out=mx, in_=Srow[:], axis=AX.X, op=ALU.max), r=["Srow"], w=["sm"])
                        kb.V(lambda: nc.vector.tensor_tensor(out=Srow[:], in0=Srow[:], in1=mx.unsqueeze(2).to_broadcast([1, 4, 129]), op=ALU.subtract), r=["Srow", "sm"], w=["Srow"])
                        kb.A(lambda: nc.scalar.activation(out=Srow[:], in_=Srow[:], func=EXP), r=["Srow"], w=["Srow"])
                        kb.V(lambda: nc.vector.tensor_reduce(out=ls, in_=Srow[:], axis=AX.X, op=ALU.add), r=["Srow"], w=["sm"])
                        for h in range(4):
                            kb.mm(PS[3][:, h:h + 1], Srow[:, h, 0:128], onesF[0:1, 0:1], True, True, r=["Srow", "onesF"], w=[PSK[3]], inc=(h == 3))
                        kb.V(lambda: nc.vector.tensor_copy(out=PTf[:], in_=PS[3][:, 0:4]), r=[PSK[3]], w=["PTf"])
                        for h in range(4):
                            o = PS[4][:, h:h + 1]
                            kb.mm(o, kc[:, 1, h * 128:(h + 1) * 128], PTf[:, h:h + 1], True, False, r=[kc.name, "PTf"], w=[PSK[4]])
                            kb.mm(o, vrow[:, h * 128:(h + 1) * 128], Srow[:, h, 128:129], False, True, r=["vrow", "Srow"], w=[PSK[4]])
                        kb.V(lambda: nc.vector.tensor_copy(out=oTs[:, g, col * 4:col * 4 + 4], in_=PS[4][:, 0:4]), r=[PSK[4]], w=["oTs"])
                        kb.A(lambda: nc.scalar.activation(out=lnl, in_=ls, func=LN), r=["sm"], w=["sm"])
                        kb.V(lambda: nc.vector.tensor_tensor(out=lseS[:, g, col * 4:col * 4 + 4], in0=lnl, in1=mx, op=ALU.add), r=["sm"], w=["lseS"])
                        kb.V(lambda: nc.vector.tensor_copy(out=lS[:, g, col * 4:col * 4 + 4], in_=ls), r=["sm"], w=["lS"])
            dbg("oTs", oTs[:], ["oTs"]); dbg("KTs", KTs[:], ["KTs"])
            mxs = sb("mxs", [1, 64], st=s6); es_ = sb("es_", [1, 3, 64], st=s6); sss = sb("sss", [1, 64], st=s6)
            coefB = sb("coefB", [128, 3, 64], st=s6); attS = sb("attS", [128, 64], st=s6); attSb = sb("attSb", [128, 4, 16], BF16, st=s6)
            kb.V(lambda: nc.vector.tensor_tensor(out=mxs[:], in0=lseS[:, 0, :], in1=lseS[:, 1, :], op=ALU.max), r=["lseS"], w=["mxs"])
            kb.V(lambda: nc.vector.tensor_tensor(out=mxs[:], in0=mxs[:], in1=lseS[:, 2, :], op=ALU.max), r=["lseS", "mxs"], w=["mxs"])
            kb.V(lambda: nc.vector.tensor_tensor(out=es_[:], in0=lseS[:], in1=mxs[:].unsqueeze(1).to_broadcast([1, 3, 64]), op=ALU.subtract), r=["lseS", "mxs"], w=["es_"])
            kb.A(lambda: nc.scalar.activation(out=es_[:], in_=es_[:], func=EXP), r=["es_"], w=["es_"])
            kb.V(lambda: nc.vector.tensor_tensor(out=sss[:], in0=es_[:, 0, :], in1=es_[:, 1, :], op=ALU.add), r=["es_"], w=["sss"])
            kb.V(lambda: nc.vector.tensor_tensor(out=sss[:], in0=sss[:], in1=es_[:, 2, :], op=ALU.add), r=["es_", "sss"], w=["sss"])
            kb.V(lambda: nc.vector.tensor_tensor(out=lS[:], in0=lS[:], in1=sss[:].unsqueeze(1).to_broadcast([1, 3, 64]), op=ALU.mult), r=["lS", "sss"], w=["lS"])
            kb.V(lambda: nc.vector.reciprocal(out=lS[:], in_=lS[:]), r=["lS"], w=["lS"])
            kb.V(lambda: nc.vector.tensor_tensor(out=es_[:], in0=es_[:], in1=lS[:], op=ALU.mult), r=["es_", "lS"], w=["es_"])
            kb.mm(PS[5][:, 0:192], onesF[0:1, :], es_[:].rearrange("p g c -> p (g c)"), True, True, r=["onesF", "es_"], w=[PSK[5]])
            kb.V(lambda: nc.vector.tensor_tensor(out=coefB[:].rearrange("p g c -> p (g c)"), in0=PS[5][:, 0:192], in1=oTs[:].rearrange("p g c -> p (g c)"), op=ALU.mult),
                 r=[PSK[5], "oTs"], w=["coefB"])
            kb.V(lambda: nc.vector.tensor_tensor(out=attS[:], in0=coefB[:, 0, :], in1=coefB[:, 1, :], op=ALU.add), r=["coefB"], w=["attS"])
            kb.V(lambda: nc.vector.tensor_tensor(out=attS[:], in0=attS[:], in1=coefB[:, 2, :], op=ALU.add), r=["coefB", "attS"], w=["attS"])
            kb.V(lambda: nc.vector.tensor_copy(out=attSb[:], in_=attS[:].rearrange("p (t h) -> p h t", h=4)), r=["attS"], w=["attSb"])
            kb.dma('sp', attT[:, :, S:S + 16].rearrange("c p t -> p c t"), attSb[:], r=["attSb"])

    with ExitStack() as st1:
        token_phase(st1, 1)
    if CFG.get("stop") == 1:
        kb.drain(); return
    kb.barrier()
    phase2()
    if CFG.get("stop") == 2:
        kb.drain(); return
    kb.barrier()
    with ExitStack() as st3:
        token_phase(st3, 3)

    kb.drain()


def PHASE2(nc, kb, env):
    pass


_NC_CACHE = {}


def _prep_inputs(inp, core):
    f = np.ascontiguousarray
    b = core
    m = {}
    m["cdist"] = np.ascontiguousarray((128 + np.arange(128)[:, None] - np.arange(256)[None, :]).astype(np.float32))
    m["x_p"] = f(inp["x_prompt"][b])
    m["x_s"] = f(inp["x_sample"][4 * b:4 * b + 4].reshape(NS, D))
    m["c5"] = f(np.concatenate([inp["c_prompt"][b:b + 1], inp["c_sample"][4 * b:4 * b + 4]], axis=0))
    m["st_ssm"] = f(inp["state_ssm"][0, 4 * b:4 * b + 4].reshape(4, 32, 128, 128))
    m["st_conv"] = f(inp["state_conv"][0, 4 * b:4 * b + 4])
    for g, nme in enumerate(("cache_kv_w128", "cache_kv_w512", "cache_kv_w2048")):
        a = inp[nme][0, 4 * b:4 * b + 4]
        m["cache%d" % g] = f(a.reshape(4, a.shape[1], 2, 512))
    m["w_ada"] = inp["w_ada"][0]
    m["b_ada"] = f(inp["b_ada"][0].reshape(144, 128))
    for n_ in ("g_pre_ffn1", "g_post_ffn1", "g_pre_mix", "g_post_mix", "g_pre_ffn2", "g_post_ffn2"):
        m[n_] = f(inp[n_][0].reshape(16, 128))
    for n_ in ("w_gu_ffn1", "w_gu_ffn2", "w_down_ffn1", "w_down_ffn2", "w_in", "w_ssm_proj", "w_att_proj", "w_out"):
        m[n_] = inp[n_][0]
    m["conv_w"] = f(inp["conv_w"][0].reshape(4, 48, 128))
    m["conv_b"] = f(inp["conv_b"][0].reshape(48, 128))
    for n_ in ("dt_bias", "a_log", "d_skip"):
        m[n_] = f(inp[n_][0].reshape(64, 1))
    m["g_ssm_norm"] = f(inp["g_ssm_norm"][0].reshape(32, 128))
    return m


def kernel(**inputs):
    inp = {k: np.asarray(v) for k, v in inputs.items()}
    if "nc" not in _NC_CACHE:
        _NC_CACHE["nc"] = build()
    nc = _NC_CACHE["nc"]
    in_maps = [_prep_inputs(inp, c) for c in range(8)]
    res = run_bass_kernel_spmd(nc, in_maps, core_ids=list(range(8)))
    R = res.results
    yp = np.stack([R[c]["y_p"] for c in range(8)])
    ys = np.concatenate([R[c]["y_s"].reshape(4, 4, D) for c in range(8)])
    ssm_p = np.stack([R[c]["ssm_p"].reshape(64, 64, 128) for c in range(8)])[None]
    ssm_s = np.concatenate([R[c]["ssm_s"].reshape(4, 64, 64, 128) for c in range(8)])[None]
    conv_p = np.stack([R[c]["conv_p"] for c in range(8)])[None]
    conv_s = np.concatenate([R[c]["conv_s"] for c in range(8)])[None]
    outs = [yp, ys, ssm_p, ssm_s, conv_p, conv_s]
    for g in range(3):
        kp = np.stack([R[c]["kv_p%d" % g].reshape(-1, 2, 4, 128) for c in range(8)])[None]
        ksm = np.concatenate([R[c]["kv_s%d" % g].reshape(4, -1, 2, 4, 128) for c in range(8)])[None]
        outs += [kp, ksm]
    return tuple(np.ascontiguousarray(o, dtype=np.float32) for o in outs)
```

```python
import numpy as np
from contextlib import ExitStack
import concourse.bass as bass
import concourse.mybir as mybir
from concourse.bass_utils import run_bass_kernel_spmd

F32 = mybir.dt.float32
BF16 = mybir.dt.bfloat16
AF = mybir.ActivationFunctionType
ALU = mybir.AluOpType
AX = mybir.AxisListType

D = 2048
S = 2048
NS = 16
NT = S + NS
DFF = 5632
NIN = 19008
DIN = 4096
EPS = 1e-6
OFF_Z, OFF_XBC, OFF_DT, OFF_Q, OFF_K, OFF_V, OFF_G = 0, 4096, 10240, 10304, 11840, 13376, 14912
DILS = ((128, 1), (512, 4), (2048, 16))
SLOPES = np.exp2(-8.0 * np.arange(1, 13, dtype=np.float32) / 12).astype(np.float32).reshape(3, 4)
NRING = 6
SAME_ENG_SYNC = True
WBUF_ELEMS = 4096
NWBUF = 5
CFG = {}


class KB:
    def __init__(s, nc, es):
        s.nc, s.es = nc, es
        s.eng = dict(pe=nc.tensor, act=nc.scalar, dve=nc.vector, pool=nc.gpsimd, sp=nc.sync)
        s.csem = {e: es.enter_context(nc.semaphore('c_' + e)) for e in ('pe', 'act', 'dve', 'pool')}
        s.cnt = {e: 0 for e in s.csem}
        s.dsem = {q: [es.enter_context(nc.semaphore('d_%s%d' % (q, i))) for i in range(NRING)] for q in ('sp', 'pool')}
        s.dcnt = {q: 0 for q in s.dsem}
        s.waited = {e: {} for e in s.eng}
        s.res = {}
        s.pend = []
        s.last_dma = {}
        s.multi = {}
        s.log = {e: [] for e in ('pe', 'act', 'dve', 'pool', 'sp')}

    def _wait(s, e, tok):
        name, sem, val, peng = tok
        if peng == e and not SAME_ENG_SYNC:
            return
        if s.waited[e].get(name, 0) >= val:
            return
        s.eng[e].wait_ge(sem, val)
        s.waited[e][name] = val
        s.log[e].append(('w', name, val))

    def _deps(s, e, r, w):
        for k in r:
            st = s.res.get(k)
            if st and st[0]:
                s._wait(e, st[0])
            for t in s.multi.get(k, ()):
                s._wait(e, t)
        for k in w:
            st = s.res.get(k)
            if st:
                if st[0]:
                    s._wait(e, st[0])
                for t in st[1].values():
                    s._wait(e, t)

    def _reg(s, tok, r, w):
        for k in r:
            st = s.res.setdefault(k, [None, {}])
            st[1][tok[0]] = tok
        for k in w:
            s.res[k] = [tok, {}]
            s.multi.pop(k, None)

    def op(s, e, fn, r=(), w=(), inc=True):
        if e != 'pe':
            assert not s.pend, "non-PE op emitted inside an open PE accumulation group"
        s._deps(e, r, w)
        inst = fn()
        if inc:
            s.cnt[e] += 1
            inst.then_inc(s.csem[e], 1)
            s.log[e].append(('i', 'c_' + e, 1))
            tok = ('c_' + e, s.csem[e], s.cnt[e], e)
            for (pr, pw) in s.pend:
                s._reg(tok, pr, pw)
            s.pend = []
            s._reg(tok, r, w)
        else:
            s.pend.append((tuple(r), tuple(w)))
        return inst

    def dma(s, q, out, in_, r=(), w=(), after=()):
        assert not s.pend
        s._deps(q, r, tuple(w) + tuple(after))
        i = s.dcnt[q]
        slot, uses = i % NRING, i // NRING
        sem = s.dsem[q][slot]
        name = 'd_%s%d' % (q, slot)
        if uses > 0:
            s._wait(q, (name, sem, 16 * uses, 'dma'))
        s.eng[q].dma_start(out=out, in_=in_).then_inc(sem, 16)
        s.log[q].append(('i', name, 16))
        s.dcnt[q] += 1
        tok = (name, sem, 16 * (uses + 1), 'dma')
        s._reg(tok, r, w)
        s.last_dma[name] = tok

    def drain(s):
        for tok in s.last_dma.values():
            s._wait('sp', tok)

    def barrier(s):
        toks = [('c_' + e, s.csem[e], s.cnt[e], e) for e in s.csem if s.cnt[e] > 0] + list(s.last_dma.values())
        for e in s.eng:
            for t in toks:
                name, sem, val, peng = t
                if s.waited[e].get(name, 0) >= val:
                    continue
                s.eng[e].wait_ge(sem, val)
                s.waited[e][name] = val
                s.log[e].append(('w', name, val))

    def fence(s, queues=('sp', 'pool')):
        for q in queues:
            for tok in list(s.last_dma.values()):
                s._wait(q, tok)

    def V(s, fn, r=(), w=()):
        return s.op('dve', fn, r, w)

    def A(s, fn, r=(), w=()):
        return s.op('act', fn, r, w)

    def G(s, fn, r=(), w=()):
        return s.op('pool', fn, r, w)

    def mm(s, out, lhsT, rhs, start, stop, r=(), w=(), inc=None):
        nc = s.nc
        return s.op('pe', lambda: nc.tensor.matmul(out, lhsT=lhsT, rhs=rhs, start=start, stop=stop), r, w,
                    inc=(stop if inc is None else inc))

    def tr(s, out, in_, ident, r=(), w=(), inc=True):
        nc = s.nc
        return s.op('pe', lambda: nc.tensor.transpose(out, in_, ident), r, w, inc=inc)


def fold_keys(kb, dst, srcs):
    st = kb.res.setdefault(dst, [None, {}])
    for k in srcs:
        s_ = kb.res.get(k)
        if s_ and s_[0]:
            st[1][s_[0][0]] = s_[0]
    kb.multi[dst] = [kb.res[k][0] for k in srcs if kb.res.get(k) and kb.res[k][0]]

def build(debug=False):
    nc = bass.Bass("TRN2", target_bir_lowering=False)
    es = ExitStack()
    with es:
        _build(nc, es, debug)
    return nc


def _build(nc, es, debug):
    kb = KB(nc, es)
    CFG['kb'] = kb

    def din(name, shape, dt=F32):
        return nc.dram_tensor(name, list(shape), dt, kind="ExternalInput").ap()

    def dout(name, shape, dt=F32):
        return nc.dram_tensor(name, list(shape), dt, kind="ExternalOutput").ap()

    def dscr(name, shape, dt=F32):
        return nc.dram_tensor(name, list(shape), dt, kind=("ExternalOutput" if debug else "Internal")).ap()

    def dbg(name, ap, keys):
        if not debug:
            return
        shp = list(ap.shape)
        t = nc.dram_tensor("dbg_" + name, shp, ap.dtype, kind="ExternalOutput").ap()
        kb.dma('sp', t, ap, r=keys)

    cdist = din("cdist", [128, 256])
    x_p = din("x_p", [S, D]); x_s = din("x_s", [NS, D]); c5 = din("c5", [5, D])
    st_ssm = din("st_ssm", [4, 32, 128, 128])
    st_conv = din("st_conv", [4, 3, 6144])
    cache = [din("cache%d" % g, [4, DILS[g][0], 2, 512]) for g in range(3)]
    w_ada = din("w_ada", [D, 18432]); b_ada = din("b_ada", [144, 128])
    gvec = {n: din(n, [16, 128]) for n in ("g_pre_ffn1", "g_post_ffn1", "g_pre_mix", "g_post_mix", "g_pre_ffn2", "g_post_ffn2")}
    w_gu = [din("w_gu_ffn1", [D, 2 * DFF]), din("w_gu_ffn2", [D, 2 * DFF])]
    w_dn = [din("w_down_ffn1", [DFF, D]), din("w_down_ffn2", [DFF, D])]
    w_in = din("w_in", [D, NIN])
    conv_w = din("conv_w", [4, 48, 128]); conv_b = din("conv_b", [48, 128])
    dt_bias = din("dt_bias", [64, 1]); a_log = din("a_log", [64, 1]); d_skip = din("d_skip", [64, 1])
    g_ssm = din("g_ssm_norm", [32, 128])
    w_ssm = din("w_ssm_proj", [DIN, D]); w_att = din("w_att_proj", [512, D]); w_out = din("w_out", [D, D])

    y_p = dout("y_p", [S, D]); y_s = dout("y_s", [NS, D])
    ssm_p = dout("ssm_p", [32, 128, 128]); ssm_s = dout("ssm_s", [4, 32, 128, 128])
    conv_p = dout("conv_p", [3, 6144]); conv_s = dout("conv_s", [4, 3, 6144])
    kv_p = [dout("kv_p%d" % g, [min(DILS[g][0], S), 2, 512]) for g in range(3)]
    kv_s = [dout("kv_s%d" % g, [4, DILS[g][0], 2, 512]) for g in range(3)]

    x1T = dscr("x1T", [16, 128, NT]); zsT = dscr("zsT", [32, 128, NT], BF16); xbcT = dscr("xbcT", [48, 128, NT], BF16)
    dtT = dscr("dtT", [64, NT]); laT = dscr("laT", [64, NT])
    qT = dscr("qT", [12, 128, NT], BF16); kT = dscr("kT", [12, 128, NT], BF16)
    vtok = dscr("vtok", [NT, 1536], BF16); kvnew = dscr("kvnew", [NS, 2, 1536])
    gT = dscr("gT", [32, 128, NT], BF16); ynT = dscr("ynT", [32, 128, NT], BF16); attT = dscr("attT", [4, 128, NT], BF16)
    oscr = dscr("oscr", [3, S, 516])

    used_names = {}

    def sb(name, shape, dt=F32, st=es):
        k = used_names.get(name, 0)
        used_names[name] = k + 1
        if k:
            name = "%s_r%d" % (name, k)
        return st.enter_context(nc.sbuf_tensor(name, list(shape), dt))

    def ps(name, shape, dt=F32, st=es):
        return st.enter_context(nc.psum_tensor(name, list(shape), dt))

    identF = sb("identF", [128, 128]); identB = sb("identB", [128, 128], BF16)
    onesB = sb("onesB", [128, 128], BF16); onesF = sb("onesF", [128, 128])
    mle = sb("mle", [128, 128]); mgt = sb("mgt", [128, 128])
    modT = sb("modT", [128, 144, 5])
    gcol = {n: sb("c_" + n, [128, 16]) for n in gvec}
    gssm_c = sb("gssm_c", [128, 32]); convw_c = sb("convw_c", [128, 4, 48]); convb_c = sb("convb_c", [128, 48])
    dtb_c = sb("dtb_c", [64, 1]); a_c = sb("a_c", [64, 1]); dsk_c = sb("dsk_c", [64, 1])
    dskP = sb("dskP", [128, 32])
    Aaf = [sb("Aaf%d" % i, [128, 16, 5]) for i in range(3)]
    Gaf = [sb("Gaf%d" % i, [128, 16, 5]) for i in range(3)]

    def cst(t, val):
        kb.G(lambda: nc.gpsimd.memset(t[:], val), w=[t.name])

    def asel(t, pattern, cmp, base=0, cm=1, fill=0.0):
        kb.G(lambda: nc.gpsimd.affine_select(out=t[:], in_=t[:], pattern=pattern, compare_op=cmp, fill=fill,
                                             base=base, channel_multiplier=cm), r=[t.name], w=[t.name])

    cst(identF, 1.0); asel(identF, [[-1, 128]], ALU.is_equal)
    cst(identB, 1.0); asel(identB, [[-1, 128]], ALU.is_equal)
    cst(onesB, 1.0); cst(onesF, 1.0)
    cst(mle, 1.0); asel(mle, [[1, 128]], ALU.is_ge, cm=-1)
    cst(mgt, 1.0); asel(mgt, [[-1, 128]], ALU.is_gt, cm=1)

    PS = [ps("psb%d" % i, [128, 512]) for i in range(8)]
    PSK = ["psb%d" % i for i in range(8)]

    def psbf(i):
        return PS[i][:].bitcast(BF16)

    with ExitStack() as p0:
        vtmp = sb("vtmp", [128, 128], st=p0)
        expd = sb("expd", [64, 32, 128], st=p0)
        kb.V(lambda: nc.vector.tensor_copy(out=expd[:].rearrange("h c (two p) -> h (c two) p", two=2),
                                           in_=identF[0:64, 0:64].unsqueeze(2).to_broadcast([64, 64, 64])), r=["identF"], w=["expd"])

        def vec_cols(src_ap, n, dst_ap, dst_key):
            kb.dma('sp', vtmp[0:n, :], src_ap, w=["vtmp"])
            kb.tr(PS[0][:, 0:n], vtmp[0:n, :], identF[0:n, 0:n], r=["vtmp", "identF"], w=[PSK[0]])
            kb.V(lambda: nc.vector.tensor_copy(out=dst_ap, in_=PS[0][:, 0:n]), r=[PSK[0]], w=[dst_key])

        for n_, t in gcol.items():
            vec_cols(gvec[n_], 16, t[:], t.name)
        vec_cols(g_ssm, 32, gssm_c[:], "gssm_c")
        for j in range(4):
            vec_cols(conv_w[j], 48, convw_c[:, j, :], "convw_c")
        vec_cols(conv_b, 48, convb_c[:], "convb_c")
        bcol = sb("bcol", [128, 144], st=p0)
        vec_cols(b_ada[0:72], 72, bcol[:, 0:72], "bcol")
        vec_cols(b_ada[72:144], 72, bcol[:, 72:144], "bcol")
        kb.dma('sp', dtb_c[:], dt_bias, w=["dtb_c"])
        kb.dma('sp', a_c[:], a_log, w=["a_c"])
        kb.dma('sp', dsk_c[:], d_skip, w=["dsk_c"])
        kb.A(lambda: nc.scalar.activation(out=a_c[:], in_=a_c[:], func=AF.Exp), r=["a_c"], w=["a_c"])
        kb.V(lambda: nc.vector.tensor_scalar(out=a_c[:], in0=a_c[:], scalar1=-1.0, scalar2=None, op0=ALU.mult), r=["a_c"], w=["a_c"])
        for hp in range(32):
            kb.mm(PS[0][:, hp:hp + 1], expd[:, hp, :], dsk_c[:, 0:1], True, True, r=["expd", "dsk_c"], w=[PSK[0]])
        kb.V(lambda: nc.vector.tensor_copy(out=dskP[:], in_=PS[0][:, 0:32]), r=[PSK[0]], w=["dskP"])

        c5t = sb("c5t", [5, D], st=p0); scT = sb("scT", [128, 16, 5], BF16, st=p0)
        kb.dma('sp', c5t[:], c5, w=["c5t"])
        for c in range(16):
            kb.tr(PS[0][:, c * 5:(c + 1) * 5], c5t[0:5, c * 128:(c + 1) * 128], identF[0:5, 0:5], r=["c5t", "identF"], w=[PSK[0]])
        kb.A(lambda: nc.scalar.activation(out=scT[:].rearrange("p c r -> p (c r)"), in_=PS[0][:, 0:80], func=AF.Silu), r=[PSK[0]], w=["scT"])
        wa = [sb("wa%d" % i, [128, 16, 512], BF16, st=p0) for i in range(3)]
        nblk = 36
        wa_src = w_ada.rearrange("(kc p) f -> p kc f", p=128)

        def wa_load(i):
            kb.dma('pool', wa[i % 3][:], wa_src[:, :, i * 512:(i + 1) * 512], w=["wa%d" % (i % 3)])

        wa_load(0); wa_load(1); wa_load(2)
        for i in range(nblk):
            pb = 1 + (i % 2)
            for j in range(4):
                for kc in range(16):
                    kb.mm(PS[pb][:, j * 5:(j + 1) * 5], wa[i % 3][:, kc, j * 128:(j + 1) * 128], scT[:, kc, :], kc == 0, kc == 15,
                          r=["wa%d" % (i % 3), "scT"], w=[PSK[pb]])
            f0 = i * 4
            kb.V(lambda: nc.vector.tensor_tensor(out=modT[:, f0:f0 + 4, :], in0=PS[pb][:, 0:20].rearrange("p (j r) -> p j r", r=5),
                                                 in1=bcol[:, f0:f0 + 4].unsqueeze(2).to_broadcast([128, 4, 5]), op=ALU.add),
                 r=[PSK[pb], "bcol"], w=["modT"])
            if i + 3 < nblk:
                wa_load(i + 3)
        pre_names = ("g_pre_ffn1", "g_pre_mix", "g_pre_ffn2"); post_names = ("g_post_ffn1", "g_post_mix", "g_post_ffn2")
        wts = (0.5, 1.0, 0.5)
        for i in range(3):
            sc = modT[:, (i * 3 + 1) * 16:(i * 3 + 2) * 16, :]
            gt = modT[:, (i * 3 + 2) * 16:(i * 3 + 3) * 16, :]
            gp = gcol[pre_names[i]]; gq = gcol[post_names[i]]
            kb.V(lambda: nc.vector.scalar_tensor_tensor(out=Aaf[i][:], in0=sc, scalar=1.0, in1=gp[:].unsqueeze(2).to_broadcast([128, 16, 5]),
                                                        op0=ALU.add, op1=ALU.mult), r=["modT", gp.name], w=[Aaf[i].name])
            kb.V(lambda: nc.vector.scalar_tensor_tensor(out=Gaf[i][:], in0=gt, scalar=wts[i], in1=gq[:].unsqueeze(2).to_broadcast([128, 16, 5]),
                                                        op0=ALU.mult, op1=ALU.mult), r=["modT", gq.name], w=[Gaf[i].name])

    kb.barrier()

    def Bsh(i, c, row):
        return modT[:, i * 48 + c, row:row + 1]

    TP = 512
    NCOL = TP + NS
    TILES = [(t * TP, TP, t == 0) for t in range(S // TP)]

    def token_phase(st, which):
        xT = sb("xT", [128, 16, NCOL], st=st)
        hT = sb("hT", [128, 16, NCOL], BF16, st=st)
        aT = sb("aT", [128, 44, NCOL], BF16, st=st)
        oT = sb("oT", [128, 16, NCOL], st=st)
        tmpA = [sb("tmpA%d" % i, [128, NCOL + 3], st=st) for i in range(2)]
        tmpB = [sb("tmpB%d" % i, [128, NCOL], st=st) for i in range(2)]
        stg = [sb("stg%d" % i, [128, NCOL], BF16, st=st) for i in range(2)]
        rstd = sb("rstd", [128, NCOL], st=st)
        wb = [sb("wb%d" % i, [128, WBUF_ELEMS], BF16, st=st) for i in range(NWBUF)]
        xin = [sb("xin%d" % i, [128, D], st=st) for i in range(1)]
        halo = sb("halo", [128, 48, 3], st=st)
        cstl = [sb("cstl%d" % i, [128, 128], st=st) for i in range(3)]
        xps = sb("xps", [128, 28], st=st); accs = sb("accs", [128, 16], st=st)
        epsc = sb("epsc", [128, 1], st=st)
        kb.G(lambda: nc.gpsimd.memset(epsc[:], EPS), w=["epsc"])
        kb.G(lambda: nc.gpsimd.memset(halo[:], 0.0), w=["halo"])
        cnt = {'w': 0, 'a': 0, 'b': 0, 's': 0, 'x': 0, 'r': 0, 'd': 0}

        def rr(lst, k):
            i = cnt[k] % len(lst); cnt[k] += 1
            return lst[i]

        def sreg():
            i = cnt['r'] % 32; cnt['r'] += 1
            return (lambda m=128: PS[6][0:m, i * 16:(i + 1) * 16]), PSK[6]

        def run_jobs(jobs):
            n = len(jobs)
            for i in range(min(NWBUF, n)):
                jobs[i][0](wb[i % NWBUF])
            for i in range(n):
                jobs[i][1](wb[i % NWBUF])
                if i + NWBUF < n:
                    jobs[i + NWBUF][0](wb[(i + NWBUF) % NWBUF])

        def wload(buf, view_shape, src_ap):
            nel = int(np.prod(view_shape[1:]))
            dst = buf[:, 0:nel]
            if len(view_shape) == 3:
                dst = dst.rearrange("p (a b) -> p a b", b=view_shape[2])
            kb.dma('pool', dst, src_ap, w=[buf.name])

        for (col0, np_, has_s) in TILES:
            n = np_ + (NS if has_s else 0)
            segs = [(0, np_, 0)] + ([(np_ + 4 * b, np_ + 4 * b + 4, 1 + b) for b in range(4)] if has_s else [])
            parts = [(0, np_, col0)] + ([(np_, n, S)] if has_s else [])
            groups = [(0, np_, False)] + ([(np_, n, True)] if has_s else [])

            def gout(gi, bank):
                c0, c1, iss = groups[gi]
                if iss:
                    f, key = sreg()
                    return f, key
                return (lambda m=128: PS[bank][0:m, 0:c1 - c0]), PSK[bank]

            def norm_stats(src, key_r):
                outs = [gout(gi, 7) for gi in range(len(groups))]
                for c in range(16):
                    sq = rr(stg, 's')
                    kb.A(lambda: nc.scalar.activation(out=sq[:, 0:n], in_=src[:, c, 0:n], func=AF.Square), r=[key_r], w=[sq.name])
                    for gi, (c0, c1, iss) in enumerate(groups):
                        kb.mm(outs[gi][0](), onesB[:], sq[:, c0:c1], c == 0, c == 15, r=[sq.name, "onesB"], w=[outs[gi][1]], inc=True)
                for gi, (c0, c1, iss) in enumerate(groups):
                    kb.A(lambda: nc.scalar.activation(out=rstd[:, c0:c1], in_=outs[gi][0](), func=AF.Sqrt, bias=epsc[:, 0:1], scale=1.0 / D),
                         r=[outs[gi][1], "epsc"], w=["rstd"])
                kb.V(lambda: nc.vector.reciprocal(out=rstd[:, 0:n], in_=rstd[:, 0:n]), r=["rstd"], w=["rstd"])

            def pre_norm(i):
                norm_stats(xT, "xT")
                for c in range(16):
                    t = rr(tmpB, 'b')
                    for (c0, c1, row) in segs:
                        kb.V(lambda: nc.vector.scalar_tensor_tensor(out=t[:, c0:c1], in0=xT[:, c, c0:c1], scalar=Aaf[i][:, c, row:row + 1],
                                                                    in1=rstd[:, c0:c1], op0=ALU.mult, op1=ALU.mult),
                             r=["xT", "rstd", Aaf[i].name], w=[t.name])
                        kb.A(lambda: nc.scalar.activation(out=hT[:, c, c0:c1], in_=t[:, c0:c1], func=AF.Identity, bias=Bsh(i, c, row), scale=1.0),
                             r=[t.name, "modT"], w=["hT"])

            def post_res(i):
                norm_stats(oT, "oT")
                for c in range(16):
                    t = rr(tmpB, 'b')
                    for (c0, c1, row) in segs:
                        kb.V(lambda: nc.vector.scalar_tensor_tensor(out=t[:, c0:c1], in0=oT[:, c, c0:c1], scalar=Gaf[i][:, c, row:row + 1],
                                                                    in1=rstd[:, c0:c1], op0=ALU.mult, op1=ALU.mult),
                             r=["oT", "rstd", Gaf[i].name], w=[t.name])
                    kb.V(lambda: nc.vector.tensor_tensor(out=xT[:, c, 0:n], in0=xT[:, c, 0:n], in1=t[:, 0:n], op=ALU.add), r=["xT", t.name], w=["xT"])

            def ffn(l):
                jobs = []
                gu = w_gu[l].rearrange("(kc p) f -> p kc f", p=128)
                dn = w_dn[l].rearrange("(fc p) d -> p fc d", p=128)
                for f in range(44):
                    def ld(buf, f=f):
                        v = buf[:, 0:4096].rearrange("p (k t m) -> p k t m", t=2, m=128)
                        kb.dma('pool', v[:, :, 0, :], gu[:, :, f * 128:(f + 1) * 128], w=[buf.name])
                        kb.dma('pool', v[:, :, 1, :], gu[:, :, DFF + f * 128:DFF + (f + 1) * 128], w=[buf.name])

                    def use(buf, f=f):
                        v = buf[:, 0:4096].rearrange("p (k t m) -> p k t m", t=2, m=128)
                        pg, pu = (1, 2) if f % 2 == 0 else (3, 4)
                        for gi, (c0, c1, iss) in enumerate(groups):
                            og, kg = gout(gi, pg); ou, ku = gout(gi, pu)
                            for kc in range(16):
                                kb.mm(og(), v[:, kc, 0, :], hT[:, kc, c0:c1], kc == 0, kc == 15, r=[buf.name, "hT"], w=[kg])
                            for kc in range(16):
                                kb.mm(ou(), v[:, kc, 1, :], hT[:, kc, c0:c1], kc == 0, kc == 15, r=[buf.name, "hT"], w=[ku])
                            t = rr(tmpB, 'b')
                            kb.A(lambda: nc.scalar.activation(out=t[:, c0:c1], in_=og(), func=AF.Silu), r=[kg], w=[t.name])
                            kb.V(lambda: nc.vector.tensor_tensor(out=aT[:, f, c0:c1], in0=t[:, c0:c1], in1=ou(), op=ALU.mult),
                                 r=[t.name, ku], w=["aT"])
                    jobs.append((ld, use))
                for d in range(16):
                    dst_ = {}
                    for half in range(2):
                        def ld(buf, d=d, half=half):
                            wload(buf, [128, 22, 128], dn[:, half * 22:(half + 1) * 22, d * 128:(d + 1) * 128])

                        def use(buf, d=d, half=half, dst_=dst_):
                            v = buf[:, 0:2816].rearrange("p (k m) -> p k m", m=128)
                            pb = (5, 7)[d % 2]
                            for gi, (c0, c1, iss) in enumerate(groups):
                                if half == 0:
                                    dst_[gi] = gout(gi, pb)
                                o_, k_ = dst_[gi]
                                for fc in range(22):
                                    last = (fc == 21)
                                    kb.mm(o_(), v[:, fc, :], aT[:, half * 22 + fc, c0:c1], half == 0 and fc == 0, half == 1 and last,
                                          r=[buf.name, "aT"], w=[k_], inc=last)
                                if half == 1:
                                    kb.A(lambda: nc.scalar.copy(out=oT[:, d, c0:c1], in_=o_()), r=[k_], w=["oT"])
                        jobs.append((ld, use))
                run_jobs(jobs)

            def load_x_tokmajor(src_rows_ap, nrows, c_at):
                xi = rr(xin, 'x')
                kb.dma('sp', xi[0:nrows, :], src_rows_ap, w=[xi.name])
                for c4 in range(4):
                    for j in range(4):
                        c = c4 * 4 + j
                        kb.tr(PS[0][:, j * 128:j * 128 + nrows], xi[0:nrows, c * 128:(c + 1) * 128], identF[0:nrows, 0:nrows],
                              r=[xi.name, "identF"], w=[PSK[0]], inc=(j == 3))
                    kb.V(lambda: nc.vector.tensor_copy(out=xT[:, c4 * 4:c4 * 4 + 4, c_at:c_at + nrows],
                                                       in_=PS[0][:].rearrange("p (j t) -> p j t", t=128)[:, :, 0:nrows]),
                         r=[PSK[0]], w=["xT"])

            def store_y_tokmajor(dst_rows_ap, nrows, c_at):
                xi = rr(xin, 'x')
                for c4 in range(4):
                    for j in range(4):
                        c = c4 * 4 + j
                        kb.tr(PS[0][0:nrows, j * 128:(j + 1) * 128], xT[:, c, c_at:c_at + nrows], identF[:, :],
                              r=["xT", "identF"], w=[PSK[0]], inc=(j == 3))
                    kb.V(lambda: nc.vector.tensor_copy(out=xi[0:nrows, c4 * 512:(c4 + 1) * 512], in_=PS[0][0:nrows, :]), r=[PSK[0]], w=[xi.name])
                kb.dma('sp', dst_rows_ap, xi[0:nrows, :], r=[xi.name])

            def out_parts(dst3, chunk, sg):
                for (c0, c1, sc0) in parts:
                    kb.dma('sp', dst3[chunk, :, sc0:sc0 + (c1 - c0)], sg[:, c0:c1], r=[sg.name])

            def in_proj():
                win = w_in.rearrange("(kc p) f -> p kc f", p=128)
                jobs = []

                def fm_job(colw, ncols2, evac, fin):
                    def ld(buf):
                        wload(buf, [128, 16, ncols2], win[:, :, colw:colw + ncols2])

                    def use(buf):
                        v = buf[:, 0:16 * ncols2].rearrange("p (k m) -> p k m", m=ncols2)
                        for j in range((ncols2 + 127) // 128):
                            m = min(128, ncols2 - j * 128)
                            pb = 1 + (cnt['a'] % 4); cnt['a'] += 1
                            for gi, (c0, c1, iss) in enumerate(groups):
                                o_, k_ = gout(gi, pb)
                                for kc in range(16):
                                    kb.mm(o_(m), v[:, kc, j * 128:j * 128 + m], hT[:, kc, c0:c1], kc == 0, kc == 15, r=[buf.name, "hT"], w=[k_])
                                evac(lambda: o_(m), k_, j, c0, c1, iss)
                            fin(j)
                    jobs.append((ld, use))

                def simple(dst3, chunk_of, actfn):
                    state = {}

                    def evac(o_, k_, j, c0, c1, iss):
                        if 'sg' not in state:
                            state['sg'] = rr(stg, 's')
                        sg = state['sg']
                        actfn(sg[:, c0:c1], o_(), k_, sg.name)

                    def fin(j):
                        out_parts(dst3, chunk_of(j), state.pop('sg'))
                    return evac, fin

                for cc in range(16):
                    ev, fin = simple(zsT, (lambda j, cc=cc: cc * 2 + j),
                                     lambda o, i, k, nm: kb.A(lambda: nc.scalar.activation(out=o, in_=i, func=AF.Silu), r=[k], w=[nm]))
                    fm_job(OFF_Z + cc * 256, 256, ev, fin)
                for cc in range(24):
                    state = {}

                    def evac(o_, k_, j, c0, c1, iss, cc=cc, state=state):
                        ch = cc * 2 + j
                        if 'sg' not in state:
                            state['sg'] = rr(stg, 's')
                        sg = state['sg']
                        if not iss:
                            nn = c1 - c0
                            xp = rr(tmpA, 'a'); acc = rr(tmpB, 'b')
                            kb.V(lambda: nc.vector.tensor_copy(out=xp[:, 0:3], in_=halo[:, ch, :]), r=["halo"], w=[xp.name])
                            kb.A(lambda: nc.scalar.copy(out=xp[:, 3:3 + nn], in_=o_()), r=[k_], w=[xp.name])
                            kb.V(lambda: nc.vector.tensor_copy(out=halo[:, ch, :], in_=xp[:, nn:nn + 3]), r=[xp.name], w=["halo"])
                            kb.V(lambda: nc.vector.tensor_scalar(out=acc[:, 0:nn], in0=xp[:, 0:nn], scalar1=convw_c[:, 0, ch:ch + 1], scalar2=None, op0=ALU.mult),
                                 r=[xp.name, "convw_c"], w=[acc.name])
                            for tap in range(1, 4):
                                kb.V(lambda: nc.vector.scalar_tensor_tensor(out=acc[:, 0:nn], in0=xp[:, tap:tap + nn], scalar=convw_c[:, tap, ch:ch + 1],
                                                                            in1=acc[:, 0:nn], op0=ALU.mult, op1=ALU.add), r=[xp.name, acc.name, "convw_c"], w=[acc.name])
                            if col0 + np_ == S:
                                kb.tr(PS[0][0:3, 0:128], xp[:, nn:nn + 3], identF[:, :], r=[xp.name, "identF"], w=[PSK[0]])
                                cs = cstl[0]
                                kb.V(lambda: nc.vector.tensor_copy(out=cs[0:3, 0:128], in_=PS[0][0:3, 0:128]), r=[PSK[0]], w=[cs.name])
                                kb.dma('sp', conv_p[:, ch * 128:(ch + 1) * 128], cs[0:3, 0:128], r=[cs.name])
                            kb.A(lambda: nc.scalar.activation(out=sg[:, c0:c1], in_=acc[:, 0:nn], func=AF.Silu, bias=convb_c[:, ch:ch + 1], scale=1.0),
                                 r=[acc.name, "convb_c"], w=[sg.name])
                        else:
                            xv = xps[:, 0:28].rearrange("p (b t) -> p b t", t=7)
                            av = accs[:, 0:16].rearrange("p (b t) -> p b t", t=4)
                            cs = cstl[0]
                            kb.dma('sp', cs[0:12, 0:128], st_conv.rearrange("b t c -> (b t) c")[:, ch * 128:(ch + 1) * 128], w=[cs.name])
                            kb.tr(PS[0][:, 0:12], cs[0:12, 0:128], identF[0:12, 0:12], r=[cs.name, "identF"], w=[PSK[0]])
                            kb.V(lambda: nc.vector.tensor_copy(out=xv[:, :, 0:3], in_=PS[0][:, 0:12].rearrange("p (b t) -> p b t", t=3)), r=[PSK[0]], w=["xps"])
                            kb.A(lambda: nc.scalar.copy(out=xv[:, :, 3:7], in_=o_().rearrange("p (b t) -> p b t", t=4)), r=[k_], w=["xps"])
                            kb.V(lambda: nc.vector.tensor_scalar(out=av, in0=xv[:, :, 0:4], scalar1=convw_c[:, 0, ch:ch + 1], scalar2=None, op0=ALU.mult),
                                 r=["xps", "convw_c"], w=["accs"])
                            for tap in range(1, 4):
                                kb.V(lambda: nc.vector.scalar_tensor_tensor(out=av, in0=xv[:, :, tap:tap + 4], scalar=convw_c[:, tap, ch:ch + 1],
                                                                            in1=av, op0=ALU.mult, op1=ALU.add), r=["xps", "accs", "convw_c"], w=["accs"])
                            cs2 = cstl[1]
                            kb.V(lambda: nc.vector.tensor_copy(out=cs2[:, 0:12].rearrange("p (b t) -> p b t", t=3), in_=xv[:, :, 4:7]), r=["xps"], w=[cs2.name])
                            kb.tr(PS[0][0:12, 0:128], cs2[:, 0:12], identF[:, :], r=[cs2.name, "identF"], w=[PSK[0]])
                            cs3 = cstl[2]
                            kb.V(lambda: nc.vector.tensor_copy(out=cs3[0:12, 0:128], in_=PS[0][0:12, 0:128]), r=[PSK[0]], w=[cs3.name])
                            kb.dma('sp', conv_s.rearrange("b t c -> (b t) c")[:, ch * 128:(ch + 1) * 128], cs3[0:12, 0:128], r=[cs3.name])
                            kb.A(lambda: nc.scalar.activation(out=sg[:, c0:c1], in_=accs[:, 0:16], func=AF.Silu, bias=convb_c[:, ch:ch + 1], scale=1.0),
                                 r=["accs", "convb_c"], w=[sg.name])

                    def fin(j, cc=cc, state=state):
                        out_parts(xbcT, cc * 2 + j, state.pop('sg'))
                    fm_job(OFF_XBC + cc * 256, 256, evac, fin)
                dstate = {}

                def evac_dt(o_, k_, j, c0, c1, iss):
                    if 't2' not in dstate:
                        dstate['t1'] = rr(tmpB, 'b'); dstate['t2'] = rr(tmpA, 'a')
                    t, t2 = dstate['t1'], dstate['t2']
                    kb.A(lambda: nc.scalar.activation(out=t[0:64, c0:c1], in_=o_(), func=AF.Exp, bias=dtb_c[:, 0:1], scale=1.0),
                         r=[k_, "dtb_c"], w=[t.name])
                    kb.A(lambda: nc.scalar.activation(out=t2[0:64, c0:c1], in_=t[0:64, c0:c1], func=AF.Ln, bias=onesF[0:64, 0:1], scale=1.0),
                         r=[t.name, "onesF"], w=[t2.name])

                def fin_dt(j):
                    t2 = dstate.pop('t2'); t = dstate.pop('t1')
                    for (c0, c1, sc0) in parts:
                        kb.dma('sp', dtT[:, sc0:sc0 + (c1 - c0)], t2[0:64, c0:c1], r=[t2.name])
                    kb.V(lambda: nc.vector.tensor_scalar(out=t[0:64, 0:n], in0=t2[0:64, 0:n], scalar1=a_c[:, 0:1], scalar2=None, op0=ALU.mult),
                         r=[t2.name, "a_c"], w=[t.name])
                    for (c0, c1, sc0) in parts:
                        kb.dma('sp', laT[:, sc0:sc0 + (c1 - c0)], t[0:64, c0:c1], r=[t.name])
                fm_job(OFF_DT, 64, evac_dt, fin_dt)
                for (off, dst3, scl) in ((OFF_Q, qT, 128.0 ** -0.5), (OFF_K, kT, 1.0)):
                    for cc in range(6):
                        ev, fin = simple(dst3, (lambda j, cc=cc: cc * 2 + j),
                                         lambda o, i, k, nm, scl=scl: kb.A(lambda: nc.scalar.mul(out=o, in_=i, mul=scl), r=[k], w=[nm]))
                        fm_job(off + cc * 256, 256, ev, fin)
                for cc in range(16):
                    ev, fin = simple(gT, (lambda j, cc=cc: cc * 2 + j),
                                     lambda o, i, k, nm: kb.A(lambda: nc.scalar.activation(out=o, in_=i, func=AF.Sigmoid), r=[k], w=[nm]))
                    fm_job(OFF_G + cc * 256, 256, ev, fin)
                blocks = [(b0, min(128, np_ - b0), False, col0 + b0) for b0 in range(0, np_, 128)] + ([(np_, NS, True, 0)] if has_s else [])
                for cg in range(12):
                    def ld(buf, cg=cg):
                        wload(buf, [128, 16, 256], win[:, :, OFF_K + cg * 256:OFF_K + (cg + 1) * 256])

                    def use(buf, cg=cg):
                        v = buf[:, 0:4096].rearrange("p (k m) -> p k m", m=256)
                        isv = cg >= 6
                        cgl = cg - 6 if isv else cg
                        g = cgl // 2
                        for (b0, m, iss, tok0) in blocks:
                            pb = 1 + (cnt['a'] % 4); cnt['a'] += 1
                            for kc in range(16):
                                kb.mm(PS[pb][0:m, 0:256], hT[:, kc, b0:b0 + m], v[:, kc, :], kc == 0, kc == 15,
                                      r=[buf.name, "hT"], w=[PSK[pb]])
                            t = rr(tmpB, 'b')
                            kb.A(lambda: nc.scalar.copy(out=t[0:m, 0:256], in_=PS[pb][0:m, 0:256]), r=[PSK[pb]], w=[t.name])
                            if iss:
                                kb.dma('sp', kvnew[:, 1 if isv else 0, cgl * 256:(cgl + 1) * 256], t[0:m, 0:256], r=[t.name])
                                vrow0 = S
                            else:
                                win_g = min(DILS[g][0], S)
                                if tok0 >= S - win_g:
                                    r0 = tok0 - (S - win_g)
                                    kb.dma('sp', kv_p[g][r0:r0 + m, 1 if isv else 0, (cgl % 2) * 256:(cgl % 2 + 1) * 256], t[0:m, 0:256], r=[t.name])
                                vrow0 = tok0
                            if isv:
                                sg = rr(stg, 's')
                                kb.V(lambda: nc.vector.tensor_copy(out=sg[0:m, 0:256], in_=t[0:m, 0:256]), r=[t.name], w=[sg.name])
                                kb.dma('sp', vtok[vrow0:vrow0 + m, cgl * 256:(cgl + 1) * 256], sg[0:m, 0:256], r=[sg.name])
                    jobs.append((ld, use))
                run_jobs(jobs)

            def mix_out():
                mrg = hT
                for (c0, c1, sc0) in parts:
                    for c in range(32):
                        kb.dma('sp', aT[:, c, c0:c1], ynT[c, :, sc0:sc0 + (c1 - c0)], w=[("aTl", c, c0)], after=["aT"])
                    for c in range(4):
                        kb.dma('sp', aT[:, 32 + c, c0:c1], attT[c, :, sc0:sc0 + (c1 - c0)], w=[("aTl", 32 + c, c0)], after=["aT"])
                aT_keys = ["aT"] + [("aTl", c, c0) for c in range(36) for (c0, c1, sc0) in parts]
                jobs = []
                ws = w_ssm.rearrange("(kc p) d -> p kc d", p=128)
                wat = w_att.rearrange("(kc p) d -> p kc d", p=128)
                wo = w_out.rearrange("(kc p) d -> p kc d", p=128)
                for d in range(16):
                    dst_ = {}

                    def ld1(buf, d=d):
                        wload(buf, [128, 32, 128], ws[:, :, d * 128:(d + 1) * 128])

                    def use1(buf, d=d, dst_=dst_):
                        v = buf[:, 0:4096].rearrange("p (k m) -> p k m", m=128)
                        for gi, (c0, c1, iss) in enumerate(groups):
                            dst_[gi] = gout(gi, 1 + 2 * (d % 2))
                            o1, k1 = dst_[gi]
                            for kc in range(32):
                                kb.mm(o1(), v[:, kc, :], aT[:, kc, c0:c1], kc == 0, kc == 31, r=[buf.name] + (aT_keys if d == 0 else ["aT"]), w=[k1])
                    jobs.append((ld1, use1))

                    def ld(buf, d=d):
                        wload(buf, [128, 4, 128], wat[:, :, d * 128:(d + 1) * 128])

                    def use(buf, d=d, dst_=dst_):
                        v = buf[:, 0:512].rearrange("p (k m) -> p k m", m=128)
                        g1 = rr(stg, 's'); g2 = rr(stg, 's')
                        for (c0, c1, sc0) in parts:
                            kb.dma('sp', g1[:, c0:c1], gT[d, :, sc0:sc0 + (c1 - c0)], w=[g1.name])
                            kb.dma('sp', g2[:, c0:c1], gT[16 + d, :, sc0:sc0 + (c1 - c0)], w=[g2.name])
                        for gi, (c0, c1, iss) in enumerate(groups):
                            o1, k1 = dst_[gi]; o2, k2 = gout(gi, 2 + 2 * (d % 2))
                            for kc in range(4):
                                kb.mm(o2(), v[:, kc, :], aT[:, 32 + kc, c0:c1], kc == 0, kc == 3, r=[buf.name, "aT"], w=[k2])
                            t1 = rr(tmpB, 'b'); t2 = rr(tmpA, 'a')
                            kb.V(lambda: nc.vector.tensor_tensor(out=t1[:, c0:c1], in0=o1(), in1=g1[:, c0:c1], op=ALU.mult), r=[k1, g1.name], w=[t1.name])
                            kb.V(lambda: nc.vector.tensor_tensor(out=t2[:, c0:c1], in0=o2(), in1=g2[:, c0:c1], op=ALU.mult), r=[k2, g2.name], w=[t2.name])
                            kb.V(lambda: nc.vector.tensor_tensor(out=mrg[:, d, c0:c1], in0=t1[:, c0:c1], in1=t2[:, c0:c1], op=ALU.add), r=[t1.name, t2.name], w=["hT"])
                    jobs.append((ld, use))
                for d2 in range(8):
                    def ld(buf, d2=d2):
                        wload(buf, [128, 16, 256], wo[:, :, d2 * 256:(d2 + 1) * 256])

                    def use(buf, d2=d2):
                        v = buf[:, 0:4096].rearrange("p (k m) -> p k m", m=256)
                        for j in range(2):
                            for gi, (c0, c1, iss) in enumerate(groups):
                                o_, k_ = gout(gi, 3 + j)
                                for kc in range(16):
                                    kb.mm(o_(), v[:, kc, j * 128:(j + 1) * 128], mrg[:, kc, c0:c1], kc == 0, kc == 15, r=[buf.name, "hT"], w=[k_])
                                kb.A(lambda: nc.scalar.copy(out=oT[:, d2 * 2 + j, c0:c1], in_=o_()), r=[k_], w=["oT"])
                    jobs.append((ld, use))
                run_jobs(jobs)

            if which == 1:
                for blk in range(np_ // 128):
                    load_x_tokmajor(x_p[col0 + blk * 128:col0 + (blk + 1) * 128, :], 128, blk * 128)
                if has_s:
                    load_x_tokmajor(x_s, NS, np_)
                pre_norm(0)
                ffn(0)
                post_res(0)
                for (c0, c1, sc0) in parts:
                    for c in range(16):
                        kb.dma('sp', x1T[c, :, sc0:sc0 + (c1 - c0)], xT[:, c, c0:c1], r=["xT"])
                pre_norm(1)
                in_proj()
            else:
                for (c0, c1, sc0) in parts:
                    for c in range(16):
                        kb.dma('sp', xT[:, c, c0:c1], x1T[c, :, sc0:sc0 + (c1 - c0)], w=[("xTl", c, c0)], after=["xT"])
                fold_keys(kb, "xT", [("xTl", c, c0) for c in range(16) for (c0, c1, sc0) in parts])
                mix_out()
                post_res(1)
                pre_norm(2)
                ffn(1)
                post_res(2)
                for blk in range(np_ // 128):
                    store_y_tokmajor(y_p[col0 + blk * 128:col0 + (blk + 1) * 128, :], 128, blk * 128)
                if has_s:
                    store_y_tokmajor(y_s, NS, np_)

    def phase2():
        EXP, LN = AF.Exp, AF.Ln

        def gate_norm_store(st_, yg, zs, n, col0, tag):
            sqall = sb("gsq" + tag, [128, 32, n], BF16, st=st_)
            rs = sb("grs" + tag, [128, n], st=st_)
            ynb = sb("gyn" + tag, [128, 32, n], BF16, st=st_)
            epsc2 = sb("geps" + tag, [128, 1], st=st_)
            kb.G(lambda: nc.gpsimd.memset(epsc2[:], EPS), w=[epsc2.name])

            def run(col0=col0):
                kb.V(lambda: nc.vector.tensor_tensor(out=yg[:], in0=yg[:], in1=zs[:], op=ALU.mult), r=[yg.name, zs.name], w=[yg.name])
                kb.A(lambda: nc.scalar.activation(out=sqall[:], in_=yg[:], func=AF.Square), r=[yg.name], w=[sqall.name])
                for c in range(32):
                    kb.mm(PS[7][:, 0:n], onesB[:], sqall[:, c, :], c == 0, c == 31, r=[sqall.name, "onesB"], w=[PSK[7]])
                kb.A(lambda: nc.scalar.activation(out=rs[:], in_=PS[7][:, 0:n], func=AF.Sqrt, bias=epsc2[:, 0:1], scale=1.0 / DIN),
                     r=[PSK[7], epsc2.name], w=[rs.name])
                kb.V(lambda: nc.vector.reciprocal(out=rs[:], in_=rs[:]), r=[rs.name], w=[rs.name])
                kb.V(lambda: nc.vector.tensor_tensor(out=yg[:], in0=yg[:], in1=gssm_c[:].unsqueeze(2).to_broadcast([128, 32, n]), op=ALU.mult),
                     r=[yg.name, "gssm_c"], w=[yg.name])
                kb.V(lambda: nc.vector.tensor_tensor(out=ynb[:], in0=yg[:], in1=rs[:].unsqueeze(1).to_broadcast([128, 32, n]), op=ALU.mult),
                     r=[yg.name, rs.name], w=[ynb.name])
                kb.dma('sp', ynT[:, :, col0:col0 + n].rearrange("c p t -> p c t"), ynb[:], r=[ynb.name])
            return run

        with ExitStack() as s2:
            xsT = sb("xsT", [128, 32, 128], BF16, st=s2); bcT = sb("bcT", [128, 16, 128], BF16, st=s2)
            zsc = sb("zsc", [128, 32, 128], BF16, st=s2)
            dtl = sb("dtl", [64, 2, 128], st=s2); dtlk = sb("dtlk", [128, 128], st=s2); decd = sb("decd", [128, 128], st=s2)
            dtde = sb("dtde", [128, 64], st=s2)
            xdtp = sb("xdtp", [128, 64, 128], BF16, st=s2); xdd = sb("xdd", [128, 4096], BF16, st=s2)
            btok = sb("btok", [128, 1024], BF16, st=s2)
            rhsL = sb("rhsL", [128, 64, 128], st=s2)
            MT = sb("MT", [128, 64, 128], BF16, st=s2); CsT = sb("CsT", [128, 64, 128], BF16, st=s2)
            CBm = sb("CBm", [128, 8, 128], st=s2)
            h32 = sb("h32", [128, 4096], st=s2); hpad = sb("hpad", [128, 64, 128], BF16, st=s2)
            yg = sb("yg", [128, 32, 128], st=s2)
            et = [sb("et%d" % i, [128, 512], st=s2) for i in range(4)]
            gns = gate_norm_store(s2, yg, zsc, 128, 0, "p")
            kb.G(lambda: nc.gpsimd.memset(xdtp[:], 0.0), w=["xdtp"])
            kb.G(lambda: nc.gpsimd.memset(hpad[:], 0.0), w=["hpad"])
            kb.G(lambda: nc.gpsimd.memset(h32[:], 0.0), w=["h32"])
            lak = dtlk[:, 64:128]; dtk = dtlk[:, 0:64]
            for c in range(S // 128):
                c0 = c * 128
                kb.dma('sp', xsT[:], xbcT[0:32, :, c0:c0 + 128].rearrange("c p t -> p c t"), w=["xsT"])
                kb.dma('sp', bcT[:], xbcT[32:48, :, c0:c0 + 128].rearrange("c p t -> p c t"), w=["bcT"])
                kb.dma('sp', zsc[:], zsT[:, :, c0:c0 + 128].rearrange("c p t -> p c t"), w=["zsc"])
                kb.dma('sp', dtl[:, 0, :], dtT[:, c0:c0 + 128], w=["dtl"])
                kb.dma('sp', dtl[:, 1, :], laT[:, c0:c0 + 128], w=["dtl"])
                kb.tr(PS[0][:, 0:64], dtl[:, 0, :], identF[0:64, 0:64], r=["dtl", "identF"], w=[PSK[0]], inc=False)
                kb.tr(PS[0][:, 64:128], dtl[:, 1, :], identF[0:64, 0:64], r=["dtl", "identF"], w=[PSK[0]])
                kb.V(lambda: nc.vector.tensor_copy(out=dtlk[:], in_=PS[0][:, 0:128]), r=[PSK[0]], w=["dtlk"])
                kb.G(lambda: nc.gpsimd.tensor_tensor(out=rhsL[:], in0=lak.unsqueeze(2).to_broadcast([128, 64, 128]),
                                                     in1=mle[:].unsqueeze(1).to_broadcast([128, 64, 128]), op=ALU.mult), r=["dtlk", "mle"], w=["rhsL"])
                kb.mm(PS[0][:, 0:64], mgt[:], lak, True, True, r=["mgt", "dtlk"], w=[PSK[0]], inc=False)
                kb.mm(PS[0][:, 64:128], onesF[:], lak, True, True, r=["onesF", "dtlk"], w=[PSK[0]])
                kb.A(lambda: nc.scalar.activation(out=decd[:], in_=PS[0][:, 0:128], func=EXP), r=[PSK[0]], w=["decd"])
                kb.V(lambda: nc.vector.tensor_tensor(out=dtde[:], in0=dtk, in1=decd[:, 0:64], op=ALU.mult), r=["dtlk", "decd"], w=["dtde"])
                for q4 in range(4):
                    pb = 5 + q4 % 2
                    for j in range(8):
                        kb.tr(psbf(pb)[:, j * 128:(j + 1) * 128], xsT[:, q4 * 8 + j, :], identB[:], r=["xsT", "identB"], w=[PSK[pb]], inc=(j == 7))
                    pv = psbf(pb).rearrange("p (hp two m) -> p hp two m", two=2, m=64)
                    for two in range(2):
                        dsel = dtk[:, q4 * 16:(q4 + 1) * 16].rearrange("p (hp two) -> p hp two", two=2)[:, :, two].unsqueeze(2).to_broadcast([128, 8, 64])
                        esel = dtde[:, q4 * 16:(q4 + 1) * 16].rearrange("p (hp two) -> p hp two", two=2)[:, :, two].unsqueeze(2).to_broadcast([128, 8, 64])
                        o1 = xdtp[:, q4 * 16:(q4 + 1) * 16, :].rearrange("p (hp two) m -> p hp two m", two=2)[:, :, two, two * 64:(two + 1) * 64]
                        o2 = xdd[:, q4 * 1024:(q4 + 1) * 1024].rearrange("p (hp two m) -> p hp two m", two=2, m=64)[:, :, two, :]
                        kb.V(lambda: nc.vector.tensor_tensor(out=o1, in0=pv[:, :, two, :], in1=dsel, op=ALU.mult), r=[PSK[pb], "dtlk"], w=["xdtp"])
                        kb.V(lambda: nc.vector.tensor_tensor(out=o2, in0=pv[:, :, two, :], in1=esel, op=ALU.mult), r=[PSK[pb], "dtde"], w=["xdd"])
                for g in range(8):
                    kb.tr(psbf(7)[:, g * 128:(g + 1) * 128], bcT[:, g, :], identB[:], r=["bcT", "identB"], w=[PSK[7]], inc=(g == 7))
                kb.V(lambda: nc.vector.tensor_copy(out=btok[:], in_=psbf(7)), r=[PSK[7]], w=["btok"])
                for g in range(8):
                    pb = 5 + g // 4
                    kb.mm(PS[pb][:, (g % 4) * 128:(g % 4 + 1) * 128], bcT[:, g, :], bcT[:, 8 + g, :], True, True, r=["bcT"], w=[PSK[pb]], inc=(g % 4 == 3))
                for gg in range(2):
                    kb.V(lambda: nc.vector.tensor_tensor(out=CBm[:, gg * 4:(gg + 1) * 4, :], in0=PS[5 + gg][:].rearrange("p (g i) -> p g i", i=128),
                                                         in1=mle[:].unsqueeze(1).to_broadcast([128, 4, 128]), op=ALU.mult), r=[PSK[5 + gg], "mle"], w=["CBm"])
                for b16 in range(16):
                    h0 = b16 * 4; g = b16 // 2
                    pa = 1 + b16 % 2; pc = 3 + b16 % 2
                    rv = rhsL[:, h0:h0 + 4, :].rearrange("p h i -> p (h i)")
                    kb.mm(PS[pa][:], mgt[:], rv, True, True, r=["mgt", "rhsL"], w=[PSK[pa]])
                    kb.mm(PS[pc][:], onesF[:], rv, True, True, r=["onesF", "rhsL"], w=[PSK[pc]])
                    ea = et[2 * (b16 % 2)]; eb = et[1 + 2 * (b16 % 2)]
                    kb.A(lambda: nc.scalar.activation(out=ea[:], in_=PS[pa][:], func=EXP), r=[PSK[pa]], w=[ea.name])
                    kb.V(lambda: nc.vector.tensor_tensor(out=MT[:, h0:h0 + 4, :], in0=ea[:].rearrange("p (h i) -> p h i", i=128),
                                                         in1=CBm[:, g:g + 1, :].to_broadcast([128, 4, 128]), op=ALU.mult), r=[ea.name, "CBm"], w=["MT"])
                    kb.A(lambda: nc.scalar.activation(out=eb[:], in_=PS[pc][:], func=EXP), r=[PSK[pc]], w=[eb.name])
                    if True:
                        kb.G(lambda: nc.gpsimd.tensor_tensor(out=CsT[:, h0:h0 + 4, :], in0=eb[:].rearrange("p (h i) -> p h i", i=128),
                                                             in1=bcT[:, 8 + g:9 + g, :].to_broadcast([128, 4, 128]), op=ALU.mult), r=[eb.name, "bcT"], w=["CsT"])
                    else:
                        kb.V(lambda: nc.vector.tensor_tensor(out=CsT[:, h0:h0 + 4, :], in0=eb[:].rearrange("p (h i) -> p h i", i=128),
                                                             in1=bcT[:, 8 + g:9 + g, :].to_broadcast([128, 4, 128]), op=ALU.mult), r=[eb.name, "bcT"], w=["CsT"])
                kb.V(lambda: nc.vector.tensor_tensor(out=yg[:], in0=xsT[:], in1=dskP[:].unsqueeze(2).to_broadcast([128, 32, 128]), op=ALU.mult),
                     r=["xsT", "dskP"], w=["yg"])
                for q8 in range(8):
                    pb = (7, 0)[q8 % 2]
                    for j in range(4):
                        hp = q8 * 4 + j
                        o = PS[pb][:, j * 128:(j + 1) * 128]
                        kb.mm(o, xdtp[:, 2 * hp, :], MT[:, 2 * hp, :], True, False, r=["xdtp", "MT"], w=[PSK[pb]])
                        kb.mm(o, xdtp[:, 2 * hp + 1, :], MT[:, 2 * hp + 1, :], False, False, r=["xdtp", "MT"], w=[PSK[pb]])
                        kb.mm(o, hpad[:, 2 * hp, :], CsT[:, 2 * hp, :], False, False, r=["hpad", "CsT"], w=[PSK[pb]])
                        kb.mm(o, hpad[:, 2 * hp + 1, :], CsT[:, 2 * hp + 1, :], False, True, r=["hpad", "CsT"], w=[PSK[pb]])
                    kb.V(lambda: nc.vector.tensor_tensor(out=yg[:, q8 * 4:(q8 + 1) * 4, :], in0=yg[:, q8 * 4:(q8 + 1) * 4, :],
                                                         in1=PS[pb][:].rearrange("p (j t) -> p j t", t=128), op=ALU.add), r=["yg", PSK[pb]], w=["yg"])
                if c == 1:
                    dbg("MT", MT[:], ["MT"]); dbg("CsT", CsT[:], ["CsT"]); dbg("CBm", CBm[:], ["CBm"])
                    dbg("yg", yg[:], ["yg"]); dbg("decd", decd[:], ["decd"]); dbg("dtlk", dtlk[:], ["dtlk"]);
                gns(c0)
                for g in range(8):
                    pb = 5 + g % 2
                    kb.mm(PS[pb][:], btok[:, g * 128:(g + 1) * 128], xdd[:, g * 512:(g + 1) * 512], True, True, r=["btok", "xdd"], w=[PSK[pb]])
                    hv = h32[:, g * 512:(g + 1) * 512].rearrange("p (h m) -> p h m", m=64)
                    kb.V(lambda: nc.vector.tensor_tensor(out=hv, in0=hv, in1=decd[:, 64 + 8 * g:72 + 8 * g].unsqueeze(2).to_broadcast([128, 8, 64]), op=ALU.mult),
                         r=["h32", "decd"], w=["h32"])
                    kb.V(lambda: nc.vector.tensor_tensor(out=h32[:, g * 512:(g + 1) * 512], in0=h32[:, g * 512:(g + 1) * 512], in1=PS[pb][:], op=ALU.add),
                         r=["h32", PSK[pb]], w=["h32"])
                    for two in range(2):
                        o1 = hpad[:, g * 8:(g + 1) * 8, :].rearrange("p (hp two) m -> p hp two m", two=2)[:, :, two, two * 64:(two + 1) * 64]
                        i1 = h32[:, g * 512:(g + 1) * 512].rearrange("p (hp two m) -> p hp two m", two=2, m=64)[:, :, two, :]
                        kb.V(lambda: nc.vector.tensor_copy(out=o1, in_=i1), r=["h32"], w=["hpad"])
            for q8 in range(8):
                for j in range(4):
                    hp = q8 * 4 + j
                    kb.tr(PS[1][:, j * 128:(j + 1) * 128], h32[:, hp * 128:(hp + 1) * 128], identF[:], r=["h32", "identF"], w=[PSK[1]], inc=(j == 3))
                kb.V(lambda: nc.vector.tensor_copy(out=et[0][:], in_=PS[1][:]), r=[PSK[1]], w=["et0"])
                kb.dma('sp', ssm_p[q8 * 4:(q8 + 1) * 4].rearrange("c p n -> p c n"), et[0][:].rearrange("p (c n) -> p c n", n=128), r=["et0"])

        kb.barrier()
        with ExitStack() as s3:
            hS = sb("hS", [128, 32, 128], st=s3); tmpS = sb("tmpS", [128, 32, 128], st=s3)
            xsS = sb("xsS", [128, 32, 16], BF16, st=s3); bcS = sb("bcS", [128, 16, 16], BF16, st=s3); zsS = sb("zsS", [128, 32, 16], BF16, st=s3)
            dtlS = sb("dtlS", [64, 2, 16], st=s3); expd2 = sb("expd2", [64, 32, 128], st=s3)
            dtP = sb("dtP", [128, 32, 16], st=s3); dAP = sb("dAP", [128, 32, 16], st=s3); dtx = sb("dtx", [128, 32, 16], st=s3)
            bctok = sb("bctok", [16, 2048], BF16, st=s3); SEL = sb("SEL", [16, 16, 128], BF16, st=s3)
            ygS = sb("ygS", [128, 32, 16], st=s3)
            gnsS = gate_norm_store(s3, ygS, zsS, 16, S, "s")
            kb.V(lambda: nc.vector.tensor_copy(out=expd2[:].rearrange("h c (two p) -> h (c two) p", two=2),
                                               in_=identF[0:64, 0:64].unsqueeze(2).to_broadcast([64, 64, 64])), r=["identF"], w=["expd2"])
            kb.V(lambda: nc.vector.tensor_copy(out=SEL[:], in_=identB[0:16, 0:16].unsqueeze(2).to_broadcast([16, 16, 128])), r=["identB"], w=["SEL"])
            kb.dma('sp', xsS[:], xbcT[0:32, :, S:S + 16].rearrange("c p t -> p c t"), w=["xsS"])
            kb.dma('sp', bcS[:], xbcT[32:48, :, S:S + 16].rearrange("c p t -> p c t"), w=["bcS"])
            kb.dma('sp', zsS[:], zsT[:, :, S:S + 16].rearrange("c p t -> p c t"), w=["zsS"])
            kb.dma('sp', dtlS[:, 0, :], dtT[:, S:S + 16], w=["dtlS"])
            kb.dma('sp', dtlS[:, 1, :], laT[:, S:S + 16], w=["dtlS"])
            for hp in range(32):
                kb.mm(PS[0][:, hp * 16:(hp + 1) * 16], expd2[:, hp, :], dtlS[:, 0, :], True, True, r=["expd2", "dtlS"], w=[PSK[0]], inc=(hp == 31))
            for hp in range(32):
                kb.mm(PS[1][:, hp * 16:(hp + 1) * 16], expd2[:, hp, :], dtlS[:, 1, :], True, True, r=["expd2", "dtlS"], w=[PSK[1]], inc=(hp == 31))
            kb.V(lambda: nc.vector.tensor_copy(out=dtP[:].rearrange("p c t -> p (c t)"), in_=PS[0][:]), r=[PSK[0]], w=["dtP"])
            kb.A(lambda: nc.scalar.activation(out=dAP[:].rearrange("p c t -> p (c t)"), in_=PS[1][:], func=EXP), r=[PSK[1]], w=["dAP"])
            kb.V(lambda: nc.vector.tensor_tensor(out=dtx[:], in0=dtP[:], in1=xsS[:], op=ALU.mult), r=["dtP", "xsS"], w=["dtx"])
            for g in range(16):
                kb.tr(psbf(2 + g // 8)[0:16, (g % 8) * 128:(g % 8 + 1) * 128], bcS[:, g, :], identB[:], r=["bcS", "identB"], w=[PSK[2 + g // 8]], inc=(g % 8 == 7))
            kb.V(lambda: nc.vector.tensor_copy(out=bctok[:, 0:1024], in_=psbf(2)[0:16, :]), r=[PSK[2]], w=["bctok"])
            kb.V(lambda: nc.vector.tensor_copy(out=bctok[:, 1024:2048], in_=psbf(3)[0:16, :]), r=[PSK[3]], w=["bctok"])
            dbg("dtP", dtP[:], ["dtP"]); dbg("dAP", dAP[:], ["dAP"]); dbg("dtx", dtx[:], ["dtx"]); dbg("bctok", bctok[:], ["bctok"])
            for b in range(4):
                kb.dma('sp', hS[:], st_ssm[b].rearrange("c p n -> p c n"), w=["hS"])
                for t in range(4):
                    col = 4 * b + t
                    for q in range(4):
                        kb.mm(PS[2 + q][:], SEL[:, col, :], bctok[:, q * 512:(q + 1) * 512], True, True, r=["SEL", "bctok"], w=[PSK[2 + q]])
                    kb.V(lambda: nc.vector.tensor_tensor(out=hS[:], in0=hS[:], in1=dAP[:, :, col:col + 1].to_broadcast([128, 32, 128]), op=ALU.mult),
                         r=["hS", "dAP"], w=["hS"])
                    for q in range(2):
                        kb.V(lambda: nc.vector.tensor_tensor(out=tmpS[:, q * 16:(q + 1) * 16, :].rearrange("p (g j) n -> p g j n", j=4),
                                                             in0=PS[2 + q][:].rearrange("p (g n) -> p g n", n=128).unsqueeze(2).to_broadcast([128, 4, 4, 128]),
                                                             in1=dtx[:, q * 16:(q + 1) * 16, col].rearrange("p (g j) -> p g j", j=4).unsqueeze(3).to_broadcast([128, 4, 4, 128]),
                                                             op=ALU.mult), r=[PSK[2 + q], "dtx"], w=["tmpS"])
                    kb.V(lambda: nc.vector.tensor_tensor(out=hS[:], in0=hS[:], in1=tmpS[:], op=ALU.add), r=["hS", "tmpS"], w=["hS"])
                    for q in range(2):
                        kb.V(lambda: nc.vector.tensor_tensor(out=tmpS[:, q * 16:(q + 1) * 16, :].rearrange("p (g j) n -> p g j n", j=4),
                                                             in0=hS[:, q * 16:(q + 1) * 16, :].rearrange("p (g j) n -> p g j n", j=4),
                                                             in1=PS[4 + q][:].rearrange("p (g n) -> p g n", n=128).unsqueeze(2).to_broadcast([128, 4, 4, 128]),
                                                             op=ALU.mult), r=[PSK[4 + q], "hS"], w=["tmpS"])
                    kb.V(lambda: nc.vector.tensor_reduce(out=ygS[:, :, col], in_=tmpS[:], axis=AX.X, op=ALU.add), r=["tmpS"], w=["ygS"])
                kb.dma('sp', ssm_s[b].rearrange("c p n -> p c n"), hS[:], r=["hS"])
            kb.V(lambda: nc.vector.tensor_tensor(out=tmpS[:, :, 0:16], in0=xsS[:], in1=dskP[:].unsqueeze(2).to_broadcast([128, 32, 16]), op=ALU.mult),
                 r=["xsS", "dskP"], w=["tmpS"])
            kb.V(lambda: nc.vector.tensor_tensor(out=ygS[:], in0=ygS[:], in1=tmpS[:, :, 0:16], op=ALU.add), r=["ygS", "tmpS"], w=["ygS"])
            dbg("ygS", ygS[:], ["ygS"])
            gnsS()

        kb.barrier()
        with ExitStack() as s4:
            QT = [sb("QT%d" % h, [128, S], BF16, st=s4) for h in range(4)]
            KT = [sb("KT%d" % h, [128, S], BF16, st=s4) for h in range(4)]
            Vb = [sb("Vb%d" % i, [128, 512], BF16, st=s4) for i in range(4)]
            dist = sb("dist", [128, 256], st=s4)
            bias4 = sb("bias4", [128, 4, 256], st=s4)
            Sb4 = [sb("Sb4_%d" % i, [128, 4, 256], st=s4) for i in range(2)]
            Pb4 = [sb("Pb4_%d" % i, [128, 4, 256], BF16, st=s4) for i in range(2)]
            PT4 = [sb("PT4_%d" % i, [128, 8, 128], BF16, st=s4) for i in range(2)]
            st4 = [sb("st4_%d" % i, [128, 16], st=s4) for i in range(2)]
            ostg = [sb("ostg%d" % i, [128, 516], st=s4) for i in range(2)]
            kb.dma('sp', dist[:], cdist, w=["dist"])
            blk_i = 0
            for g, (win_, dil) in enumerate(DILS):
                for h in range(4):
                    kb.dma('sp', QT[h][:], qT[g * 4 + h, :, 0:S], w=[QT[h].name])
                    kb.dma('sp', KT[h][:], kT[g * 4 + h, :, 0:S], w=[KT[h].name])
                    kb.V(lambda: nc.vector.tensor_scalar(out=bias4[:, h, :], in0=dist[:], scalar1=float(-SLOPES[g, h] * dil), scalar2=None, op0=ALU.mult),
                         r=["dist"], w=["bias4"])
                for h in range(4):
                    kb.G(lambda: nc.gpsimd.affine_select(out=bias4[:, h, :], in_=bias4[:, h, :], pattern=[[-1, 256]], compare_op=ALU.is_ge, fill=-1e30,
                                                         base=128, channel_multiplier=1), r=["bias4"], w=["bias4"])
                    kb.G(lambda: nc.gpsimd.affine_select(out=bias4[:, h, :], in_=bias4[:, h, :], pattern=[[1, 256]], compare_op=ALU.is_ge, fill=-1e30,
                                                         base=0, channel_multiplier=-1), r=["bias4"], w=["bias4"])
                M = S // dil
                blks = []
                for r_ in range(dil):
                    vprev = None
                    for nb in range(M // 128):
                        t0 = r_ + dil * 128 * nb
                        vcur = Vb[blk_i % 4]; par = blk_i % 2; blk_i += 1
                        blks.append(dict(sl=slice(t0, t0 + dil * 127 + 1, dil), slp=slice(t0 - dil * 128, t0 - dil * 128 + dil * 127 + 1, dil),
                                         nb=nb, vcur=vcur, vprev=vprev, par=par))
                        vprev = vcur

                def front(B):
                    sl, slp, nb, vcur, par = B['sl'], B['slp'], B['nb'], B['vcur'], B['par']
                    kb.dma('sp', vcur[:], vtok[sl, g * 512:(g + 1) * 512], w=[vcur.name])
                    Sb = Sb4[par]; Pb = Pb4[par]; st_ = st4[par]
                    kw = 0 if nb > 0 else 128
                    for h in range(4):
                        pa = 1 + h // 2; c_ = (h % 2) * 256
                        kb.mm(PS[pa][:, c_ + 128:c_ + 256], QT[h][:, sl], KT[h][:, sl], True, True, r=[QT[h].name, KT[h].name], w=[PSK[pa]],
                              inc=(nb == 0 and h % 2 == 1))
                        if nb > 0:
                            kb.mm(PS[pa][:, c_:c_ + 128], QT[h][:, sl], KT[h][:, slp], True, True, r=[QT[h].name, KT[h].name], w=[PSK[pa]], inc=(h % 2 == 1))
                    for hh in range(2):
                        kb.V(lambda: nc.vector.tensor_tensor(out=Sb[:, hh * 2:hh * 2 + 2, kw:256], in0=PS[1 + hh][:].rearrange("p (h k) -> p h k", k=256)[:, :, kw:256],
                                                             in1=bias4[:, hh * 2:hh * 2 + 2, kw:256], op=ALU.add), r=[PSK[1 + hh], "bias4"], w=[Sb.name])
                    mx = st_[:, 0:4]; ls = st_[:, 4:8]
                    kb.V(lambda: nc.vector.tensor_reduce(out=mx, in_=Sb[:, :, kw:256], axis=AX.X, op=ALU.max), r=[Sb.name], w=[st_.name])
                    kb.V(lambda: nc.vector.tensor_tensor(out=Sb[:, :, kw:256], in0=Sb[:, :, kw:256], in1=mx.unsqueeze(2).to_broadcast([128, 4, 256 - kw]), op=ALU.subtract),
                         r=[Sb.name, st_.name], w=[Sb.name])
                    kb.A(lambda: nc.scalar.activation(out=Pb[:, :, kw:256], in_=Sb[:, :, kw:256], func=EXP), r=[Sb.name], w=[Pb.name])
                    kb.V(lambda: nc.vector.tensor_reduce(out=ls, in_=Pb[:, :, kw:256], axis=AX.X, op=ALU.add), r=[Pb.name], w=[st_.name])

                def back(B):
                    sl, nb, vcur, vprev, par = B['sl'], B['nb'], B['vcur'], B['vprev'], B['par']
                    og = ostg[par]; Pb = Pb4[par]; PT = PT4[par]; st_ = st4[par]
                    mx = st_[:, 0:4]; ls = st_[:, 4:8]; rl = st_[:, 8:12]; lnl = st_[:, 12:16]
                    nblk = 2 if nb > 0 else 1
                    for h in range(4):
                        for j in range(nblk):
                            cb = (j if nb > 0 else 1) * 128
                            kb.tr(psbf(3)[:, (h * 2 + j) * 128:(h * 2 + j + 1) * 128], Pb[:, h, cb:cb + 128], identB[:], r=[Pb.name, "identB"], w=[PSK[3]],
                                  inc=(h == 3 and j == nblk - 1))
                    kb.V(lambda: nc.vector.tensor_copy(out=PT[:].rearrange("p (h j) q -> p h j q", j=2)[:, :, 0:nblk, :],
                                                       in_=psbf(3).rearrange("p (h j q) -> p h j q", j=2, q=128)[:, :, 0:nblk, :]), r=[PSK[3]], w=[PT.name])
                    for h in range(4):
                        o = PS[4][:, h * 128:(h + 1) * 128]
                        if nb > 0:
                            kb.mm(o, PT[:, h * 2, :], vprev[:, h * 128:(h + 1) * 128], True, False, r=[PT.name, vprev.name], w=[PSK[4]])
                            kb.mm(o, PT[:, h * 2 + 1, :], vcur[:, h * 128:(h + 1) * 128], False, True, r=[PT.name, vcur.name], w=[PSK[4]], inc=(h == 3))
                        else:
                            kb.mm(o, PT[:, h * 2, :], vcur[:, h * 128:(h + 1) * 128], True, True, r=[PT.name, vcur.name], w=[PSK[4]], inc=(h == 3))
                    kb.V(lambda: nc.vector.reciprocal(out=rl, in_=ls), r=[st_.name], w=[st_.name])
                    kb.V(lambda: nc.vector.tensor_tensor(out=og[:, 0:512].rearrange("p (h d) -> p h d", d=128), in0=PS[4][:].rearrange("p (h d) -> p h d", d=128),
                                                         in1=rl.unsqueeze(2).to_broadcast([128, 4, 128]), op=ALU.mult), r=[PSK[4], st_.name], w=[og.name])
                    kb.A(lambda: nc.scalar.activation(out=lnl, in_=ls, func=LN), r=[st_.name], w=[st_.name])
                    kb.V(lambda: nc.vector.tensor_tensor(out=og[:, 512:516], in0=lnl, in1=mx, op=ALU.add), r=[st_.name], w=[og.name])
                    kb.dma('sp', oscr[g, sl, :], og[:], r=[og.name])

                front(blks[0])
                for i_ in range(len(blks)):
                    if i_ + 1 < len(blks):
                        front(blks[i_ + 1])
                    back(blks[i_])
        kb.barrier()
        with ExitStack() as s5:
            om = [sb("om%d" % i, [128, 3, 516], st=s5) for i in range(2)]
            mw = sb("mw", [128, 32], st=s5); att = sb("att", [128, 512], st=s5); attb = sb("attb", [128, 512], BF16, st=s5)
            mt2 = sb("mt2", [128, 512], st=s5); asg = [sb("asg%d" % i, [128, 4, 128], BF16, st=s5) for i in range(2)]
            for tb in range(S // 128):
                o_ = om[tb % 2]
                kb.dma('sp', o_[:], oscr[:, tb * 128:(tb + 1) * 128, :].rearrange("g t f -> t g f"), w=[o_.name])
                mx = mw[:, 0:4]; e3 = mw[:, 4:16].rearrange("p (g h) -> p g h", h=4); ss = mw[:, 16:20]; rs_ = mw[:, 20:24]
                kb.V(lambda: nc.vector.tensor_tensor(out=mx, in0=o_[:, 0, 512:516], in1=o_[:, 1, 512:516], op=ALU.max), r=[o_.name], w=["mw"])
                kb.V(lambda: nc.vector.tensor_tensor(out=mx, in0=mx, in1=o_[:, 2, 512:516], op=ALU.max), r=[o_.name, "mw"], w=["mw"])
                kb.V(lambda: nc.vector.tensor_tensor(out=e3, in0=o_[:, :, 512:516], in1=mx.unsqueeze(1).to_broadcast([128, 3, 4]), op=ALU.subtract), r=[o_.name, "mw"], w=["mw"])
                kb.A(lambda: nc.scalar.activation(out=e3, in_=e3, func=EXP), r=["mw"], w=["mw"])
                kb.V(lambda: nc.vector.tensor_tensor(out=ss, in0=e3[:, 0, :], in1=e3[:, 1, :], op=ALU.add), r=["mw"], w=["mw"])
                kb.V(lambda: nc.vector.tensor_tensor(out=ss, in0=ss, in1=e3[:, 2, :], op=ALU.add), r=["mw"], w=["mw"])
                kb.V(lambda: nc.vector.reciprocal(out=rs_, in_=ss), r=["mw"], w=["mw"])
                kb.V(lambda: nc.vector.tensor_tensor(out=e3, in0=e3, in1=rs_.unsqueeze(1).to_broadcast([128, 3, 4]), op=ALU.mult), r=["mw"], w=["mw"])
                for g in range(3):
                    dst = att if g == 0 else mt2
                    kb.V(lambda: nc.vector.tensor_tensor(out=dst[:].rearrange("p (h d) -> p h d", d=128), in0=o_[:, g, 0:512].rearrange("p (h d) -> p h d", d=128),
                                                         in1=e3[:, g, :].unsqueeze(2).to_broadcast([128, 4, 128]), op=ALU.mult), r=[o_.name, "mw"], w=[dst.name])
                    if g > 0:
                        kb.V(lambda: nc.vector.tensor_tensor(out=att[:], in0=att[:], in1=mt2[:], op=ALU.add), r=["att", "mt2"], w=["att"])
                kb.V(lambda: nc.vector.tensor_copy(out=attb[:], in_=att[:]), r=["att"], w=["attb"])
                for h in range(4):
                    kb.tr(psbf(1)[:, h * 128:(h + 1) * 128], attb[:, h * 128:(h + 1) * 128], identB[:], r=["attb", "identB"], w=[PSK[1]], inc=(h == 3))
                a_ = asg[tb % 2]
                kb.V(lambda: nc.vector.tensor_copy(out=a_[:].rearrange("p h t -> p (h t)"), in_=psbf(1)[:, 0:512]), r=[PSK[1]], w=[a_.name])
                kb.dma('sp', attT[:, :, tb * 128:(tb + 1) * 128].rearrange("c p t -> p c t"), a_[:], r=[a_.name])

        kb.barrier()
        with ExitStack() as s6:
            qS = sb("qS", [128, 12, 16], BF16, st=s6); kS = sb("kS", [128, 12, 16], BF16, st=s6)
            Kc = [sb("Kc%d" % i, [128, 2, 512], st=s6) for i in range(4)]
            KTs = sb("KTs", [128, 4, 129], st=s6); qSf = sb("qSf", [128, 12, 16], st=s6); PTf = sb("PTf", [128, 4], st=s6); Vcb = sb("Vcb", [128, 512], BF16, st=s6)
            vrow = sb("vrow", [1, 512], st=s6); vrowb = sb("vrowb", [1, 512], BF16, st=s6)
            sbias = sb("sbias", [1, 3, 4, 129], st=s6); sd = sb("sd", [1, 129], st=s6)
            Srow = sb("Srow", [1, 4, 129], st=s6); Prb = sb("Prb", [1, 4, 129], BF16, st=s6)
            PTs = sb("PTs", [128, 4], BF16, st=s6); oneb = sb("oneb", [1, 1], BF16, st=s6)
            oTs = sb("oTs", [128, 3, 64], st=s6); lseS = sb("lseS", [1, 3, 64], st=s6); lS = sb("lS", [1, 3, 64], st=s6)
            sm = sb("sm", [1, 16], st=s6)
            kb.G(lambda: nc.gpsimd.memset(oneb[:], 1.0), w=["oneb"])
            kb.dma('sp', sd[:], cdist[0:1, 0:129], w=["sd"])
            for g in range(3):
                for h in range(4):
                    kb.V(lambda: nc.vector.tensor_scalar(out=sbias[:, g, h, :], in0=sd[:], scalar1=float(-SLOPES[g, h] * DILS[g][1]), scalar2=None, op0=ALU.mult),
                         r=["sd"], w=["sbias"])
            kb.dma('sp', qS[:], qT[:, :, S:S + 16].rearrange("c p t -> p c t"), w=["qS"])
            kb.dma('sp', kS[:], kT[:, :, S:S + 16].rearrange("c p t -> p c t"), w=["kS"])
            kb.V(lambda: nc.vector.tensor_copy(out=qSf[:], in_=qS[:]), r=["qS"], w=["qSf"])
            ui = 0
            for b in range(4):
                for g, (lb, dil) in enumerate(DILS):
                    kb.dma('sp', kv_s[g][b, 0:lb - 4], cache[g][b, 4:lb])
                    kb.dma('sp', kv_s[g][b, lb - 4:lb], kvnew[4 * b:4 * b + 4, :, g * 512:(g + 1) * 512])
                    for t in range(4):
                        col = 4 * b + t
                        kc = Kc[ui % 4]; ui += 1
                        ncache = 128 - t if dil == 1 else 128
                        kb.dma('sp', kc[0:ncache], cache[g][b, t:t + dil * (ncache - 1) + 1:dil], w=[kc.name])
                        if ncache < 128:
                            kb.dma('sp', kc[ncache:128], kvnew[4 * b:4 * b + t, :, g * 512:(g + 1) * 512], w=[kc.name])
                        kb.dma('sp', vrow[:], kvnew[col:col + 1, 1, g * 512:(g + 1) * 512], w=["vrow"])
                        for h in range(4):
                            kb.tr(PS[1][:, h * 128:(h + 1) * 128], kc[:, 0, h * 128:(h + 1) * 128], identF[:], r=[kc.name, "identF"], w=[PSK[1]], inc=(h == 3))
                        kb.V(lambda: nc.vector.tensor_copy(out=KTs[:, :, 0:128], in_=PS[1][:].rearrange("p (h m) -> p h m", m=128)), r=[PSK[1]], w=["KTs"])
                        kb.V(lambda: nc.vector.tensor_copy(out=KTs[:, :, 128:129], in_=kS[:, g * 4:(g + 1) * 4, col:col + 1]), r=["kS"], w=["KTs"])
                        for h in range(4):
                            pbk = 2 if h < 2 else 6
                            kb.mm(PS[pbk][0:1, (h % 2) * 129:(h % 2 + 1) * 129], qSf[:, g * 4 + h, col:col + 1], KTs[:, h, :], True, True,
                                  r=["qSf", "KTs"], w=[PSK[pbk]], inc=(h % 2 == 1))
                        for hh in range(2):
                            pbk = 2 if hh == 0 else 6
                            kb.V(lambda: nc.vector.tensor_tensor(out=Srow[:, hh * 2:hh * 2 + 2, :].rearrange("p h m -> p (h m)"), in0=PS[pbk][0:1, 0:258],
                                                                 in1=sbias[:, g, hh * 2:hh * 2 + 2, :].rearrange("p h m -> p (h m)"), op=ALU.add),
                                 r=[PSK[pbk], "sbias"], w=["Srow"])
                        mx = sm[:, 0:4]; ls = sm[:, 4:8]; lnl = sm[:, 8:12]
                        kb.V(lambda: nc.vector.tensor_reduce(out=mx, in_=Srow[:], axis=AX.X, op=ALU.max), r=["Srow"], w=["sm"])
                        kb.V(lambda: nc.vector.tensor_tensor(out=Srow[:], in0=Srow[:], in1=mx.unsqueeze(2).to_broadcast([1, 4, 129]), op=ALU.subtract), r=["Srow", "sm"], w=["Srow"])
                        kb.A(lambda: nc.scalar.activation(out=Srow[:], in_=Srow[:], func=EXP), r=["Srow"], w=["Srow"])
                        kb.V(lambda: nc.vector.tensor_reduce(out=ls, in_=Srow[:], axis=AX.X, op=ALU.add), r=["Srow"], w=["sm"])
                        for h in range(4):
                            kb.mm(PS[3][:, h:h + 1], Srow[:, h, 0:128], onesF[0:1, 0:1], True, True, r=["Srow", "onesF"], w=[PSK[3]], inc=(h == 3))
                        kb.V(lambda: nc.vector.tensor_copy(out=PTf[:], in_=PS[3][:, 0:4]), r=[PSK[3]], w=["PTf"])
                        for h in range(4):
                            o = PS[4][:, h:h + 1]
                            kb.mm(o, kc[:, 1, h * 128:(h + 1) * 128], PTf[:, h:h + 1], True, False, r=[kc.name, "PTf"], w=[PSK[4]])
                            kb.mm(o, vrow[:, h * 128:(h + 1) * 128], Srow[:, h, 128:129], False, True, r=["vrow", "Srow"], w=[PSK[4]])
                        kb.V(lambda: nc.vector.tensor_copy(out=oTs[:, g, col * 4:col * 4 + 4], in_=PS[4][:, 0:4]), r=[PSK[4]], w=["oTs"])
                        kb.A(lambda: nc.scalar.activation(out=lnl, in_=ls, func=LN), r=["sm"], w=["sm"])
                        kb.V(lambda: nc.vector.tensor_tensor(out=lseS[:, g, col * 4:col * 4 + 4], in0=lnl, in1=mx, op=ALU.add), r=["sm"], w=["lseS"])
                        kb.V(lambda: nc.vector.tensor_copy(out=lS[:, g, col * 4:col * 4 + 4], in_=ls), r=["sm"], w=["lS"])
            dbg("oTs", oTs[:], ["oTs"]); dbg("KTs", KTs[:], ["KTs"])
            mxs = sb("mxs", [1, 64], st=s6); es_ = sb("es_", [1, 3, 64], st=s6); sss = sb("sss", [1, 64], st=s6)
            coefB = sb("coefB", [128, 3, 64], st=s6); attS = sb("attS", [128, 64], st=s6); attSb = sb("attSb", [128, 4, 16], BF16, st=s6)
            kb.V(lambda: nc.vector.tensor_tensor(out=mxs[:], in0=lseS[:, 0, :], in1=lseS[:, 1, :], op=ALU.max), r=["lseS"], w=["mxs"])
            kb.V(lambda: nc.vector.tensor_tensor(out=mxs[:], in0=mxs[:], in1=lseS[:, 2, :], op=ALU.max), r=["lseS", "mxs"], w=["mxs"])
            kb.V(lambda: nc.vector.tensor_tensor(out=es_[:], in0=lseS[:], in1=mxs[:].unsqueeze(1).to_broadcast([1, 3, 64]), op=ALU.subtract), r=["lseS", "mxs"], w=["es_"])
            kb.A(lambda: nc.scalar.activation(out=es_[:], in_=es_[:], func=EXP), r=["es_"], w=["es_"])
            kb.V(lambda: nc.vector.tensor_tensor(out=sss[:], in0=es_[:, 0, :], in1=es_[:, 1, :], op=ALU.add), r=["es_"], w=["sss"])
            kb.V(lambda: nc.vector.tensor_tensor(out=sss[:], in0=sss[:], in1=es_[:, 2, :], op=ALU.add), r=["es_", "sss"], w=["sss"])
            kb.V(lambda: nc.vector.tensor_tensor(out=lS[:], in0=lS[:], in1=sss[:].unsqueeze(1).to_broadcast([1, 3, 64]), op=ALU.mult), r=["lS", "sss"], w=["lS"])
            kb.V(lambda: nc.vector.reciprocal(out=lS[:], in_=lS[:]), r=["lS"], w=["lS"])
            kb.V(lambda: nc.vector.tensor_tensor(out=es_[:], in0=es_[:], in1=lS[:], op=ALU.mult), r=["es_", "lS"], w=["es_"])
            kb.mm(PS[5][:, 0:192], onesF[0:1, :], es_[:].rearrange("p g c -> p (g c)"), True, True, r=["onesF", "es_"], w=[PSK[5]])
            kb.V(lambda: nc.vector.tensor_tensor(out=coefB[:].rearrange("p g c -> p (g c)"), in0=PS[5][:, 0:192], in1=oTs[:].rearrange("p g c -> p (g c)"), op=ALU.mult),
                 r=[PSK[5], "oTs"], w=["coefB"])
            kb.V(lambda: nc.vector.tensor_tensor(out=attS[:], in0=coefB[:, 0, :], in1=coefB[:, 1, :], op=ALU.add), r=["coefB"], w=["attS"])
            kb.V(lambda: nc.vector.tensor_tensor(out=attS[:], in0=attS[:], in1=coefB[:, 2, :], op=ALU.add), r=["coefB", "attS"], w=["attS"])
            kb.V(lambda: nc.vector.tensor_copy(out=attSb[:], in_=attS[:].rearrange("p (t h) -> p h t", h=4)), r=["attS"], w=["attSb"])
            kb.dma('sp', attT[:, :, S:S + 16].rearrange("c p t -> p c t"), attSb[:], r=["attSb"])

    with ExitStack() as st1:
        token_phase(st1, 1)
    if CFG.get("stop") == 1:
        kb.drain(); return
    kb.barrier()
    phase2()
    if CFG.get("stop") == 2:
        kb.drain(); return
    kb.barrier()
    with ExitStack() as st3:
        token_phase(st3, 3)

    kb.drain()


def PHASE2(nc, kb, env):
    pass


_NC_CACHE = {}


def _prep_inputs(inp, core):
    f = np.ascontiguousarray
    b = core
    m = {}
    m["cdist"] = np.ascontiguousarray((128 + np.arange(128)[:, None] - np.arange(256)[None, :]).astype(np.float32))
    m["x_p"] = f(inp["x_prompt"][b])
    m["x_s"] = f(inp["x_sample"][4 * b:4 * b + 4].reshape(NS, D))
    m["c5"] = f(np.concatenate([inp["c_prompt"][b:b + 1], inp["c_sample"][4 * b:4 * b + 4]], axis=0))
    m["st_ssm"] = f(inp["state_ssm"][0, 4 * b:4 * b + 4].reshape(4, 32, 128, 128))
    m["st_conv"] = f(inp["state_conv"][0, 4 * b:4 * b + 4])
    for g, nme in enumerate(("cache_kv_w128", "cache_kv_w512", "cache_kv_w2048")):
        a = inp[nme][0, 4 * b:4 * b + 4]
        m["cache%d" % g] = f(a.reshape(4, a.shape[1], 2, 512))
    m["w_ada"] = inp["w_ada"][0]
    m["b_ada"] = f(inp["b_ada"][0].reshape(144, 128))
    for n_ in ("g_pre_ffn1", "g_post_ffn1", "g_pre_mix", "g_post_mix", "g_pre_ffn2", "g_post_ffn2"):
        m[n_] = f(inp[n_][0].reshape(16, 128))
    for n_ in ("w_gu_ffn1", "w_gu_ffn2", "w_down_ffn1", "w_down_ffn2", "w_in", "w_ssm_proj", "w_att_proj", "w_out"):
        m[n_] = inp[n_][0]
    m["conv_w"] = f(inp["conv_w"][0].reshape(4, 48, 128))
    m["conv_b"] = f(inp["conv_b"][0].reshape(48, 128))
    for n_ in ("dt_bias", "a_log", "d_skip"):
        m[n_] = f(inp[n_][0].reshape(64, 1))
    m["g_ssm_norm"] = f(inp["g_ssm_norm"][0].reshape(32, 128))
    return m


def kernel(**inputs):
    inp = {k: np.asarray(v) for k, v in inputs.items()}
    if "nc" not in _NC_CACHE:
        _NC_CACHE["nc"] = build()
    nc = _NC_CACHE["nc"]
    in_maps = [_prep_inputs(inp, c) for c in range(8)]
    res = run_bass_kernel_spmd(nc, in_maps, core_ids=list(range(8)))
    R = res.results
    yp = np.stack([R[c]["y_p"] for c in range(8)])
    ys = np.concatenate([R[c]["y_s"].reshape(4, 4, D) for c in range(8)])
    ssm_p = np.stack([R[c]["ssm_p"].reshape(64, 64, 128) for c in range(8)])[None]
    ssm_s = np.concatenate([R[c]["ssm_s"].reshape(4, 64, 64, 128) for c in range(8)])[None]
    conv_p = np.stack([R[c]["conv_p"] for c in range(8)])[None]
    conv_s = np.concatenate([R[c]["conv_s"] for c in range(8)])[None]
    outs = [yp, ys, ssm_p, ssm_s, conv_p, conv_s]
    for g in range(3):
        kp = np.stack([R[c]["kv_p%d" % g].reshape(-1, 2, 4, 128) for c in range(8)])[None]
        ksm = np.concatenate([R[c]["kv_s%d" % g].reshape(4, -1, 2, 4, 128) for c in range(8)])[None]
        outs += [kp, ksm]
    return tuple(np.ascontiguousarray(o, dtype=np.float32) for o in outs)
```
